# Optimizing a Trainium2 kernel written in Bass

```python
import math
import jax, jax.numpy as jnp
from jax import lax
import numpy as np

D_MODEL = 2048
BATCH = 16
SEQ = 256
DEPTH = 2
DEC_BATCH = 8
DEC_SEQ = 1024
PAST_LEN = 256

GRID_W = 64
N_BRANCH = 4
BRANCH_W = D_MODEL // 4
EPS = 1e-6
LRU_BLOCKS = 8
LRU_BS = BRANCH_W // LRU_BLOCKS
LRU_CONV_W = 4
LRU_PAD_L = 2
LRU_C = 8.0
S5_GROUP = 16
S5_NGROUP = BRANCH_W // S5_GROUP
S5_STATE = 64
HEAD_DIM = 64
N_Q_HEADS = BRANCH_W // HEAD_DIM
N_KV_HEADS = 2
Q_PER_KV = N_Q_HEADS // N_KV_HEADS
KV_W = N_KV_HEADS * HEAD_DIM
WINDOW = 128
BLOCK = 128
ROPE_BASE = 10000.0
ROPE_NF = HEAD_DIM // 4
ATTN_SCALE = 1.0 / math.sqrt(HEAD_DIM)
NEG_INF = -1e30
RWKV_HEAD = 64
RWKV_NH = BRANCH_W // RWKV_HEAD
W_LORA = 64
A_LORA = 64
G_LORA = 128
RWKV_GN_EPS = 64e-5
D_FF = 5504
FFN_CONV_W = 3
FFN_PAD_L = 1
IN_WIDTHS = (BRANCH_W, BRANCH_W,
             BRANCH_W,
             BRANCH_W, KV_W, KV_W,
             BRANCH_W, BRANCH_W, BRANCH_W,
             W_LORA, A_LORA, G_LORA,
             N_BRANCH * D_MODEL)
D_IN = sum(IN_WIDTHS)

kernel_name = 'hybrid_dit_prefix_step'


def rmsnorm(x, g):
    x32 = x.astype(jnp.float32)
    y = x32 * lax.rsqrt(jnp.mean(x32 * x32, axis=-1, keepdims=True) + EPS)
    return y.astype(x.dtype) * g


def split_cols(p):
    outs = []
    off = 0
    for w in IN_WIDTHS:
        outs.append(p[..., off:off + w])
        off += w
    return outs


def dwconv(x, w, pad_left):
    K = w.shape[0]
    L = x.shape[1]
    xp = jnp.pad(x, ((0, 0), (pad_left, K - 1 - pad_left), (0, 0)))
    out = xp[:, 0:L] * w[0]
    for j in range(1, K):
        out = out + xp[:, j:j + L] * w[j]
    return out


def _lin_combine(e1, e2):
    a1, b1 = e1
    a2, b2 = e2
    return a1 * a2, a2 * b1 + b2


def linear_scan(a, b, h0, reverse):
    if reverse:
        a = jnp.flip(a, 1)
        b = jnp.flip(b, 1)
    b = b.at[:, 0].add(a[:, 0] * h0)
    _, h = lax.associative_scan(_lin_combine, (a, b), axis=1)
    if reverse:
        h = jnp.flip(h, 1)
    return h


def _cplx_combine(e1, e2):
    ar1, ai1, br1, bi1 = e1
    ar2, ai2, br2, bi2 = e2
    return (ar2 * ar1 - ai2 * ai1, ar2 * ai1 + ai2 * ar1,
            ar2 * br1 - ai2 * bi1 + br2, ar2 * bi1 + ai2 * br1 + bi2)


def complex_scan(ar, ai, br, bi, h0r, h0i, reverse):
    ar = jnp.broadcast_to(ar, br.shape)
    ai = jnp.broadcast_to(ai, br.shape)
    if reverse:
        ar, ai, br, bi = (jnp.flip(t, 1) for t in (ar, ai, br, bi))
    br = br.at[:, 0].add(ar[:, 0] * h0r - ai[:, 0] * h0i)
    bi = bi.at[:, 0].add(ar[:, 0] * h0i + ai[:, 0] * h0r)
    _, _, hr, hi = lax.associative_scan(_cplx_combine, (ar, ai, br, bi), axis=1)
    if reverse:
        hr = jnp.flip(hr, 1)
        hi = jnp.flip(hi, 1)
    return hr, hi


def rglru(x, wa, ba, wx, bx, lam, h0, reverse):
    B, L, W = x.shape
    xb = x.reshape(B, L, LRU_BLOCKS, LRU_BS)
    r = jax.nn.sigmoid(jnp.einsum('blnc,ncd->blnd', xb, wa).reshape(B, L, W) + ba)
    i = jax.nn.sigmoid(jnp.einsum('blnc,ncd->blnd', xb, wx).reshape(B, L, W) + bx)
    log_a = -LRU_C * r * jax.nn.softplus(-lam)
    a = jnp.exp(log_a)
    b = jnp.sqrt(-jnp.expm1(2.0 * log_a)) * (i * x)
    return linear_scan(a, b, h0, reverse)


def s5_dir(u, a_re, a_im, log_dt, b_re, b_im, c_re, c_im, h0r, h0i, reverse):
    dt = jnp.exp(log_dt)[:, None]
    mag = jnp.exp(a_re * dt)
    abar_r = mag * jnp.cos(a_im * dt)
    abar_i = mag * jnp.sin(a_im * dt)
    den = a_re * a_re + a_im * a_im
    nr = abar_r - 1.0
    fr = (nr * a_re + abar_i * a_im) / den
    fi = (abar_i * a_re - nr * a_im) / den
    bbar_r = fr[..., None] * b_re - fi[..., None] * b_im
    bbar_i = fr[..., None] * b_im + fi[..., None] * b_re
    bu_r = jnp.einsum('blgp,gnp->blgn', u, bbar_r)
    bu_i = jnp.einsum('blgp,gnp->blgn', u, bbar_i)
    hr, hi = complex_scan(abar_r, abar_i, bu_r, bu_i, h0r, h0i, reverse)
    y = jnp.einsum('blgn,gpn->blgp', hr, c_re) - jnp.einsum('blgn,gpn->blgp', hi, c_im)
    fin = 0 if reverse else -1
    return y, hr[:, fin], hi[:, fin]


def axial_rope(x):
    L = x.shape[1]
    t = jnp.arange(L)
    rows = t // GRID_W
    cols = t % GRID_W
    inv = ROPE_BASE ** (-jnp.arange(ROPE_NF, dtype=jnp.float32) / ROPE_NF)
    shape = (1, L) + (1,) * (x.ndim - 3) + (ROPE_NF,)

    def rot(xa, pos):
        ang = (pos.astype(jnp.float32)[:, None] * inv).reshape(shape)
        cos = jnp.cos(ang).astype(x.dtype)
        sin = jnp.sin(ang).astype(x.dtype)
        x1, x2 = xa[..., :ROPE_NF], xa[..., ROPE_NF:]
        return jnp.concatenate([x1 * cos - x2 * sin, x1 * sin + x2 * cos], axis=-1)

    half = HEAD_DIM // 2
    return jnp.concatenate([rot(x[..., :half], rows), rot(x[..., half:], cols)], axis=-1)


def attn_block(q, k, v, bias, sink):
    s = jnp.einsum('bqhgd,bkhd->bhgqk', q, k).astype(jnp.float32) * ATTN_SCALE
    if bias is not None:
        s = s + bias
    sk = jnp.broadcast_to(sink.astype(jnp.float32)[None, :, :, None, None], s.shape[:-1] + (1,))
    pr = jax.nn.softmax(jnp.concatenate([s, sk], axis=-1), axis=-1)[..., :-1]
    return jnp.einsum('bhgqk,bkhd->bqhgd', pr.astype(v.dtype), v)


def ctx_attention(q, k, v, sink):
    B, L = q.shape[:2]
    nb = L // BLOCK
    qb = q.reshape(B, nb, BLOCK, N_KV_HEADS, Q_PER_KV, HEAD_DIM).swapaxes(0, 1)
    out = lax.map(lambda qi: attn_block(qi, k, v, None, sink), qb)
    return out.swapaxes(0, 1).reshape(B, L, N_KV_HEADS, Q_PER_KV, HEAD_DIM)


def lat_attention(q, k, v, k_ctx, v_ctx, sink):
    B, L = q.shape[:2]
    P = k_ctx.shape[1]
    nb = L // BLOCK
    kp = jnp.pad(k, ((0, 0), (WINDOW, WINDOW), (0, 0), (0, 0)))
    vp = jnp.pad(v, ((0, 0), (WINDOW, WINDOW), (0, 0), (0, 0)))
    qb = q.reshape(B, nb, BLOCK, N_KV_HEADS, Q_PER_KV, HEAD_DIM).swapaxes(0, 1)
    nkeys = BLOCK + 2 * WINDOW

    def body(args):
        j, qi = args
        kb = lax.dynamic_slice_in_dim(kp, j * BLOCK, nkeys, axis=1)
        vb = lax.dynamic_slice_in_dim(vp, j * BLOCK, nkeys, axis=1)
        kabs = j * BLOCK - WINDOW + jnp.arange(nkeys)
        qabs = j * BLOCK + jnp.arange(BLOCK)
        ok = (jnp.abs(kabs[None, :] - qabs[:, None]) <= WINDOW) & (kabs[None, :] >= 0) & (kabs[None, :] < L)
        band_bias = jnp.where(ok, 0.0, NEG_INF).astype(jnp.float32)
        bias = jnp.concatenate([jnp.zeros((BLOCK, P), jnp.float32), band_bias], axis=1)
        return attn_block(qi, jnp.concatenate([k_ctx, kb], axis=1),
                          jnp.concatenate([v_ctx, vb], axis=1), bias, sink)

    out = lax.map(body, (jnp.arange(nb), qb))
    return out.swapaxes(0, 1).reshape(B, L, N_KV_HEADS, Q_PER_KV, HEAD_DIM)


def token_shift(x, mu):
    prev = jnp.pad(x, ((0, 0), (1, 0), (0, 0)))[:, :-1]
    nxt = jnp.pad(x, ((0, 0), (0, 1), (0, 0)))[:, 1:]
    return x + mu * (0.5 * (prev + nxt) - x)


def rwkv_dir(r, w, k, v, kk, a, S0, reverse):
    xs = tuple(t.swapaxes(0, 1) for t in (r, w, k, v, kk, a))

    def step(S, inp):
        rt, wt, kt, vt, kkt, at = inp
        sa = jnp.einsum('bhvk,bhk->bhv', S, -kkt)
        S = S * wt[:, :, None, :] + sa[..., None] * (kkt * at)[:, :, None, :] + vt[..., None] * kt[:, :, None, :]
        return S, jnp.einsum('bhvk,bhk->bhv', S, rt)

    S, ys = lax.scan(step, S0, xs, reverse=reverse)
    return ys.swapaxes(0, 1), S


def head_norm(y, w, b):
    B, L = y.shape[:2]
    y32 = y.astype(jnp.float32)
    mu = jnp.mean(y32, axis=-1, keepdims=True)
    var = jnp.mean(jnp.square(y32 - mu), axis=-1, keepdims=True)
    yn = ((y32 - mu) * lax.rsqrt(var + RWKV_GN_EPS)).astype(y.dtype)
    return yn.reshape(B, L, BRANCH_W) * w + b


def mixer_block(h, p, l, ctx):
    lat = ctx is not None
    B, L, _ = h.shape
    (a_x, a_gate, b_u, c_q, c_k, c_v, d_r, d_k, d_v, d_wl, d_al, d_gl, merge) = split_cols(h @ p['w_in'][l])

    xa = dwconv(a_x, p['lru_conv_w'][l], LRU_PAD_L) + p['lru_conv_b'][l]
    lru0 = ctx['lru'] if lat else jnp.zeros((B, 2, BRANCH_W), h.dtype)
    hs = [rglru(xa, p['lru_wa'][l, d], p['lru_ba'][l, d], p['lru_wx'][l, d], p['lru_bx'][l, d],
                p['lru_lam'][l, d], lru0[:, d], d == 1) for d in range(2)]
    y_a = jax.nn.gelu(a_gate) * (hs[0] + hs[1])
    lru_fin = jnp.stack([hs[0][:, -1], hs[1][:, 0]], axis=1)

    u = b_u.reshape(B, L, S5_NGROUP, S5_GROUP)
    s50 = ctx['s5'] if lat else jnp.zeros((B, 2, 2, S5_NGROUP, S5_STATE), h.dtype)
    y_s = p['s5_d'][l].reshape(S5_NGROUP, S5_GROUP) * u
    s5_fin = []
    for d in range(2):
        y_dir, hr_f, hi_f = s5_dir(u, p['s5_a_re'][l, d], p['s5_a_im'][l, d], p['s5_log_dt'][l, d],
                                   p['s5_b_re'][l, d], p['s5_b_im'][l, d], p['s5_c_re'][l, d], p['s5_c_im'][l, d],
                                   s50[:, d, 0], s50[:, d, 1], d == 1)
        y_s = y_s + y_dir
        s5_fin.append(jnp.stack([hr_f, hi_f], axis=1))
    s5_fin = jnp.stack(s5_fin, axis=1)
    y_s = jax.nn.gelu(y_s.reshape(B, L, BRANCH_W))
    y_b = y_s * jax.nn.sigmoid(y_s @ p['s5_glu_w'][l] + p['s5_glu_b'][l])

    q = c_q.reshape(B, L, N_KV_HEADS, Q_PER_KV, HEAD_DIM)
    k = c_k.reshape(B, L, N_KV_HEADS, HEAD_DIM)
    v = c_v.reshape(B, L, N_KV_HEADS, HEAD_DIM)
    sink = p['attn_sink'][l]
    if lat:
        o = lat_attention(axial_rope(q), axial_rope(k), v, ctx['k'], ctx['v'], sink)
    else:
        o = ctx_attention(q, k, v, sink)
    y_c = o.reshape(B, L, BRANCH_W)

    mu = p['rwkv_mu'][l]
    shp = (B, L, RWKV_NH, RWKV_HEAD)
    r = token_shift(d_r, mu[0]).reshape(shp)
    kh = token_shift(d_k, mu[1]).reshape(shp)
    vh = token_shift(d_v, mu[2]).reshape(shp)
    g = jax.nn.sigmoid(d_gl) @ p['rwkv_g2'][l]
    kk = kh * p['rwkv_k_k'][l].reshape(RWKV_NH, RWKV_HEAD)
    kk32 = kk.astype(jnp.float32)
    kk = (kk32 * lax.rsqrt(jnp.sum(kk32 * kk32, axis=-1, keepdims=True) + 1e-12)).astype(kh.dtype)
    k_a = p['rwkv_k_a'][l].reshape(RWKV_NH, RWKV_HEAD)
    wkv0 = ctx['wkv'] if lat else jnp.zeros((B, 2, RWKV_NH, RWKV_HEAD, RWKV_HEAD), h.dtype)
    tw = jnp.tanh(d_wl)
    y_wkv = 0.0
    wkv_fin = []
    for d in range(2):
        z = p['rwkv_w0'][l, d] + tw @ p['rwkv_w2'][l, d]
        decay = jnp.exp(-jnp.exp(-jax.nn.softplus(-z) - 0.5)).reshape(shp)
        a = jax.nn.sigmoid(p['rwkv_a0'][l, d] + d_al @ p['rwkv_a2'][l, d]).reshape(shp)
        kd = kh * (1.0 + (a - 1.0) * k_a)
        y_dir, S_fin = rwkv_dir(r, decay, kd, vh, kk, a, wkv0[:, d], d == 1)
        y_wkv = y_wkv + y_dir
        wkv_fin.append(S_fin)
    wkv_fin = jnp.stack(wkv_fin, axis=1)
    y_d = head_norm(y_wkv, p['rwkv_ln_w'][l], p['rwkv_ln_b'][l])
    bonus = jnp.sum(r * kh * p['rwkv_r_k'][l], axis=-1, keepdims=True) * vh
    y_d = (y_d + bonus.reshape(B, L, BRANCH_W)) * g

    ys = jnp.stack([y_a, y_b, y_c, y_d], axis=2)
    proj = jnp.einsum('blnw,nwd->blnd', ys, p['branch_w'][l])
    gates = jax.nn.sigmoid(merge.reshape(B, L, N_BRANCH, D_MODEL))
    out = jnp.sum(gates * proj, axis=2) @ p['w_out'][l]
    new = None if lat else (k, v, lru_fin, s5_fin, wkv_fin)
    return out, new


def conv_ffn(h, p, l):
    u = h @ p['ffn_w_in'][l]
    gate, val = u[..., :D_FF], u[..., D_FF:]
    gate = dwconv(gate, p['ffn_conv_w'][l], FFN_PAD_L) + p['ffn_conv_b'][l]
    return (jax.nn.silu(gate) * val) @ p['ffn_w_out'][l]


def layer(x, mod, p, l, ctx):
    sh1, sc1, g1, sh2, sc2, g2 = jnp.split(mod[:, None, :], 6, axis=-1)
    h = rmsnorm(x, p['norm1'][l]) * (1.0 + sc1) + sh1
    out, new = mixer_block(h, p, l, ctx)
    x = x + g1 * out
    h = rmsnorm(x, p['norm2'][l]) * (1.0 + sc2) + sh2
    x = x + g2 * conv_ffn(h, p, l)
    return x, new


def setup_inputs(seed: int = 0) -> dict:
    key = jax.random.key(seed)
    cnt = [0]

    def nk():
        cnt[0] += 1
        return jax.random.fold_in(key, cnt[0])

    def nrm(shape, scale):
        return jax.random.normal(nk(), shape, jnp.float32) * scale

    def uni(shape, lo, hi):
        return jax.random.uniform(nk(), shape, jnp.float32, lo, hi)

    W, D, G, N, P = BRANCH_W, D_MODEL, S5_NGROUP, S5_STATE, S5_GROUP
    lru_a = uni((DEPTH, 2, W), 0.9, 0.999) ** (1.0 / LRU_C)
    return {
        'x_prompt': nrm((BATCH, SEQ, D), 1.0),
        'x_sample': nrm((DEC_BATCH, DEC_SEQ, D), 1.0),
        'cache_k': nrm((DEC_BATCH, DEPTH, PAST_LEN, N_KV_HEADS, HEAD_DIM), 1.0),
        'cache_v': nrm((DEC_BATCH, DEPTH, PAST_LEN, N_KV_HEADS, HEAD_DIM), 1.0),
        'state_lru': nrm((DEC_BATCH, DEPTH, 2, W), 0.5),
        'state_s5': nrm((DEC_BATCH, DEPTH, 2, 2, G, N), 0.5),
        'state_wkv': nrm((DEC_BATCH, DEPTH, 2, RWKV_NH, RWKV_HEAD, RWKV_HEAD), 0.5),
        'c': nrm((DEC_BATCH, D), 1.0),
        'c_ctx': nrm((D,), 1.0),
        'mod_w': nrm((DEPTH, D, 6 * D), 0.5 * D ** -0.5),
        'mod_b': nrm((DEPTH, 6 * D), 0.02),
        'norm1': 1.0 + nrm((DEPTH, D), 0.02),
        'norm2': 1.0 + nrm((DEPTH, D), 0.02),
        'norm_final': 1.0 + nrm((D,), 0.02),
        'w_in': nrm((DEPTH, D, D_IN), D ** -0.5),
        'lru_conv_w': nrm((DEPTH, LRU_CONV_W, W), LRU_CONV_W ** -0.5),
        'lru_conv_b': nrm((DEPTH, W), 0.02),
        'lru_wa': nrm((DEPTH, 2, LRU_BLOCKS, LRU_BS, LRU_BS), LRU_BS ** -0.5),
        'lru_ba': nrm((DEPTH, 2, W), 0.02),
        'lru_wx': nrm((DEPTH, 2, LRU_BLOCKS, LRU_BS, LRU_BS), LRU_BS ** -0.5),
        'lru_bx': nrm((DEPTH, 2, W), 0.02),
        'lru_lam': jnp.log(lru_a) - jnp.log1p(-lru_a),
        's5_a_re': -0.5 + nrm((DEPTH, 2, G, N), 0.01),
        's5_a_im': math.pi * jnp.arange(N, dtype=jnp.float32) + nrm((DEPTH, 2, G, N), 0.01),
        's5_log_dt': uni((DEPTH, 2, G), math.log(1e-3), math.log(1e-1)),
        's5_b_re': nrm((DEPTH, 2, G, N, P), (2 * P) ** -0.5),
        's5_b_im': nrm((DEPTH, 2, G, N, P), (2 * P) ** -0.5),
        's5_c_re': nrm((DEPTH, 2, G, P, N), (2 * N) ** -0.5),
        's5_c_im': nrm((DEPTH, 2, G, P, N), (2 * N) ** -0.5),
        's5_d': nrm((DEPTH, W), 1.0),
        's5_glu_w': nrm((DEPTH, W, W), W ** -0.5),
        's5_glu_b': nrm((DEPTH, W), 0.02),
        'attn_sink': nrm((DEPTH, N_KV_HEADS, Q_PER_KV), 0.5),
        'rwkv_mu': uni((DEPTH, 3, W), 0.0, 1.0),
        'rwkv_w0': uni((DEPTH, 2, W), -6.0, 0.0),
        'rwkv_w2': nrm((DEPTH, 2, W_LORA, W), 0.5 * W_LORA ** -0.5),
        'rwkv_a0': nrm((DEPTH, 2, W), 0.2),
        'rwkv_a2': nrm((DEPTH, 2, A_LORA, W), 0.5 * A_LORA ** -0.5),
        'rwkv_g2': nrm((DEPTH, G_LORA, W), G_LORA ** -0.5),
        'rwkv_k_k': 0.85 + nrm((DEPTH, W), 0.02),
        'rwkv_k_a': 1.0 + nrm((DEPTH, W), 0.02),
        'rwkv_r_k': nrm((DEPTH, RWKV_NH, RWKV_HEAD), 0.1),
        'rwkv_ln_w': 1.0 + nrm((DEPTH, W), 0.02),
        'rwkv_ln_b': nrm((DEPTH, W), 0.02),
        'branch_w': nrm((DEPTH, N_BRANCH, W, D), W ** -0.5),
        'w_out': nrm((DEPTH, D, D), D ** -0.5),
        'ffn_w_in': nrm((DEPTH, D, 2 * D_FF), D ** -0.5),
        'ffn_conv_w': nrm((DEPTH, FFN_CONV_W, D_FF), FFN_CONV_W ** -0.5),
        'ffn_conv_b': nrm((DEPTH, D_FF), 0.02),
        'ffn_w_out': nrm((DEPTH, D_FF, D), D_FF ** -0.5),
    }


def reference(x_prompt, x_sample, cache_k, cache_v, state_lru, state_s5, state_wkv, c, c_ctx,
              mod_w, mod_b, norm1, norm2, norm_final, w_in,
              lru_conv_w, lru_conv_b, lru_wa, lru_ba, lru_wx, lru_bx, lru_lam,
              s5_a_re, s5_a_im, s5_log_dt, s5_b_re, s5_b_im, s5_c_re, s5_c_im, s5_d, s5_glu_w, s5_glu_b,
              attn_sink,
              rwkv_mu, rwkv_w0, rwkv_w2, rwkv_a0, rwkv_a2, rwkv_g2, rwkv_k_k, rwkv_k_a, rwkv_r_k,
              rwkv_ln_w, rwkv_ln_b,
              branch_w, w_out, ffn_w_in, ffn_conv_w, ffn_conv_b, ffn_w_out):
    p = dict(norm1=norm1, norm2=norm2, w_in=w_in,
             lru_conv_w=lru_conv_w, lru_conv_b=lru_conv_b, lru_wa=lru_wa, lru_ba=lru_ba,
             lru_wx=lru_wx, lru_bx=lru_bx, lru_lam=lru_lam,
             s5_a_re=s5_a_re, s5_a_im=s5_a_im, s5_log_dt=s5_log_dt, s5_b_re=s5_b_re, s5_b_im=s5_b_im,
             s5_c_re=s5_c_re, s5_c_im=s5_c_im, s5_d=s5_d, s5_glu_w=s5_glu_w, s5_glu_b=s5_glu_b,
             attn_sink=attn_sink,
             rwkv_mu=rwkv_mu, rwkv_w0=rwkv_w0, rwkv_w2=rwkv_w2, rwkv_a0=rwkv_a0, rwkv_a2=rwkv_a2,
             rwkv_g2=rwkv_g2, rwkv_k_k=rwkv_k_k, rwkv_k_a=rwkv_k_a, rwkv_r_k=rwkv_r_k,
             rwkv_ln_w=rwkv_ln_w, rwkv_ln_b=rwkv_ln_b,
             branch_w=branch_w, w_out=w_out, ffn_w_in=ffn_w_in, ffn_conv_w=ffn_conv_w,
             ffn_conv_b=ffn_conv_b, ffn_w_out=ffn_w_out)

    xp = x_prompt
    ks, vs, lrus, s5s, wkvs = [], [], [], [], []
    for l in range(DEPTH):
        mod_ctx = jax.nn.silu(c_ctx)[None, :] @ mod_w[l] + mod_b[l]
        xp, new = layer(xp, mod_ctx, p, l, None)
        k_l, v_l, lru_l, s5_l, wkv_l = new
        ks.append(k_l)
        vs.append(v_l)
        lrus.append(lru_l)
        s5s.append(s5_l)
        wkvs.append(wkv_l)
    y_prompt = rmsnorm(xp, norm_final)

    xs = x_sample
    for l in range(DEPTH):
        mod_lat = jax.nn.silu(c) @ mod_w[l] + mod_b[l]
        ctx = dict(k=cache_k[:, l], v=cache_v[:, l], lru=state_lru[:, l],
                   s5=state_s5[:, l], wkv=state_wkv[:, l])
        xs, _ = layer(xs, mod_lat, p, l, ctx)
    y_sample = rmsnorm(xs, norm_final)

    new_cache_k = jnp.stack(ks, axis=1)
    new_cache_v = jnp.stack(vs, axis=1)
    new_state_lru = jnp.stack(lrus, axis=1)
    new_state_s5 = jnp.stack(s5s, axis=1)
    new_state_wkv = jnp.stack(wkvs, axis=1)
    return (y_prompt, y_sample, new_cache_k, new_cache_v, new_state_lru, new_state_s5, new_state_wkv)
```

```python
import math
import numpy as np
import concourse.bass as bass
import concourse.mybir as mybir
from concourse.bass_utils import run_bass_kernel_spmd

F32 = mybir.dt.float32
BF16 = mybir.dt.bfloat16
I32 = mybir.dt.int32
ALU = mybir.AluOpType
AF = mybir.ActivationFunctionType

D = 2048
DEPTH = 2
NKT = 16
W = 512
DIN = 12288
DFF = 5504
NFT = 43
EPS = 1e-6
GN_EPS = 64e-5
C_GELU = 1.5957691216057308
OFF = dict(ax=0, ag=512, bu=1024, cq=1536, ck=2048, cv=2176, dr=2304, dk=2816, dv=3328, dwl=3840, dal=3904, dgl=3968, mg=4096)


class _Stop(Exception):
    pass


class Sync:
    def __init__(self, nc, n_dma_sems=40):
        self.nc = nc
        self.engs = {'pe': nc.tensor, 'act': nc.scalar, 'dve': nc.vector, 'pool': nc.gpsimd, 'sp': nc.sync}
        self.sems = {}
        self.cnt = {}
        for k in self.engs:
            self.sems[k] = nc.semaphore("s_" + k).__enter__()
            self.cnt[k] = 0
        self.dma_sems = []
        for i in range(n_dma_sems):
            self.dma_sems.append([nc.semaphore("d_%d" % i).__enter__(), 0])
        self.dma_rr = 0
        self.waited = {k: {} for k in self.engs}
        self.lastw = {}
        self.readers = {}
        self.n_inst = 0
        self.n_wait = 0
        self.out_events = []

    def _need(self, e, events):
        eng = self.engs[e]
        w = self.waited[e]
        best = {}
        for ev in events:
            if ev is None:
                continue
            sem, val, owner, nm = ev
            if w.get(nm, 0) >= val:
                continue
            if nm not in best or best[nm][1] < val:
                best[nm] = (sem, val)
        for nm, (sem, val) in best.items():
            eng.wait_ge(sem, val)
            w[nm] = val
            self.n_wait += 1

    def _deps(self, e, reads, writes, excl):
        evs = []
        for k in reads:
            ev = self.lastw.get(k)
            if ev is not None and not (ev[2] == e and e == 'pe'):
                evs.append(ev)
        for k in list(writes) + list(excl):
            ev = self.lastw.get(k)
            if ev is not None and ev[2] != e:
                evs.append(ev)
            for r in self.readers.get(k, ()):
                if r[2] != e:
                    evs.append(r)
        return evs

    def _record(self, ev, reads, writes, excl):
        for k in list(writes) + list(excl):
            self.lastw[k] = ev
            self.readers[k] = []
        for k in reads:
            if k in writes or k in excl:
                continue
            self.readers.setdefault(k, []).append(ev)

    def op(self, e, fn, reads=(), writes=(), excl=()):
        self._need(e, self._deps(e, reads, writes, excl))
        ins = fn(self.engs[e])
        self.cnt[e] += 1
        ins.then_inc(self.sems[e], 1)
        ev = (self.sems[e], self.cnt[e], e, 's_' + e)
        self._record(ev, reads, writes, excl)
        self.n_inst += 1
        return ins

    def dma(self, q, out, in_, reads=(), writes=(), is_out=False, **kw):
        slot = self.dma_sems[self.dma_rr]
        idx = self.dma_rr
        self.dma_rr = (self.dma_rr + 1) % len(self.dma_sems)
        sem, val = slot
        evs = self._deps('dma_issue', reads, writes, ())
        if val > 0:
            evs.append((sem, val, 'dma', 'd_%d' % idx))
        self._need(q, evs)
        ins = self.engs[q].dma_start(out=out, in_=in_, **kw)
        ins.then_inc(sem, 16)
        slot[1] = val + 16
        ev = (sem, val + 16, 'dma', 'd_%d' % idx)
        self._record(ev, reads, writes, ())
        if is_out:
            self.out_events.append(ev)
        self.n_inst += 1
        return ev

    def barrier(self):
        evs = [(self.sems[k], self.cnt[k], k, 's_' + k) for k in ('pe', 'act', 'dve', 'pool') if self.cnt[k] > 0]
        for e in ('pe', 'act', 'dve', 'pool', 'sp'):
            self._need(e, [ev for ev in evs if ev[2] != e] + self.out_events)
        self.out_events = []

    def final(self):
        self.barrier()
        evs = []
        for i, (sem, val) in enumerate(self.dma_sems):
            if val:
                evs.append((sem, val, 'dma', 'd_%d' % i))
        self._need('sp', evs)


def _bcast(ap, shape):
    return ap.to_broadcast(list(shape))


class Builder:
    def __init__(self, cfg=None):
        self.cfg = cfg or {}
        self.nc = bass.Bass("TRN2", target_bir_lowering=False)
        self.S = Sync(self.nc)
        self.dram_in = {}
        self.dram_out = {}
        self.dbg = {}
        self._uid = 0
        self.marks = []
        self.bank_rr = 0
        self.bank_n = 8

    def din(self, name, shape, dt=F32):
        t = self.nc.dram_tensor(name, list(shape), dt, kind="ExternalInput").ap()
        self.dram_in[name] = t
        return t

    def dout(self, name, shape, dt=F32):
        t = self.nc.dram_tensor(name, list(shape), dt, kind="ExternalOutput").ap()
        self.dram_out[name] = t
        return t

    def sb(self, name, shape, dt=F32):
        return self.nc.sbuf_tensor(self.uid(name + "_"), list(shape), dt).__enter__()

    def sbc(self, name, shape, dt=F32):
        return self.nc.sbuf_tensor(self.uid(name + "_"), list(shape), dt)

    def mark(self, name):
        self.marks.append((name, self.S.cnt['act']))

    def uid(self, p="k"):
        self._uid += 1
        return "%s%d" % (p, self._uid)

    def bank(self):
        b = self.bank_rr % self.bank_n
        self.bank_rr = (b + 1) % self.bank_n
        return b

    def pe(self, fn, reads=(), writes=(), excl=()):
        return self.S.op('pe', fn, reads, writes, excl)

    def act(self, fn, reads=(), writes=(), excl=()):
        return self.S.op('act', fn, reads, writes, excl)

    def dve(self, fn, reads=(), writes=(), excl=()):
        return self.S.op('dve', fn, reads, writes, excl)

    def pool(self, fn, reads=(), writes=(), excl=()):
        return self.S.op('pool', fn, reads, writes, excl)

    def ld(self, out, in_, writes, q='sp', reads=(), **kw):
        return self.S.dma(q, out, in_, reads=reads, writes=writes, **kw)

    def st(self, out, in_, reads, q='sp'):
        return self.S.dma(q, out, in_, reads=reads, writes=(), is_out=True)

    def dump(self, name, ap, key, shape):
        if name not in self.cfg.get('dump', ()):
            return
        o = self.dout("dbg_" + name, shape)
        if ap.dtype != F32:
            self.S.dma('pool', o, ap, reads=[key], is_out=True)
        else:
            self.S.dma('sp', o, ap, reads=[key], is_out=True)

    def wload(self, src, nkt, ncols):
        i = self.w_rr
        self.w_rr = (self.w_rr + 1) % len(self.wbufs)
        assert nkt * ncols <= 4096
        view = self.wbufs[i][:, 0:nkt * ncols].rearrange("p (k n) -> p k n", n=ncols)
        stg = getattr(self, 'stg', None)
        wc = getattr(self, 'wc_mode', None)
        if wc is not None:
            mode, l_ = wc
            idx = self.wc_idx[l_]
            self.wc_idx[l_] += 1
            full = self.wbufs[i][:, 0:nkt * ncols]
            if mode == 'load' and idx < self.wc_n[l_]:
                self.S.dma('sp', full, self.wcache[l_][idx, :, 0:nkt * ncols], writes=['wb%d' % i])
                return view, 'wb%d' % i
            self.S.dma('pool', view, src.rearrange("(k p) n -> p k n", p=128), writes=['wb%d' % i])
            if mode == 'store' and idx < WC_MAX:
                self.S.dma('sp', self.wcache[l_][idx, :, 0:nkt * ncols], full, reads=['wb%d' % i], is_out=True)
                self.wc_n[l_] = idx + 1
            return view, 'wb%d' % i
        if not stg:
            self.S.dma('pool', view, src.rearrange("(k p) n -> p k n", p=128), writes=['wb%d' % i])
            return view, 'wb%d' % i
        kper = max(1, 1024 // ncols)
        for k0 in range(0, nkt, kper):
            kk = min(kper, nkt - k0)
            j = self.stg_rr
            self.stg_rr = (j + 1) % len(stg)
            sv = stg[j][:, 0:kk * ncols].rearrange("p (k n) -> p k n", n=ncols)
            self.S.dma('sp', sv, src[k0 * 128:(k0 + kk) * 128, :].rearrange("(k p) n -> p k n", p=128), writes=['stg%d' % j])
            self.S.op('pool', lambda e, sv=sv, k0=k0, kk=kk: e.tensor_copy(view[:, k0:k0 + kk, :], sv), reads=['stg%d' % j], writes=['wb%d' % i])
        return view, 'wb%d' % i

    def stg_on(self, n=4):
        self._stg_cms = [self.sbc("stg%d" % j, [128, 1024], F32) for j in range(n)]
        self.stg = [c.__enter__() for c in self._stg_cms]
        self.stg_rr = 0

    def stg_off(self):
        self.S.barrier()
        for c in reversed(self._stg_cms):
            c.__exit__(None, None, None)
        self.stg = None


def _pp_layout():
    cols = {}
    o = 0
    for name, n in [('norm1', 16), ('norm2', 16), ('mod_b', 96), ('fcw', 3 * NFT), ('fcb', NFT),
                    ('lcw', 16), ('lcb', 4), ('lba', 8), ('lbx', 8), ('llam', 8),
                    ('s5d', 4), ('s5gb', 4),
                    ('mu', 12), ('w0', 8), ('a0', 8), ('kk', 4), ('ka', 4), ('rk', 4), ('lnw', 4), ('lnb', 4),
                    ('are', 32), ('aim', 32), ('ldt', 32), ('sink', 8), ('normf', 16)]:
        cols[name] = (o, n)
        o += n
    return cols, o


PPC, NPP = _pp_layout()


def _fm(v, nt):
    return np.ascontiguousarray(np.asarray(v, np.float32).reshape(nt, 128).T)


def pack_params(inp, l):
    pp = np.zeros((128, NPP), np.float32)

    def put(name, arr):
        o, n = PPC[name]
        arr = np.asarray(arr, np.float32)
        assert arr.shape == (128, n), (name, arr.shape, n)
        pp[:, o:o + n] = arr

    put('norm1', _fm(inp['norm1'][l], 16))
    put('norm2', _fm(inp['norm2'][l], 16))
    put('mod_b', _fm(inp['mod_b'][l], 96))
    put('fcw', np.concatenate([_fm(inp['ffn_conv_w'][l, j], NFT) for j in range(3)], axis=1))
    put('fcb', _fm(inp['ffn_conv_b'][l], NFT))
    put('lcw', np.concatenate([_fm(inp['lru_conv_w'][l, j], 4) for j in range(4)], axis=1))
    put('lcb', _fm(inp['lru_conv_b'][l], 4))
    put('lba', np.concatenate([_fm(inp['lru_ba'][l, d], 4) for d in range(2)], axis=1))
    put('lbx', np.concatenate([_fm(inp['lru_bx'][l, d], 4) for d in range(2)], axis=1))
    put('llam', np.concatenate([_fm(inp['lru_lam'][l, d], 4) for d in range(2)], axis=1))
    put('s5d', _fm(inp['s5_d'][l], 4))
    put('s5gb', _fm(inp['s5_glu_b'][l], 4))
    put('mu', np.concatenate([_fm(inp['rwkv_mu'][l, j], 4) for j in range(3)], axis=1))
    put('w0', np.concatenate([_fm(inp['rwkv_w0'][l, d], 4) for d in range(2)], axis=1))
    put('a0', np.concatenate([_fm(inp['rwkv_a0'][l, d], 4) for d in range(2)], axis=1))
    put('kk', _fm(inp['rwkv_k_k'][l], 4))
    put('ka', _fm(inp['rwkv_k_a'][l], 4))
    put('rk', _fm(inp['rwkv_r_k'][l].reshape(-1), 4))
    put('lnw', _fm(inp['rwkv_ln_w'][l], 4))
    put('lnb', _fm(inp['rwkv_ln_b'][l], 4))
    put('are', np.concatenate([_fm(inp['s5_a_re'][l, d].reshape(-1), 16) for d in range(2)], axis=1))
    put('aim', np.concatenate([_fm(inp['s5_a_im'][l, d].reshape(-1), 16) for d in range(2)], axis=1))
    put('ldt', np.concatenate([_fm(np.repeat(inp['s5_log_dt'][l, d], 64), 16) for d in range(2)], axis=1))
    put('sink', np.broadcast_to(np.asarray(inp['attn_sink'][l], np.float32).reshape(1, 8), (128, 8)))
    put('normf', _fm(inp['norm_final'], 16))
    return pp


WC_MAX = 200


GROUPS = [dict(name='s', NT=1024, seqs=[(0, 1024)], lat=True, gi=0),
          dict(name='p', NT=512, seqs=[(0, 256), (256, 256)], lat=False, gi=1)]


def _setup(self):
    nc = self.nc
    cfg = self.cfg
    self.w_rr = 0
    self.wbufs = [self.sb("wbuf%d" % i, [128, 4096], BF16) for i in range(3)]
    self.ps = [nc.psum_tensor("ps%d" % i, [128, 512], F32).__enter__() for i in range(8)]
    self.xT = {'s': self.din("xT_s", [D, 1024]), 'p': self.din("xT_p", [D, 512])}
    self.yT = {'s': self.dout("yT_s", [D, 1024]), 'p': self.dout("yT_p", [D, 512])}
    self.d_pp = self.din("pp", [DEPTH, 128, NPP])
    self.d_cc = self.din("cc", [128, 16, 2])
    self.d_modw = self.din("mod_w", [DEPTH, D, 6 * D])
    self.d_win = self.din("w_in", [DEPTH, D, DIN])
    self.d_bw = self.din("branch_w", [DEPTH, 4 * W, D])
    self.d_wout = self.din("w_out", [DEPTH, D, D])
    self.d_fwin = self.din("ffn_w_in", [DEPTH, D, 2 * DFF])
    self.d_fwout = self.din("ffn_w_out", [DEPTH, DFF, D])
    self.d_consts = self.din("consts", [128, 128 * 4])
    self.x = self.sb("x", [128, 16, 1024])
    self.h = self.sb("h", [128, 16, 1024], BF16)
    self.yb = self.sb("yb", [128, 16, 1024], BF16)
    self.pp = self.sb("ppk", [128, DEPTH, NPP])
    self.modv = self.sb("modv", [128, DEPTH, 2, 96])
    self.mA = self.sb("mA", [128, DEPTH, 2, 2, 16])
    self.cst = self.sb("cst", [128, 8])
    self.ones_bf = self.sb("ones_bf", [128, 128], BF16)
    self.cf = self.sb("cf", [128, 4, 128])
    self.ident = self.cf[:, 0, :]
    self.bones = self.cf[:, 1, :]
    for i, v in enumerate([EPS, 1.0, GN_EPS, 1e-12, 0.0, -1.0, 0.5, 2.0]):
        self.pool(lambda e, i=i, v=v: e.memset(self.cst[:, i:i + 1], v), writes=['cst'])
    self.pool(lambda e: e.memset(self.ones_bf[:], 1.0), writes=['ones_bf'])
    self.ld(self.cf[:], self.d_consts.rearrange("p (a b) -> p a b", b=128), writes=['cf'])
    self.ld(self.pp[:], self.d_pp.rearrange("l p n -> p l n"), writes=['pp'])


def ppc(self, l, name, j=0, n=1):
    o, cnt = PPC[name]
    return self.pp[:, l, o + j:o + j + n]


def _prologue_mod(self):
    with self.sbc("cc32", [128, 16, 2]) as cc32, self.sbc("ccb", [128, 16, 2], BF16) as ccb:
        self.ld(cc32[:], self.d_cc, writes=['cc32'])
        self.act(lambda e: e.activation(ccb[:], cc32[:], AF.Silu), reads=['cc32'], writes=['ccb'])
        layers = self.cfg.get('layers', list(range(DEPTH)))
        for l in layers:
            b = self.bank()
            bk = 'ps%d' % b
            psv = self.ps[b][:, 0:192].rearrange("p (j g) -> p j g", g=2)
            for sl in range(48):
                wv, wk = self.wload(self.d_modw[l, :, sl * 256:(sl + 1) * 256], 16, 256)
                for jj in range(2):
                    j = sl * 2 + jj
                    for kt in range(16):
                        self.pe(lambda e, j=j, kt=kt, jj=jj, wv=wv: e.matmul(psv[:, j, :], wv[:, kt, jj * 128:(jj + 1) * 128], ccb[:, kt, :],
                                                                              start=(kt == 0), stop=(kt == 15)),
                                reads=[wk, 'ccb'], excl=[bk])
            o, n = PPC['mod_b']
            for g in range(2):
                self.dve(lambda e, g=g, l=l: e.tensor_tensor(self.modv[:, l, g, :], psv[:, :, g], self.pp[:, l, o:o + 96], op=ALU.add),
                         reads=['pp'], writes=['modv'], excl=[bk])
            for g in range(2):
                for wi, (nm, so) in enumerate([('norm1', 16), ('norm2', 64)]):
                    no, _ = PPC[nm]
                    self.dve(lambda e, g=g, l=l, wi=wi, so=so, no=no: e.scalar_tensor_tensor(
                        self.mA[:, l, g, wi, :], self.modv[:, l, g, so:so + 16], 1.0, self.pp[:, l, no:no + 16], op0=ALU.add, op1=ALU.mult),
                        reads=['modv', 'pp'], writes=['mA'])


def _norm(self, grp, scale_ap_fn, shift_ap_fn, out_fn, out_key_fn, rkeys):
    NT = grp['NT']
    ntb = NT // 512
    with self.sbc("rstd", [128, 1024]) as rstd, self.sbc("sq0", [128, 512], BF16) as sq0, self.sbc("sq1", [128, 512], BF16) as sq1, \
            self.sbc("nt0", [128, 1024]) as nt0, self.sbc("nt1", [128, 1024]) as nt1:
        sq = [sq0, sq1]
        ntmp = [nt0, nt1]
        for tb in range(ntb):
            b = self.bank()
            bk = 'ps%d' % b
            ts = slice(tb * 512, (tb + 1) * 512)
            for kt in range(16):
                s_ = sq[kt % 2]
                sk = 'sq%d' % (kt % 2)
                self.act(lambda e, kt=kt, s_=s_: e.activation(s_[:], self.x[:, kt, ts], AF.Square), reads=['x'], writes=[sk])
                self.pe(lambda e, kt=kt, s_=s_: e.matmul(self.ps[b][:], self.ones_bf[:], s_[:], start=(kt == 0), stop=(kt == 15)),
                        reads=[sk, 'ones_bf'], excl=[bk])
            self.act(lambda e: e.activation(rstd[:, ts], self.ps[b][:], AF.Sqrt, bias=self.cst[:, 0:1], scale=1.0 / D),
                     reads=['cst'], writes=['rstd'], excl=[bk])
        self.dve(lambda e: e.reciprocal(rstd[:, 0:NT], rstd[:, 0:NT]), reads=['rstd'], writes=['rstd'])
        for kt in range(16):
            t_ = ntmp[kt % 2]
            tk = 'nt%d' % (kt % 2)
            self.dve(lambda e, kt=kt, t_=t_: e.tensor_tensor(t_[:, 0:NT], self.x[:, kt, 0:NT], rstd[:, 0:NT], op=ALU.mult),
                     reads=['x', 'rstd'], writes=[tk])
            self.act(lambda e, kt=kt, t_=t_: e.activation(out_fn(kt), t_[:, 0:NT], AF.Identity, bias=shift_ap_fn(kt), scale=scale_ap_fn(kt)),
                     reads=[tk] + list(rkeys), writes=[out_key_fn(kt)])
    self.S.barrier()


def _norm_mod(self, grp, l, which):
    g = grp['gi']
    NT = grp['NT']
    so = 0 if which == 0 else 48
    _norm(self, grp,
          lambda kt: self.mA[:, l, g, which, kt:kt + 1],
          lambda kt: self.modv[:, l, g, so + kt:so + kt + 1],
          lambda kt: self.h[:, kt, 0:NT], lambda kt: 'h', ['mA', 'modv'])


def _ffn(self, grp, l):
    NT = grp['NT']
    g = grp['gi']
    ntb = NT // 512
    seqs = grp['seqs']
    fcw, _ = PPC['fcw']
    fcb, _ = PPC['fcb']
    gb = self.yb
    wbx = [self.sbc("wbufx%d" % j, [128, 4096], BF16) for j in range(3)]
    for c_ in wbx:
        self.wbufs.append(c_.__enter__())
    self.w_rr = 0
    with self.sbc("gt0", [128, 1024]) as gt0, self.sbc("gt1", [128, 1024]) as gt1, \
            self.sbc("cv0", [128, 1024]) as cv0, self.sbc("cv1", [128, 1024]) as cv1:
        gts = [gt0, gt1]
        cvs = [cv0, cv1]
        chunks = [(c0, min(c0 + 8, NFT)) for c0 in range(0, NFT, 8)]
        for ci, (f0, f1) in enumerate(chunks):
            par = ci % 2
            gkey = 'gb%d' % par
            nf = f1 - f0
            fl = list(range(f0, f1))
            for p0 in range(0, nf, 2):
                pair = fl[p0:p0 + 2]
                np_ = len(pair)
                wg, wgk = self.wload(self.d_fwin[l, :, pair[0] * 128:(pair[0] + np_) * 128], 16, np_ * 128)
                wv, wvk = self.wload(self.d_fwin[l, :, DFF + pair[0] * 128:DFF + (pair[0] + np_) * 128], 16, np_ * 128)
                for pi, ft in enumerate(pair):
                    gbanks = [self.bank() for _ in range(ntb)]
                    vbanks = [self.bank() for _ in range(ntb)]
                    for (wt, wk, banks) in ((wg, wgk, gbanks), (wv, wvk, vbanks)):
                        for tb in range(ntb):
                            b = banks[tb]
                            for kt in range(16):
                                self.pe(lambda e, b=b, kt=kt, tb=tb, wt=wt, pi=pi: e.matmul(
                                    self.ps[b][:], wt[:, kt, pi * 128:(pi + 1) * 128], self.h[:, kt, tb * 512:(tb + 1) * 512],
                                    start=(kt == 0), stop=(kt == 15)), reads=[wk, 'h'], excl=['ps%d' % b])
                    gt = gts[ft % 2]
                    gk = 'gt%d' % (ft % 2)
                    cv = cvs[ft % 2]
                    ck = 'cv%d' % (ft % 2)
                    for tb in range(ntb):
                        b = gbanks[tb]
                        self.act(lambda e, b=b, tb=tb, gt=gt: e.activation(gt[:, tb * 512:(tb + 1) * 512], self.ps[b][:], AF.Copy),
                                 writes=[gk], excl=['ps%d' % b])
                    w0 = self.pp[:, l, fcw + 0 * NFT + ft:fcw + 0 * NFT + ft + 1]
                    w1 = self.pp[:, l, fcw + 1 * NFT + ft:fcw + 1 * NFT + ft + 1]
                    w2 = self.pp[:, l, fcw + 2 * NFT + ft:fcw + 2 * NFT + ft + 1]
                    bb = self.pp[:, l, fcb + ft:fcb + ft + 1]
                    self.dve(lambda e, gt=gt, cv=cv, w1=w1, bb=bb: e.tensor_scalar(cv[:, 0:NT], gt[:, 0:NT], w1, bb, op0=ALU.mult, op1=ALU.add),
                             reads=[gk, 'pp'], writes=[ck])
                    for (o, L) in seqs:
                        self.dve(lambda e, gt=gt, cv=cv, w0=w0, o=o, L=L: e.scalar_tensor_tensor(
                            cv[:, o + 1:o + L], gt[:, o:o + L - 1], w0, cv[:, o + 1:o + L], op0=ALU.mult, op1=ALU.add),
                            reads=[gk, ck, 'pp'], writes=[ck])
                        self.dve(lambda e, gt=gt, cv=cv, w2=w2, o=o, L=L: e.scalar_tensor_tensor(
                            cv[:, o:o + L - 1], gt[:, o + 1:o + L], w2, cv[:, o:o + L - 1], op0=ALU.mult, op1=ALU.add),
                            reads=[gk, ck, 'pp'], writes=[ck])
                    self.act(lambda e, cv=cv: e.activation(cv[:, 0:NT], cv[:, 0:NT], AF.Silu), reads=[ck], writes=[ck])
                    for tb in range(ntb):
                        b = vbanks[tb]
                        self.dve(lambda e, b=b, tb=tb, cv=cv, ft=ft: e.tensor_tensor(
                            gb[:, par * 8 + (ft - f0), tb * 512:(tb + 1) * 512], cv[:, tb * 512:(tb + 1) * 512], self.ps[b][:], op=ALU.mult),
                            reads=[ck], writes=[gkey], excl=['ps%d' % b])
            for cs in range(4):
                w2v, w2k = self.wload(self.d_fwout[l, f0 * 128:f1 * 128, cs * 512:(cs + 1) * 512], nf, 512)
                for jj in range(4):
                    j = cs * 4 + jj
                    g2 = self.modv[:, l, g, 80 + j:80 + j + 1]
                    for tb in range(ntb):
                        b = self.bank()
                        for k in range(nf):
                            self.pe(lambda e, b=b, k=k, tb=tb, jj=jj, w2v=w2v: e.matmul(
                                self.ps[b][:], w2v[:, k, jj * 128:(jj + 1) * 128], gb[:, par * 8 + k, tb * 512:(tb + 1) * 512],
                                start=(k == 0), stop=(k == nf - 1)), reads=[w2k, gkey], excl=['ps%d' % b])
                        self.dve(lambda e, b=b, j=j, tb=tb, g2=g2: e.scalar_tensor_tensor(
                            self.x[:, j, tb * 512:(tb + 1) * 512], self.ps[b][:], g2, self.x[:, j, tb * 512:(tb + 1) * 512], op0=ALU.mult, op1=ALU.add),
                            reads=['modv', 'x'], writes=['x'], excl=['ps%d' % b])
    self.S.barrier()
    for c_ in reversed(wbx):
        self.wbufs.pop()
        c_.__exit__(None, None, None)
    self.w_rr = 0


def _final_norm(self, grp):
    NT = grp['NT']
    o, _ = PPC['normf']
    with self.sbc("yo0", [128, 1024]) as yo0, self.sbc("yo1", [128, 1024]) as yo1:
        yo = [yo0, yo1]
        yv = self.yT[grp['name']].rearrange("(k p) n -> p k n", p=128)

        def out_fn(kt):
            return yo[kt % 2][:, 0:NT]

        _norm_final_impl(self, grp, yo, yv, o)


def _norm_final_impl(self, grp, yo, yv, o):
    NT = grp['NT']
    ntb = NT // 512
    with self.sbc("rstd", [128, 1024]) as rstd, self.sbc("sq0", [128, 512], BF16) as sq0, self.sbc("sq1", [128, 512], BF16) as sq1:
        sq = [sq0, sq1]
        for tb in range(ntb):
            b = self.bank()
            bk = 'ps%d' % b
            ts = slice(tb * 512, (tb + 1) * 512)
            for kt in range(16):
                s_ = sq[kt % 2]
                sk = 'sq%d' % (kt % 2)
                self.act(lambda e, kt=kt, s_=s_: e.activation(s_[:], self.x[:, kt, ts], AF.Square), reads=['x'], writes=[sk])
                self.pe(lambda e, kt=kt, s_=s_: e.matmul(self.ps[b][:], self.ones_bf[:], s_[:], start=(kt == 0), stop=(kt == 15)),
                        reads=[sk, 'ones_bf'], excl=[bk])
            self.act(lambda e: e.activation(rstd[:, ts], self.ps[b][:], AF.Sqrt, bias=self.cst[:, 0:1], scale=1.0 / D),
                     reads=['cst'], writes=['rstd'], excl=[bk])
        self.dve(lambda e: e.reciprocal(rstd[:, 0:NT], rstd[:, 0:NT]), reads=['rstd'], writes=['rstd'])
        for kt in range(16):
            t_ = yo[kt % 2]
            tk = 'yo%d' % (kt % 2)
            self.dve(lambda e, kt=kt, t_=t_: e.scalar_tensor_tensor(t_[:, 0:NT], self.x[:, kt, 0:NT], self.pp[:, 0, o + kt:o + kt + 1], rstd[:, 0:NT],
                                                                    op0=ALU.mult, op1=ALU.mult), reads=['x', 'rstd', 'pp'], writes=[tk])
            self.st(yv[:, kt, :], t_[:, 0:NT], reads=[tk])
    self.S.barrier()


def _build(self):
    cfg = self.cfg
    _setup(self)
    _setup_mixer(self)
    self.mark('start')
    _prologue_mod(self)
    self.S.barrier()
    self.mark('prologue_end')
    layers = cfg.get('layers', list(range(DEPTH)))
    phases = cfg.get('phases', ('mix', 'ffn'))
    for grp in GROUPS:
        if grp['name'] not in cfg.get('groups', ('s', 'p')):
            continue
        NT = grp['NT']
        xv = self.xT[grp['name']].rearrange("(k p) n -> p k n", p=128)
        evs_ = [self.ld(self.x[:, kt, 0:NT], xv[:, kt, :], writes=['x']) for kt in range(16)]
        for e_ in ('pe', 'act', 'dve', 'pool'):
            self.S._need(e_, evs_)
        for l in layers:
            if self.cfg.get('wcache', True) and len(cfg.get('groups', ('s', 'p'))) == 2:
                self.wc_mode = ('store' if grp['name'] == 's' else 'load', l)
                self.wc_idx[l] = 0
            if 'mix' in phases:
                _norm_mod(self, grp, l, 0)
                self.mark(grp['name'] + ' l' + str(l) + ' norm1_end')
                _mixer(self, grp, l)
                self.mark(grp['name'] + ' l' + str(l) + ' mixer_end')
            if 'ffn' in phases:
                _norm_mod(self, grp, l, 1)
                _ffn(self, grp, l)
                self.mark(grp['name'] + ' l' + str(l) + ' ffn_end')
            self.wc_mode = None
        _final_norm(self, grp)
        if not grp['lat']:
            for si in range(2):
                self.st(self.o_s5[si].rearrange("l p c -> p l c"), self.s5o[:, si, :, :], reads=['s5o'])
    self.S.final()
    return self.nc


def rope_tables():
    t = np.arange(1024)
    inv = (10000.0 ** (-np.arange(16, dtype=np.float32) / 16.0)).astype(np.float32)
    cos = np.zeros((64, 1024), np.float32)
    sin = np.zeros((64, 1024), np.float32)
    for half, pos in ((0, t // 64), (1, t % 64)):
        ang = pos.astype(np.float32)[None, :] * inv[:, None]
        c_, s_ = np.cos(ang).astype(np.float32), np.sin(ang).astype(np.float32)
        cos[half * 32:half * 32 + 16] = c_
        cos[half * 32 + 16:half * 32 + 32] = c_
        sin[half * 32:half * 32 + 16] = -s_
        sin[half * 32 + 16:half * 32 + 32] = s_
    return np.stack([np.concatenate([cos, cos], 0), np.concatenate([sin, sin], 0)], 0)


def host_consts():
    c = np.zeros((128, 4, 128), np.float32)
    idx = np.arange(128)
    c[idx ^ 16, 2, idx] = 1.0
    c[:, 0, :] = np.eye(128, dtype=np.float32)
    c[0:64, 1, 0:64] = 1.0
    c[64:128, 1, 64:128] = 1.0
    return c.reshape(128, 512)


def make_in_maps(inp, cfg=None, ncores=8):
    cfg = cfg or {}
    pp = np.stack([pack_params(inp, l) for l in range(DEPTH)], axis=0)
    consts = host_consts()
    shared = dict(pp=pp, consts=consts,
                  mod_w=np.asarray(inp['mod_w'], np.float32), w_in=np.asarray(inp['w_in'], np.float32),
                  branch_w=np.asarray(inp['branch_w'], np.float32).reshape(DEPTH, 4 * W, D),
                  w_out=np.asarray(inp['w_out'], np.float32), ffn_w_in=np.asarray(inp['ffn_w_in'], np.float32),
                  ffn_w_out=np.asarray(inp['ffn_w_out'], np.float32))
    maps = []
    for c in range(ncores):
        m = dict(shared)
        m['xT_s'] = np.ascontiguousarray(np.asarray(inp['x_sample'][c], np.float32).T)
        m['xT_p'] = np.ascontiguousarray(np.asarray(inp['x_prompt'][2 * c:2 * c + 2], np.float32).reshape(512, D).T)
        cc = np.stack([_fm(inp['c'][c], 16), _fm(inp['c_ctx'], 16)], axis=-1)
        m['cc'] = np.ascontiguousarray(cc)
        _mixer_in_maps(inp, c, m)
        maps.append(m)
    return maps


def _proj(self, l, col0, ncols):
    return self.wload(self.d_win[l, :, col0:col0 + ncols], 16, ncols)


def _proj_tile(self, wv, wk, ci, NT):
    banks = []
    for tb in range(NT // 512):
        b = self.bank()
        for kt in range(16):
            self.pe(lambda e, b=b, kt=kt, tb=tb: e.matmul(self.ps[b][:], wv[:, kt, ci * 128:(ci + 1) * 128], self.h[:, kt, tb * 512:(tb + 1) * 512],
                                                           start=(kt == 0), stop=(kt == 15)), reads=[wk, 'h'], excl=['ps%d' % b])
        banks.append(b)
    return banks


def _evac(self, eng, dst, dkey, banks, func=None, reads=(), **kw):
    for tb, b in enumerate(banks):
        if eng == 'act':
            self.act(lambda e, b=b, tb=tb: e.activation(dst[:, tb * 512:(tb + 1) * 512], self.ps[b][:], func or AF.Copy, **kw),
                     reads=reads, writes=[dkey], excl=['ps%d' % b])
        else:
            self.dve(lambda e, b=b, tb=tb: e.tensor_copy(dst[:, tb * 512:(tb + 1) * 512], self.ps[b][:]), reads=reads, writes=[dkey], excl=['ps%d' % b])


def _gelu_from(self, src, skey, NT, t1, t1k, out, okey):
    self.act(lambda e: e.activation(t1[:, 0:NT], src[:, 0:NT], AF.Square), reads=[skey], writes=[t1k])
    self.dve(lambda e: e.tensor_scalar(t1[:, 0:NT], t1[:, 0:NT], 0.044715, 1.0, op0=ALU.mult, op1=ALU.add), reads=[t1k], writes=[t1k])
    self.dve(lambda e: e.tensor_tensor(t1[:, 0:NT], t1[:, 0:NT], src[:, 0:NT], op=ALU.mult), reads=[t1k, skey], writes=[t1k])
    self.act(lambda e: e.activation(t1[:, 0:NT], t1[:, 0:NT], AF.Sigmoid, scale=C_GELU), reads=[t1k], writes=[t1k])
    self.dve(lambda e: e.tensor_tensor(out[:, 0:NT], t1[:, 0:NT], src[:, 0:NT], op=ALU.mult), reads=[t1k, skey], writes=[okey])


def _mix_A(self, grp, l, T):
    NT, seqs, lat = grp['NT'], grp['seqs'], grp['lat']
    lcw, _ = PPC['lcw']
    with self.sbc("lruw", [128, 16, 128]) as lruw, self.sbc("clam", [128, 8]) as clam:
        self.ld(lruw[:], self.d_lruw[l].rearrange("m p n -> p m n"), writes=['lruw'])
        self.act(lambda e: e.activation(clam[:], ppc(self, l, 'llam', 0, 8), AF.Sigmoid), reads=['pp'], writes=['clam'])
        self.act(lambda e: e.activation(clam[:], clam[:], AF.Ln), reads=['clam'], writes=['clam'])
        self.dve(lambda e: e.tensor_scalar(clam[:], clam[:], 8.0, None, op0=ALU.mult), reads=['clam'], writes=['clam'])
        axs, xa, av, ig, hs0, hs1, t1, agc = T[0:8]
        hs = [hs0, hs1]
        for i in range(4):
            wx_, wxk = _proj(self, l, OFF['ax'] + i * 128, 128)
            wg_, wgk = _proj(self, l, OFF['ag'] + i * 128, 128)
            bx_ = _proj_tile(self, wx_, wxk, 0, NT)
            bg_ = _proj_tile(self, wg_, wgk, 0, NT)
            _evac(self, 'act', axs, 'T0', bx_)
            _evac(self, 'act', agc, 'T7', bg_)
            wj = [self.pp[:, l, lcw + j * 4 + i:lcw + j * 4 + i + 1] for j in range(4)]
            self.dve(lambda e: e.tensor_scalar(xa[:, 0:NT], axs[:, 0:NT], wj[2], ppc(self, l, 'lcb', i), op0=ALU.mult, op1=ALU.add),
                     reads=['T0', 'pp'], writes=['T1'])
            for (o, L) in seqs:
                for (j, sh) in ((0, -2), (1, -1), (3, 1)):
                    if sh < 0:
                        dst = xa[:, o - sh:o + L]
                        src = axs[:, o:o + L + sh]
                    else:
                        dst = xa[:, o:o + L - sh]
                        src = axs[:, o + sh:o + L]
                    self.dve(lambda e, dst=dst, src=src, j=j: e.scalar_tensor_tensor(dst, src, wj[j], dst, op0=ALU.mult, op1=ALU.add),
                             reads=['T0', 'T1', 'pp'], writes=['T1'])
            self.dump('xa%d' % i, xa[:, 0:NT], 'T1', [128, NT])
            for d in range(2):
                ba_ = [self.bank() for _ in range(NT // 512)]
                bi_ = [self.bank() for _ in range(NT // 512)]
                for (wi, banks) in ((0, ba_), (1, bi_)):
                    mi = (wi * 2 + d) * 4 + i
                    for tb, b in enumerate(banks):
                        self.pe(lambda e, b=b, tb=tb, mi=mi: e.matmul(self.ps[b][:], lruw[:, mi, :], xa[:, tb * 512:(tb + 1) * 512], start=True, stop=True),
                                reads=['lruw', 'T1'], excl=['ps%d' % b])
                _evac(self, 'act', av, 'T2', ba_, AF.Sigmoid, reads=['pp'], bias=ppc(self, l, 'lba', d * 4 + i))
                _evac(self, 'act', ig, 'T3', bi_, AF.Sigmoid, reads=['pp'], bias=ppc(self, l, 'lbx', d * 4 + i))
                self.dve(lambda e, d=d: e.tensor_scalar(t1[:, 0:NT], av[:, 0:NT], clam[:, d * 4 + i:d * 4 + i + 1], None, op0=ALU.mult), reads=['T2', 'clam'], writes=['T6'])
                self.dve(lambda e: e.tensor_scalar(av[:, 0:NT], t1[:, 0:NT], 1.0 / 120.0, None, op0=ALU.mult), reads=['T6'], writes=['T2'])
                for ck_ in (1.0 / 24.0, 1.0 / 6.0, 0.5, 1.0):
                    self.dve(lambda e, ck_=ck_: e.scalar_tensor_tensor(av[:, 0:NT], av[:, 0:NT], ck_, t1[:, 0:NT], op0=ALU.add, op1=ALU.mult), reads=['T2', 'T6'], writes=['T2'])
                self.dve(lambda e: e.scalar_tensor_tensor(t1[:, 0:NT], av[:, 0:NT], 2.0, av[:, 0:NT], op0=ALU.add, op1=ALU.mult), reads=['T2', 'T6'], writes=['T6'])
                self.act(lambda e: e.activation(t1[:, 0:NT], t1[:, 0:NT], AF.Sqrt, scale=-1.0), reads=['T6'], writes=['T6'])
                self.dve(lambda e: e.tensor_scalar(av[:, 0:NT], av[:, 0:NT], 1.0, None, op0=ALU.add), reads=['T2', 'T6'], writes=['T2'])
                self.dve(lambda e: e.tensor_tensor(ig[:, 0:NT], ig[:, 0:NT], xa[:, 0:NT], op=ALU.mult), reads=['T3', 'T1'], writes=['T3'])
                self.dve(lambda e: e.tensor_tensor(ig[:, 0:NT], ig[:, 0:NT], t1[:, 0:NT], op=ALU.mult), reads=['T3', 'T6'], writes=['T3'])
                hk = 'T%d' % (4 + d)
                for si, (o, L) in enumerate(seqs):
                    h0 = self.lat_lru[:, (l * 2 + d) * 4 + i:(l * 2 + d) * 4 + i + 1] if lat else 0.0
                    rk = ['T2', 'T3'] + (['lat_lru'] if lat else [])
                    if d == 0:
                        self.dve(lambda e, o=o, L=L, h0=h0: e.tensor_tensor_scan(hs[0][:, o:o + L], av[:, o:o + L], ig[:, o:o + L], h0, op0=ALU.mult, op1=ALU.add),
                                 reads=rk, writes=[hk])
                    else:
                        self.dve(lambda e, o=o, L=L, h0=h0: e.tensor_tensor_scan(hs[1][:, o:o + L][:, ::-1], av[:, o:o + L][:, ::-1], ig[:, o:o + L][:, ::-1], h0,
                                                                                  op0=ALU.mult, op1=ALU.add), reads=rk, writes=[hk])
                    if not lat:
                        col = o + L - 1 if d == 0 else o
                        self.st(self.o_lru[si, l, d, i * 128:(i + 1) * 128].rearrange("(p a) -> p a", a=1), hs[d][:, col:col + 1], reads=[hk])
            self.dve(lambda e: e.tensor_tensor(hs0[:, 0:NT], hs0[:, 0:NT], hs1[:, 0:NT], op=ALU.add), reads=['T4', 'T5'], writes=['T4'])
            _gelu_from(self, agc, 'T7', NT, t1, 'T6', agc, 'T7')
            self.dve(lambda e, i=i: e.tensor_tensor(self.yb[:, i, 0:NT], agc[:, 0:NT], hs0[:, 0:NT], op=ALU.mult), reads=['T7', 'T4'], writes=['yb'])
    self.S.barrier()


def _merge_out(self, grp, l):
    NT = grp['NT']
    g = grp['gi']
    ntb = NT // 512
    wb4 = self.sbc("wbuf3", [128, 4096], BF16)
    self.wbufs.append(wb4.__enter__())
    self.w_rr = 0
    with self.sbc("z", [128, 4, 1024]) as z, self.sbc("zb", [128, 4, 1024], BF16) as zb, self.sbc("sg", [128, 1024], BF16) as sg, \
            self.sbc("tm", [128, 1024], BF16) as tm:
        for zc in range(4):
            for n in range(4):
                wb, wbk = self.wload(self.d_bw[l, n * 512:(n + 1) * 512, zc * 512:(zc + 1) * 512], 4, 512)
                for jp in range(2):
                    c0 = OFF['mg'] + n * D + (zc * 4 + jp * 2) * 128
                    wg, wgk = _proj(self, l, c0, 256)
                    for j2 in range(2):
                        jj = jp * 2 + j2
                        gb_ = _proj_tile(self, wg, wgk, j2, NT)
                        pb_ = []
                        for tb in range(ntb):
                            b = self.bank()
                            for k in range(4):
                                self.pe(lambda e, b=b, k=k, tb=tb: e.matmul(self.ps[b][:], wb[:, k, jj * 128:(jj + 1) * 128], self.yb[:, n * 4 + k, tb * 512:(tb + 1) * 512],
                                                                          start=(k == 0), stop=(k == 3)), reads=[wbk, 'yb'], excl=['ps%d' % b])
                            pb_.append(b)
                        _evac(self, 'act', sg, 'sg', gb_, AF.Sigmoid)
                        for tb, b in enumerate(pb_):
                            ts = slice(tb * 512, (tb + 1) * 512)
                            if n == 0:
                                self.dve(lambda e, b=b, ts=ts: e.tensor_tensor(z[:, jj, ts], sg[:, ts], self.ps[b][:], op=ALU.mult),
                                         reads=['sg'], writes=['z'], excl=['ps%d' % b])
                            else:
                                self.dve(lambda e, b=b, ts=ts: e.tensor_tensor(tm[:, ts], sg[:, ts], self.ps[b][:], op=ALU.mult),
                                         reads=['sg'], writes=['tm'], excl=['ps%d' % b])
                                self.dve(lambda e, ts=ts: e.tensor_tensor(z[:, jj, ts], z[:, jj, ts], tm[:, ts], op=ALU.add),
                                         reads=['tm', 'z'], writes=['z'])
            for jj in range(4):
                self.act(lambda e, jj=jj: e.activation(zb[:, jj, 0:NT], z[:, jj, 0:NT], AF.Copy), reads=['z'], writes=['zb'])
            if zc == 0:
                self.dump('z0', zb[:, :, 0:NT], 'zb', [128, 4, NT])
            for cs in range(4):
                wo, wok = self.wload(self.d_wout[l, zc * 512:(zc + 1) * 512, cs * 512:(cs + 1) * 512], 4, 512)
                for jj in range(4):
                    jo = cs * 4 + jj
                    g1 = self.modv[:, l, g, 32 + jo:32 + jo + 1]
                    for tb in range(ntb):
                        b = self.bank()
                        ts = slice(tb * 512, (tb + 1) * 512)
                        for k in range(4):
                            self.pe(lambda e, b=b, k=k, ts=ts: e.matmul(self.ps[b][:], wo[:, k, jj * 128:(jj + 1) * 128], zb[:, k, ts], start=(k == 0), stop=(k == 3)),
                                    reads=[wok, 'zb'], excl=['ps%d' % b])
                        self.dve(lambda e, b=b, jo=jo, ts=ts, g1=g1: e.scalar_tensor_tensor(self.x[:, jo, ts], self.ps[b][:], g1, self.x[:, jo, ts], op0=ALU.mult, op1=ALU.add),
                                 reads=['modv', 'x'], writes=['x'], excl=['ps%d' % b])
    self.S.barrier()
    self.wbufs.pop()
    wb4.__exit__(None, None, None)
    self.w_rr = 0


def _mixer(self, grp, l):
    NT = grp['NT']
    parts = self.cfg.get('parts', 'ABCD')
    self.S.barrier()
    for kt in range(16):
        self.st(self.xspill[:, kt, 0:NT], self.x[:, kt, 0:NT], reads=['x'])
    if 'A' not in parts:
        self.S.barrier()
    if 'A' in parts:
        Tcm = [self.sbc("T%d" % i, [128, 1024]) for i in range(8)]
        T = [c.__enter__() for c in Tcm]
        _mix_A(self, grp, l, T)
        self.mark(grp['name'] + ' l' + str(l) + ' mixA_end')
        for c in reversed(Tcm):
            c.__exit__(None, None, None)
    if 'B' in parts:
        _mix_B(self, grp, l)
        self.mark(grp['name'] + ' l' + str(l) + ' mixB_end')
    if 'C' in parts:
        _mix_C(self, grp, l)
        self.mark(grp['name'] + ' l' + str(l) + ' mixC_end')
    if 'D' in parts:
        _mix_D(self, grp, l)
        self.mark(grp['name'] + ' l' + str(l) + ' mixD_end')
    self.S.barrier()
    evs_ = [self.ld(self.x[:, kt, 0:NT], self.xspill[:, kt, 0:NT], writes=['x']) for kt in range(16)]
    for e_ in ('pe', 'act', 'dve', 'pool'):
        self.S._need(e_, evs_)
    for n in range(4):
        self.dump('y%d' % n, self.yb[:, n * 4:(n + 1) * 4, 0:NT], 'yb', [128, 4, NT])
    if self.cfg.get('merge', True):
        _merge_out(self, grp, l)


def _setup_mixer(self):
    self.d_lruw = self.din("lruw", [DEPTH, 16, 128, 128])
    self.d_latlru = self.din("lat_lru", [128, DEPTH * 2 * 4])
    self.lat_lru = self.sb("lat_lru", [128, DEPTH * 2 * 4])
    self.ld(self.lat_lru[:], self.d_latlru, writes=['lat_lru'])
    self.o_lru = self.dout("o_lru", [2, DEPTH, 2, W])
    self.d_s5bl = self.din("s5bl", [DEPTH, 4, 128, 1024])
    self.d_s5cl = self.din("s5cl", [DEPTH, 4, 128, 1024])
    self.d_glu = self.din("s5_glu_w", [DEPTH, W, W])
    self.d_lats5 = self.din("lat_s5", [128, DEPTH * 2 * 32])
    self.lat_s5 = self.sb("lat_s5", [128, DEPTH * 2 * 32])
    self.ld(self.lat_s5[:], self.d_lats5, writes=['lat_s5'])
    self.o_s5 = self.dout("o_s5", [2, DEPTH, 128, 64])
    self.d_amask = self.din("amask", [128, 2, 128])
    self.d_kcT = self.din("kcT", [DEPTH, 128, 2, 256])
    self.d_vc = self.din("vc", [DEPTH, 256, 128])
    self.d_rope = self.din("rope", [2, 128, 1024])
    self.d_wkdup = self.din("wkdup", [DEPTH, 2, D, 128])
    self.o_k = self.dout("o_k", [2, DEPTH, 256, 128])
    self.o_v = self.dout("o_v", [2, DEPTH, 256, 128])
    self.d_dmask = self.din("dmask", [128, 4, 512])
    self.d_lora = self.din("lora", [DEPTH, 4, 128, 5, 128])
    self.d_latwkv = self.din("lat_wkv", [DEPTH, 2, 4, 128, 128])
    self.o_wkv = self.dout("o_wkv", [2, DEPTH, 2, 8, 64, 64])
    self.xspill = self.nc.dram_tensor("xspill", [128, 16, 1024], F32, kind="Internal").ap()
    self.wcache = [self.nc.dram_tensor("wcache%d" % l_, [WC_MAX, 128, 4096], BF16, kind="Internal").ap() for l_ in range(DEPTH)]
    self.wc_n = [0] * DEPTH
    self.wc_idx = [0] * DEPTH
    self.wc_mode = None
    self.s5o = self.sb("s5o", [128, 2, DEPTH, 64])
    self.pool(lambda e: e.memset(self.s5o[:], 0.0), writes=['s5o'])
    self.d_iota = self.din("iota", [128, 512])
    self.iota = self.sb("iota", [128, 512])
    self.ld(self.iota[:], self.d_iota, writes=['iota'])
    self.cst2 = self.sb("cst2", [128, 2])
    self.pool(lambda e: e.memset(self.cst2[:, 0:1], math.pi / 2.0), writes=['cst'])


def _mixer_in_maps(inp, c, m):
    lw = np.zeros((DEPTH, 2, 2, 4, 128, 128), np.float32)
    for l in range(DEPTH):
        for wi, nm in enumerate(('lru_wa', 'lru_wx')):
            for d in range(2):
                for i in range(4):
                    for hb in range(2):
                        lw[l, wi, d, i, hb * 64:(hb + 1) * 64, hb * 64:(hb + 1) * 64] = inp[nm][l, d, 2 * i + hb]
    m['lruw'] = lw.reshape(DEPTH, 16, 128, 128)
    sl = np.asarray(inp['state_lru'][c], np.float32)
    m['lat_lru'] = np.ascontiguousarray(sl.reshape(DEPTH * 2 * 4, 128).T)
    kk = np.arange(128)[:, None]
    qq = np.arange(128)[None, :]
    m['amask'] = np.stack([(kk >= qq), (kk <= qq)], axis=1).astype(np.float32)
    ck = np.asarray(inp['cache_k'][c], np.float32)
    kct = ck.transpose(0, 2, 3, 1)
    m['kcT'] = np.ascontiguousarray(np.stack([kct, kct], axis=2).reshape(DEPTH, 2, 128, 256).transpose(0, 2, 1, 3))
    m['vc'] = np.ascontiguousarray(np.asarray(inp['cache_v'][c], np.float32).reshape(DEPTH, 256, 128))
    m['rope'] = rope_tables()
    wk = np.asarray(inp['w_in'], np.float32)[:, :, OFF['ck']:OFF['ck'] + 128].reshape(DEPTH, D, 2, 64)
    m['wkdup'] = np.ascontiguousarray(np.stack([wk, wk], axis=3).transpose(0, 2, 1, 3, 4).reshape(DEPTH, 2, D, 128))
    p_ = np.arange(128)[:, None]
    f_ = np.arange(128)[None, :]
    Us, Ui, Ls, Li = [a.astype(np.float32) for a in ((f_ > p_), (f_ >= p_), (f_ < p_), (f_ <= p_))]
    dm = np.zeros((128, 4, 4, 128), np.float32)
    for u in range(4):
        fw = u < 2
        dm[:, 0, u, :] = -(Us if fw else Ls)
        dm[:, 1, u, :] = -(Ls if fw else Us)
        dm[:, 2, u, :] = (Us if fw else Ls)
        dm[:, 3, u, :] = (Ui if fw else Li)
    m['dmask'] = dm.reshape(128, 4, 512)
    lo = np.zeros((DEPTH, 4, 128, 5, 128), np.float32)
    for l in range(DEPTH):
        for i in range(4):
            cs = slice(i * 128, (i + 1) * 128)
            for d in range(2):
                lo[l, i, 0:64, d, :] = inp['rwkv_w2'][l, d][:, cs]
                lo[l, i, 64:128, 2 + d, :] = inp['rwkv_a2'][l, d][:, cs]
            lo[l, i, :, 4, :] = inp['rwkv_g2'][l][:, cs]
    m['lora'] = lo
    sw = np.asarray(inp['state_wkv'][c], np.float32)
    lwk = np.zeros((DEPTH, 2, 4, 128, 128), np.float32)
    for i in range(4):
        for hh in range(2):
            lwk[:, :, i, hh * 64:(hh + 1) * 64, hh * 64:(hh + 1) * 64] = sw[:, :, 2 * i + hh].transpose(0, 1, 3, 2)
    m['lat_wkv'] = lwk
    m['iota'] = np.broadcast_to(np.arange(1, 513, dtype=np.float32)[None, :], (128, 512)).copy()
    m['s5_glu_w'] = np.asarray(inp['s5_glu_w'], np.float32)
    bl = np.zeros((DEPTH, 4, 8, 16, 2, 2, 2, 2, 64), np.float32)
    cl = np.zeros((DEPTH, 4, 2, 64, 2, 2, 4, 4, 16), np.float32)
    for l in range(DEPTH):
        for d in range(2):
            for r, (bn, cn) in enumerate((('s5_b_re', 's5_c_re'), ('s5_b_im', 's5_c_im'))):
                Bm = np.asarray(inp[bn][l, d], np.float32)
                Cm = np.asarray(inp[cn][l, d], np.float32)
                for ci in range(4):
                    for g8 in range(8):
                        g = ci * 8 + g8
                        j = (g8 % 4) // 2
                        g2 = g8 % 2
                        bl[l, ci, g8, :, d, r, j, g2, :] = Bm[g].T
                        hf = g8 // 4
                        g4 = g8 % 4
                        stl = hf * 2 + j
                        cl[l, ci, g2, :, d, r, stl, g4, :] = Cm[g].T
    m['s5bl'] = bl.reshape(DEPTH, 4, 128, 1024)
    m['s5cl'] = cl.reshape(DEPTH, 4, 128, 1024)
    s5 = np.asarray(inp['state_s5'][c], np.float32)
    t = s5.reshape(DEPTH, 2, 2, 16, 2, 64).transpose(4, 5, 0, 2, 1, 3)
    m['lat_s5'] = np.ascontiguousarray(t.reshape(128, DEPTH * 2 * 32))


TWO_PI = 2.0 * math.pi


def _mix_B(self, grp, l):
    NT, seqs, lat = grp['NT'], grp['seqs'], grp['lat']
    ntb = NT // 512
    SM = 'smB'
    X = [self.x[:, k, :] for k in range(16)]
    sets = []
    specs = [("s5sm", [128, 24, 32], F32), ("s5i0", [128, 512], I32), ("s5i1", [128, 512], I32), ("hri", [128, 4, 1024], BF16), ("ub", [128, 1024], BF16),
             ("Bl", [128, 1024], BF16), ("C12", [128, 1024], BF16), ("carry", [128, 4], F32), ("fin", [128, 2, 2, 32], F32)]
    cms = [self.sbc(n, sh, dt) for (n, sh, dt) in specs]
    (sm, ti0, ti1, hri, ub, Bl, C12, carry, fin) = [c.__enter__() for c in cms]
    ti = ti0
    for d_ in range(2):
        xs = [X[3 * d_ + q][:, h_ * 512:(h_ + 1) * 512] for q in range(3) for h_ in range(2)]
        sets.append(tuple(xs) + ((ti0, ti1)[d_],))
    ys, uf = X[6], X[7]
    A2 = sets[0][3]
    if True:
        are = ppc(self, l, 'are', 0, 32)
        aim = ppc(self, l, 'aim', 0, 32)
        ldt = ppc(self, l, 'ldt', 0, 32)
        (dt, mag, phi, t0, t1, cr, sr, abr, abi, den, fr, fi, h0r, h0i, cL, sL, nfi, t2, t3) = [sm[:, i, :] for i in range(19)]
        smi = ti[:, 0:32]

        def V(fn, extra=()):
            self.dve(fn, reads=[SM, 'pp'] + list(extra), writes=[SM])

        def Ac(fn, extra=()):
            self.act(fn, reads=[SM, 'pp', 'cst'] + list(extra), writes=[SM])

        def sincos(ang_turns, c_out, s_out):
            V(lambda e: e.tensor_copy(smi, ang_turns))
            V(lambda e: e.tensor_copy(t1, smi))
            V(lambda e: e.tensor_tensor(t1, ang_turns, t1, op=ALU.subtract))
            Ac(lambda e: e.activation(s_out, t1, AF.Sin, scale=TWO_PI))
            Ac(lambda e: e.activation(t1, t1, AF.Abs))
            Ac(lambda e: e.activation(c_out, t1, AF.Sin, scale=-TWO_PI, bias=self.cst2[:, 0:1]))

        Ac(lambda e: e.activation(dt, ldt, AF.Exp))
        V(lambda e: e.tensor_tensor(t0, are, dt, op=ALU.mult))
        V(lambda e: e.tensor_scalar(mag, t0, 1.0 / 120.0, None, op0=ALU.mult))
        for ck_ in (1.0 / 24.0, 1.0 / 6.0, 0.5, 1.0):
            V(lambda e, ck_=ck_: e.scalar_tensor_tensor(mag, mag, ck_, t0, op0=ALU.add, op1=ALU.mult))
        V(lambda e: e.tensor_scalar(mag, mag, 1.0, None, op0=ALU.add))
        V(lambda e: e.scalar_tensor_tensor(phi, aim, 1.0 / TWO_PI, dt, op0=ALU.mult, op1=ALU.mult))
        sincos(phi, cr, sr)
        V(lambda e: e.tensor_tensor(abr, mag, cr, op=ALU.mult))
        V(lambda e: e.tensor_tensor(abi, mag, sr, op=ALU.mult))
        V(lambda e: e.tensor_tensor(den, are, are, op=ALU.mult))
        V(lambda e: e.tensor_tensor(t0, aim, aim, op=ALU.mult))
        V(lambda e: e.tensor_tensor(den, den, t0, op=ALU.add))
        V(lambda e: e.reciprocal(den, den))
        V(lambda e: e.tensor_scalar(t0, abr, -1.0, None, op0=ALU.add))
        V(lambda e: e.tensor_tensor(fr, t0, are, op=ALU.mult))
        V(lambda e: e.tensor_tensor(t2, abi, aim, op=ALU.mult))
        V(lambda e: e.tensor_tensor(fr, fr, t2, op=ALU.add))
        V(lambda e: e.tensor_tensor(fr, fr, den, op=ALU.mult))
        V(lambda e: e.tensor_tensor(fi, abi, are, op=ALU.mult))
        V(lambda e: e.tensor_tensor(t2, t0, aim, op=ALU.mult))
        V(lambda e: e.tensor_tensor(fi, fi, t2, op=ALU.subtract))
        V(lambda e: e.tensor_tensor(fi, fi, den, op=ALU.mult))
        V(lambda e: e.tensor_scalar(nfi, fi, -1.0, None, op0=ALU.mult))
        if lat:
            s0r = self.lat_s5[:, (l * 2 + 0) * 32:(l * 2 + 0) * 32 + 32]
            s0i = self.lat_s5[:, (l * 2 + 1) * 32:(l * 2 + 1) * 32 + 32]
            V(lambda e: e.tensor_tensor(t0, fr, fr, op=ALU.mult))
            V(lambda e: e.tensor_tensor(t2, fi, fi, op=ALU.mult))
            V(lambda e: e.tensor_tensor(t0, t0, t2, op=ALU.add))
            V(lambda e: e.reciprocal(t0, t0))
            V(lambda e: e.tensor_tensor(h0r, s0r, fr, op=ALU.mult), ['lat_s5'])
            V(lambda e: e.tensor_tensor(t2, s0i, fi, op=ALU.mult), ['lat_s5'])
            V(lambda e: e.tensor_tensor(h0r, h0r, t2, op=ALU.add))
            V(lambda e: e.tensor_tensor(h0r, h0r, t0, op=ALU.mult))
            V(lambda e: e.tensor_tensor(h0i, s0i, fr, op=ALU.mult), ['lat_s5'])
            V(lambda e: e.tensor_tensor(t2, s0r, fi, op=ALU.mult), ['lat_s5'])
            V(lambda e: e.tensor_tensor(h0i, h0i, t2, op=ALU.subtract))
            V(lambda e: e.tensor_tensor(h0i, h0i, t0, op=ALU.mult))
        else:
            Lq = seqs[0][1]
            V(lambda e: e.tensor_scalar(t3, phi, float(Lq), None, op0=ALU.mult))
            sincos(t3, cL, sL)

        def unit(ci, hf, hs_, j, stl, st, d, col, si, so, L, nblk, Lb, bi_, bk_):
            cosT, sinT, A1, A2, gr, gi, ti = sets[d]
            k0, k1, k2, k3, k4, k5, kti, khri, kcar = ['%s_%d' % (n, d) for n in ('T0', 'T1', 'T2', 'T3', 'T4', 'T5', 'ti', 'hri', 'carry')]
            o = so + bk_ * Lb
            cs_ = slice(o, o + Lb)
            ls_ = slice(0, Lb)
            for r_, dst, dk in ((0, A1, k2), (1, A2, k3)):
                b = self.bank()
                self.pe(lambda e, b=b, r_=r_: e.matmul(self.ps[b][:, 0:Lb], blv[hs_, d, r_, j, :], ub[hs_, cs_], start=True, stop=True),
                        reads=['Bl', 'ub'], excl=['ps%d' % b])
                self.act(lambda e, b=b, dst=dst: e.activation(dst[:, ls_], self.ps[b][:, 0:Lb], AF.Copy), writes=[dk], excl=['ps%d' % b])
            tl = bk_ * Lb
            if d == 0:
                io = self.iota[:, 0:Lb]
                ioff = float(tl)
            else:
                io = self.iota[:, 0:Lb][:, ::-1]
                ioff = float(L - tl - Lb)
            ph = sm[:, 2, col:col + 1]
            share_tab = (not lat) and si > 0 and nblk == 1 and seqs[si][1] == seqs[0][1]
            if not share_tab:
                self.dve(lambda e: e.tensor_scalar(sinT[:, ls_], io, ioff, ph, op0=ALU.add, op1=ALU.mult), reads=['iota', SM], writes=[k1])
                yield
                self.dve(lambda e: e.tensor_copy(ti[:, ls_], sinT[:, ls_]), reads=[k1], writes=[kti])
                yield
                self.dve(lambda e: e.tensor_copy(cosT[:, ls_], ti[:, ls_]), reads=[kti], writes=[k0])
                yield
                self.dve(lambda e: e.tensor_tensor(cosT[:, ls_], sinT[:, ls_], cosT[:, ls_], op=ALU.subtract), reads=[k1, k0], writes=[k0])
                yield
                self.act(lambda e: e.activation(sinT[:, ls_], cosT[:, ls_], AF.Sin, scale=TWO_PI), reads=[k0], writes=[k1])
                self.act(lambda e: e.activation(cosT[:, ls_], cosT[:, ls_], AF.Abs), reads=[k0, k1], writes=[k0])
                self.act(lambda e: e.activation(cosT[:, ls_], cosT[:, ls_], AF.Sin, scale=-TWO_PI, bias=self.cst2[:, 0:1]), reads=[k0, 'cst'], writes=[k0])
            yield
            self.dve(lambda e: e.tensor_tensor(gr[:, ls_], A1[:, ls_], cosT[:, ls_], op=ALU.mult), reads=[k2, k0], writes=[k4])
            self.pool(lambda e: e.tensor_tensor(gi[:, ls_], A2[:, ls_], cosT[:, ls_], op=ALU.mult), reads=[k3, k0], writes=[k5])
            yield
            self.dve(lambda e: e.tensor_tensor(A2[:, ls_], A2[:, ls_], sinT[:, ls_], op=ALU.mult), reads=[k3, k1, k5], writes=[k3])
            self.pool(lambda e: e.tensor_tensor(A1[:, ls_], A1[:, ls_], sinT[:, ls_], op=ALU.mult), reads=[k2, k1, k4], writes=[k2])
            yield
            self.dve(lambda e: e.tensor_tensor(gr[:, ls_], gr[:, ls_], A2[:, ls_], op=ALU.add), reads=[k4, k3], writes=[k4])
            self.pool(lambda e: e.tensor_tensor(gi[:, ls_], gi[:, ls_], A1[:, ls_], op=ALU.subtract), reads=[k5, k2], writes=[k5])
            yield
            mg_ = _bcast(sm[:, 1, col:col + 1], [128, Lb])
            if bi_ == 0:
                inr = sm[:, 12, col:col + 1] if lat else 0.0
                ini = sm[:, 13, col:col + 1] if lat else 0.0
            else:
                inr = carry[:, 2 * d:2 * d + 1]
                ini = carry[:, 2 * d + 1:2 * d + 2]
            for (gt_, gk_, in_) in ((gr, k4, inr), (gi, k5, ini)):
                if d == 0:
                    self.dve(lambda e, gt_=gt_, in_=in_: e.tensor_tensor_scan(gt_[:, ls_], mg_, gt_[:, ls_], in_, op0=ALU.mult, op1=ALU.add),
                             reads=[gk_, SM, kcar], writes=[gk_])
                else:
                    self.dve(lambda e, gt_=gt_, in_=in_: e.tensor_tensor_scan(gt_[:, ls_][:, ::-1], mg_, gt_[:, ls_][:, ::-1], in_, op0=ALU.mult, op1=ALU.add),
                             reads=[gk_, SM, kcar], writes=[gk_])
                yield
            ec = Lb - 1 if d == 0 else 0
            if bi_ < nblk - 1:
                self.act(lambda e: e.activation(carry[:, 2 * d:2 * d + 1], gr[:, ec:ec + 1], AF.Copy), reads=[k4], writes=[kcar])
                self.act(lambda e: e.activation(carry[:, 2 * d + 1:2 * d + 2], gi[:, ec:ec + 1], AF.Copy), reads=[k5], writes=[kcar])
            elif not lat:
                self.act(lambda e: e.activation(fin[:, si, 0, col:col + 1], gr[:, ec:ec + 1], AF.Copy), reads=[k4], writes=['fin'])
                self.act(lambda e: e.activation(fin[:, si, 1, col:col + 1], gi[:, ec:ec + 1], AF.Copy), reads=[k5], writes=['fin'])
            self.dve(lambda e: e.tensor_tensor(A1[:, ls_], gr[:, ls_], cosT[:, ls_], op=ALU.mult), reads=[k4, k0], writes=[k2])
            self.pool(lambda e: e.tensor_tensor(A2[:, ls_], gi[:, ls_], cosT[:, ls_], op=ALU.mult), reads=[k5, k0], writes=[k3])
            yield
            self.dve(lambda e: e.tensor_tensor(gi[:, ls_], gi[:, ls_], sinT[:, ls_], op=ALU.mult), reads=[k5, k1, k3], writes=[k5])
            self.pool(lambda e: e.tensor_tensor(gr[:, ls_], gr[:, ls_], sinT[:, ls_], op=ALU.mult), reads=[k4, k1, k2], writes=[k4])
            yield
            self.dve(lambda e: e.tensor_tensor(hri[:, d * 2 + 0, cs_], A1[:, ls_], gi[:, ls_], op=ALU.subtract), reads=[k2, k5], writes=[khri])
            self.pool(lambda e: e.tensor_tensor(hri[:, d * 2 + 1, cs_], A2[:, ls_], gr[:, ls_], op=ALU.add), reads=[k3, k4], writes=[khri])

        self.bank_n = 6
        self.bank_rr = 0
        for ci in range(4):
            wu, wuk = _proj(self, l, OFF['bu'] + ci * 128, 128)
            bu_ = _proj_tile(self, wu, wuk, 0, NT)
            _evac(self, 'act', uf, 'T7', bu_)
            self.dve(lambda e: e.tensor_copy(ub[:, 0:NT], uf[:, 0:NT]), reads=['T7'], writes=['ub'])
            self.S.dma('pool', Bl[:], self.d_s5bl[l, ci], writes=['Bl'])
            cl = ys
            self.ld(cl[:], self.d_s5cl[l, ci], writes=['T6'])
            clv = cl[:].rearrange("p (d r s c) -> p d r s c", d=2, r=2, s=4)
            c12v = C12[:].rearrange("p (d r s c) -> p d r s c", d=2, r=2, s=4)
            tmpv = A2[:, 0:256].rearrange("p (s c) -> p s c", s=4)
            tmp2 = A2[:, 256:512].rearrange("p (s c) -> p s c", s=4)
            for d in range(2):
                frb = _bcast(sm[:, 10, d * 16 + ci * 4:d * 16 + ci * 4 + 4].unsqueeze(2), [128, 4, 64])
                fib = _bcast(sm[:, 11, d * 16 + ci * 4:d * 16 + ci * 4 + 4].unsqueeze(2), [128, 4, 64])
                self.dve(lambda e, d=d, frb=frb: e.tensor_tensor(tmpv, clv[:, d, 0], frb, op=ALU.mult), reads=['T6', SM], writes=['T3_0'])
                self.dve(lambda e, d=d, fib=fib: e.tensor_tensor(tmp2, clv[:, d, 1], fib, op=ALU.mult), reads=['T6', SM], writes=['T3_0'])
                self.dve(lambda e, d=d: e.tensor_tensor(c12v[:, d, 0], tmpv, tmp2, op=ALU.subtract), reads=['T3_0'], writes=['C12'])
                self.dve(lambda e, d=d, fib=fib: e.tensor_tensor(tmpv, clv[:, d, 0], fib, op=ALU.mult), reads=['T6', SM, 'C12'], writes=['T3_0'])
                self.dve(lambda e, d=d, frb=frb: e.tensor_tensor(tmp2, clv[:, d, 1], frb, op=ALU.mult), reads=['T6', SM], writes=['T3_0'])
                self.dve(lambda e, d=d: e.scalar_tensor_tensor(c12v[:, d, 1], tmpv, -1.0, tmp2, op0=ALU.mult, op1=ALU.subtract), reads=['T3_0'], writes=['C12'])
            blv = Bl[:].rearrange("p (d r j c) -> p d r j c", d=2, r=2, j=2)
            for hf in range(2):
                hs_ = slice(hf * 64, (hf + 1) * 64)
                accb = [6, 7][:ntb]
                for j in range(2):
                    stl = hf * 2 + j
                    st = ci * 4 + stl
                    calls = [[], []]
                    for d in range(2):
                        col = d * 16 + st
                        for si, (so, L) in enumerate(seqs):
                            nblk = max(1, L // 512)
                            Lb = L // nblk
                            order = list(range(nblk)) if d == 0 else list(range(nblk - 1, -1, -1))
                            for bi_, bk_ in enumerate(order):
                                calls[d].append((ci, hf, hs_, j, stl, st, d, col, si, so, L, nblk, Lb, bi_, bk_))
                    for c0_, c1_ in zip(calls[0], calls[1]):
                        gens = [unit(*c0_), unit(*c1_)]
                        while gens:
                            for g_ in list(gens):
                                try:
                                    next(g_)
                                except StopIteration:
                                    gens.remove(g_)
                    for tb in range(ntb):
                        b = accb[tb]
                        ts = slice(tb * 512, (tb + 1) * 512)
                        for d in range(2):
                            for r_ in range(2):
                                first = (j == 0 and d == 0 and r_ == 0)
                                last = (j == 1 and d == 1 and r_ == 1)
                                self.pe(lambda e, b=b, d=d, r_=r_, ts=ts, first=first, last=last: e.matmul(
                                    self.ps[b][0:64, :], c12v[:, d, r_, stl, :], hri[:, d * 2 + r_, ts], start=first, stop=last),
                                    reads=['C12', 'hri_0', 'hri_1'], excl=['ps%d' % b])
                for tb in range(ntb):
                    b = accb[tb]
                    ts = slice(tb * 512, (tb + 1) * 512)
                    self.dve(lambda e, b=b, ts=ts: e.tensor_copy(ys[hs_, ts], self.ps[b][0:64, :]), writes=['T6'], excl=['ps%d' % b])
            self.dve(lambda e: e.scalar_tensor_tensor(ys[:, 0:NT], uf[:, 0:NT], ppc(self, l, 's5d', ci), ys[:, 0:NT], op0=ALU.mult, op1=ALU.add),
                     reads=['T6', 'T7', 'pp'], writes=['T6'])
            _gelu_from(self, ys, 'T6', NT, uf, 'T7', ys, 'T6')
            self.act(lambda e, ci=ci: e.activation(self.yb[:, 4 + ci, 0:NT], ys[:, 0:NT], AF.Copy), reads=['T6'], writes=['yb'])
        self.bank_n = 8
        if not lat and self.cfg.get('skip_fin') != 1:
            for si in range(len(seqs)):
                gr_, gi_ = fin[:, si, 0, :], fin[:, si, 1, :]
                V(lambda e: e.tensor_tensor(t0, gr_, cL, op=ALU.mult), ['fin'])
                V(lambda e: e.tensor_tensor(t2, gi_, sL, op=ALU.mult), ['fin'])
                V(lambda e: e.tensor_tensor(t0, t0, t2, op=ALU.subtract))
                V(lambda e: e.tensor_tensor(t2, gi_, cL, op=ALU.mult), ['fin'])
                V(lambda e: e.tensor_tensor(t3, gr_, sL, op=ALU.mult), ['fin'])
                V(lambda e: e.tensor_tensor(t2, t2, t3, op=ALU.add))
                V(lambda e: e.tensor_tensor(h0r, fr, t0, op=ALU.mult))
                V(lambda e: e.tensor_tensor(t3, fi, t2, op=ALU.mult))
                V(lambda e: e.tensor_tensor(h0r, h0r, t3, op=ALU.subtract))
                V(lambda e: e.tensor_tensor(h0i, fr, t2, op=ALU.mult))
                V(lambda e: e.tensor_tensor(t3, fi, t0, op=ALU.mult))
                V(lambda e: e.tensor_tensor(h0i, h0i, t3, op=ALU.add))
                for r_, src in ((0, h0r), (1, h0i)):
                    self.dve(lambda e, r_=r_, src=src, si=si: e.tensor_copy(self.s5o[:, si, l, r_ * 32:(r_ + 1) * 32], src), reads=[SM], writes=['s5o'])
                self.S.barrier()
    self.S.barrier()
    for c in reversed(cms):
        c.__exit__(None, None, None)
    with self.sbc("sgl", [128, 4, 1024], BF16) as sgl:
        wgl, wglk = self.wload(self.d_glu[l], 4, 512)
        for co in range(4):
            for tb in range(ntb):
                b = self.bank()
                ts = slice(tb * 512, (tb + 1) * 512)
                for k in range(4):
                    self.pe(lambda e, b=b, k=k, ts=ts: e.matmul(self.ps[b][:], wgl[:, k, co * 128:(co + 1) * 128], self.yb[:, 4 + k, ts], start=(k == 0), stop=(k == 3)),
                            reads=[wglk, 'yb'], excl=['ps%d' % b])
                self.act(lambda e, b=b, ts=ts: e.activation(sgl[:, co, ts], self.ps[b][:], AF.Sigmoid, bias=ppc(self, l, 's5gb', co)),
                         reads=['pp'], writes=['sgl'], excl=['ps%d' % b])
        self.S.barrier()
        for co in range(4):
            self.dve(lambda e, co=co: e.tensor_tensor(self.yb[:, 4 + co, 0:NT], self.yb[:, 4 + co, 0:NT], sgl[:, co, 0:NT], op=ALU.mult),
                     reads=['yb', 'sgl'], writes=['yb'])
    self.S.barrier()


def _mix_C(self, grp, l):
    NT, seqs, lat = grp['NT'], grp['seqs'], grp['lat']
    ntb = NT // 512
    ntt = NT // 128
    specs = [("qk", [128, 6, 1024], BF16), ("vP", [128, 8, 4, 128], BF16), ("vcP", [128, 2, 4, 128], BF16), ("kcT", [128, 2, 256], BF16),
             ("es", [128, 8], F32), ("msk", [128, 2, 128], BF16)]
    cms = [self.sbc(n, sh, dt) for (n, sh, dt) in specs]
    qk, vP, vcP, kcT, es, msk = [c.__enter__() for c in cms]
    self.act(lambda e: e.activation(es[:], ppc(self, l, 'sink', 0, 8), AF.Exp), reads=['pp'], writes=['es'])
    self.S.dma('pool', msk[:], self.d_amask, writes=['msk'])
    self.pool(lambda e: e.memset(vP[:], 0.0), writes=['vP'])
    if lat:
        self.pool(lambda e: e.memset(vcP[:], 0.0), writes=['vcP'])
        self.S.dma('pool', kcT[:], self.d_kcT[l], writes=['kcT'])
    specs1 = [("ropc", [128, 1024], F32), ("rops", [128, 1024], F32), ("qf", [128, 1024], F32), ("rt", [128, 1024], F32), ("kvf0", [128, 256], F32), ("kvf1", [128, 256], F32)]
    cms1 = [self.sbc(n, sh, dt) for (n, sh, dt) in specs1]
    ropc, rops, qf, rt, kvf0, kvf1 = [c.__enter__() for c in cms1]
    kvf = [kvf0, kvf1]
    if lat:
        self.ld(ropc[:], self.d_rope[0], writes=['ropc'])
        self.ld(rops[:], self.d_rope[1], writes=['rops'])
        with self.sbc("vcl", [128, 2, 128], BF16) as vcl:
            self.S.dma('pool', vcl[:], self.d_vc[l].rearrange("(t p) c -> p t c", p=128), writes=['vcl'])
            for t in range(2):
                for g in range(2):
                    for pos in range(2):
                        self.dve(lambda e, t=t, g=g, pos=pos: e.tensor_copy(vcP[:, t, g * 2 + pos, pos * 64:(pos + 1) * 64], vcl[:, t, g * 64:(g + 1) * 64]),
                                 reads=['vcl'], writes=['vcP'])
            self.S.barrier()
    wkv, wkvk = _proj(self, l, OFF['ck'], 256)
    for tt in range(ntt):
        b = self.bank()
        for kt in range(16):
            self.pe(lambda e, b=b, kt=kt, tt=tt: e.matmul(self.ps[b][:, 0:256], self.h[:, kt, tt * 128:(tt + 1) * 128], wkv[:, kt, :], start=(kt == 0), stop=(kt == 15)),
                    reads=[wkvk, 'h'], excl=['ps%d' % b])
        for g in range(2):
            for pos in range(2):
                self.dve(lambda e, b=b, tt=tt, g=g, pos=pos: e.tensor_copy(vP[:, tt, g * 2 + pos, pos * 64:(pos + 1) * 64], self.ps[b][:, 128 + g * 64:128 + (g + 1) * 64]),
                         writes=['vP'], excl=['ps%d' % b])
        if not lat:
            kf = kvf[tt % 2]
            kk_ = 'kvf%d' % (tt % 2)
            self.act(lambda e, b=b, kf=kf: e.activation(kf[:], self.ps[b][:, 0:256], AF.Copy), writes=[kk_], excl=['ps%d' % b])
            si = (tt * 128) // 256
            t0 = (tt * 128) % 256
            self.st(self.o_k[si, l, t0:t0 + 128, :], kf[:, 0:128], reads=[kk_])
            self.st(self.o_v[si, l, t0:t0 + 128, :], kf[:, 128:256], reads=[kk_])
    for ti_ in range(6):
        if ti_ < 4:
            wq, wqk = _proj(self, l, OFF['cq'] + ti_ * 128, 128)
        else:
            wq, wqk = self.wload(self.d_wkdup[l, ti_ - 4], 16, 128)
        bq = _proj_tile(self, wq, wqk, 0, NT)
        if not lat:
            _evac(self, 'act', qk[:, ti_, :], 'qk', bq)
        else:
            _evac(self, 'act', qf, 'qf', bq)
            for tb in range(ntb):
                ts = slice(tb * 512, (tb + 1) * 512)
                b = self.bank()
                self.pe(lambda e, b=b, ts=ts: e.matmul(self.ps[b][:], self.cf[:, 2, :], qf[:, ts], start=True, stop=True), reads=['cf', 'qf'], excl=['ps%d' % b])
                self.dve(lambda e, b=b, ts=ts: e.tensor_tensor(rt[:, ts], self.ps[b][:], rops[:, ts], op=ALU.mult), reads=['rops'], writes=['rt'], excl=['ps%d' % b])
            self.pool(lambda e: e.tensor_tensor(qf[:, 0:NT], qf[:, 0:NT], ropc[:, 0:NT], op=ALU.mult), reads=['qf', 'ropc'], writes=['qf'])
            self.dve(lambda e, ti_=ti_: e.tensor_tensor(qk[:, ti_, 0:NT], qf[:, 0:NT], rt[:, 0:NT], op=ALU.add), reads=['qf', 'rt'], writes=['qk'])
    self.S.barrier()
    for c in reversed(cms1):
        c.__exit__(None, None, None)
    if self.cfg.get('dump_qk'):
        self.dump('qk', qk[:, :, 0:NT], 'qk', [128, 6, NT])
    specs2 = [("pt0", [128, 512], BF16), ("pt1", [128, 512], BF16), ("pt2", [128, 512], BF16), ("rden", [128, 512], F32), ("oneP", [128, 2, 128], BF16)]
    cms2 = [self.sbc(n, sh, dt) for (n, sh, dt) in specs2]
    pt0, pt1, pt2, rden, oneP = [c.__enter__() for c in cms2]
    pts = [pt0, pt1, pt2]
    self.pool(lambda e: e.memset(oneP[:], 0.0), writes=['oneP'])
    self.pool(lambda e: e.memset(oneP[:, 0, 0:64], 1.0), writes=['oneP'])
    self.pool(lambda e: e.memset(oneP[:, 1, 64:128], 1.0), writes=['oneP'])
    self.bank_n = 6
    self.bank_rr = 0
    pt_rr = [0]

    def head_chunk(h, so, L, c0, ncol):
        g = h // 4
        hp = h % 2
        hs_ = slice(hp * 64, (hp + 1) * 64)
        qt = h // 2
        cols = slice(so + c0, so + c0 + ncol)
        qb0 = c0 // 128
        nqb = ncol // 128
        keys = []
        if lat:
            keys += [('ctx', 0), ('ctx', 1)]
            for i in range(max(0, qb0 - 1), min(L // 128 - 1, qb0 + nqb) + 1):
                keys.append(('lat', i))
        else:
            for i in range(L // 128):
                keys.append(('own', i))
        for ki, (kind, i) in enumerate(keys):
            b = self.bank()
            if kind == 'ctx':
                lhs = kcT[hs_, g, i * 128:(i + 1) * 128]
                lk = 'kcT'
                vl = vcP[:, i, g * 2 + hp, :]
                vk = 'vcP'
            else:
                lhs = qk[hs_, 4 + g, so + i * 128:so + (i + 1) * 128]
                lk = 'qk'
                vl = vP[:, (so // 128) + i, g * 2 + hp, :]
                vk = 'vP'
            if kind == 'lat':
                jlo = max(0, i - 1 - qb0)
                jhi = min(nqb - 1, i + 1 - qb0)
            else:
                jlo, jhi = 0, nqb - 1
            cl, ch = jlo * 128, (jhi + 1) * 128
            qcols = slice(so + c0 + cl, so + c0 + ch)
            self.pe(lambda e, b=b, lhs=lhs: e.matmul(self.ps[b][:, cl:ch], lhs, qk[hs_, qt, qcols], start=True, stop=True), reads=[lk, 'qk'], excl=['ps%d' % b])
            pi = pt_rr[0] % 3
            pt_rr[0] += 1
            pt = pts[pi]
            pk = 'pt%d' % pi
            self.act(lambda e, b=b, pt=pt: e.activation(pt[:, cl:ch], self.ps[b][:, cl:ch], AF.Exp, scale=0.125), writes=[pk], excl=['ps%d' % b])
            if kind == 'lat':
                for jq in range(jlo, jhi + 1):
                    qb = qb0 + jq
                    blk = pt[:, jq * 128:(jq + 1) * 128]
                    if qb == i + 1:
                        self.dve(lambda e, blk=blk: e.tensor_tensor(blk, blk, msk[:, 0, :], op=ALU.mult), reads=[pk, 'msk'], writes=[pk])
                    elif qb == i - 1:
                        self.dve(lambda e, blk=blk: e.tensor_tensor(blk, blk, msk[:, 1, :], op=ALU.mult), reads=[pk, 'msk'], writes=[pk])
            first = (ki == 0)
            last = (ki == len(keys) - 1)
            self.pe(lambda e, vl=vl, pt=pt, first=first, last=last: e.matmul(self.ps[6][:, cl:ch], vl, pt[:, cl:ch], start=first, stop=last, skip_group_check=True),
                    reads=[vk, pk], excl=['ps6'])
            self.pe(lambda e, pt=pt, first=first, last=last: e.matmul(self.ps[7][:, cl:ch], oneP[:, hp, :], pt[:, cl:ch], start=first, stop=last, skip_group_check=True),
                    reads=['oneP', pk], excl=['ps7'])
        self.dve(lambda e: e.tensor_scalar(rden[hs_, 0:ncol], self.ps[7][hs_, 0:ncol], es[hs_, h:h + 1], None, op0=ALU.add), reads=['es'], writes=['rden'], excl=['ps7'])
        self.dve(lambda e: e.reciprocal(rden[hs_, 0:ncol], rden[hs_, 0:ncol]), reads=['rden'], writes=['rden'])
        self.dve(lambda e: e.tensor_tensor(self.yb[hs_, 8 + qt, cols], self.ps[6][hs_, 0:ncol], rden[hs_, 0:ncol], op=ALU.mult), reads=['rden'], writes=['yb'], excl=['ps6'])

    for (so, L) in seqs:
        csz = min(512, L)
        for c0 in range(0, L, csz):
            for h in range(8):
                head_chunk(h, so, L, c0, csz)
    self.bank_n = 8
    self.S.barrier()
    for c in reversed(cms2):
        c.__exit__(None, None, None)
    for c in reversed(cms):
        c.__exit__(None, None, None)


def _mix_D(self, grp, l):
    NT, seqs, lat = grp['NT'], grp['seqs'], grp['lat']
    ntb = NT // 512
    nch_all = NT // 128
    X = [self.x[:, k, :] for k in range(16)]
    XK = ['X%d' % k for k in range(16)]
    bonus, vtokf, yacc = X[0], X[1], X[2]
    vtok = vtokf.rearrange("p (c n) -> p c n", n=128)
    til = [[X[3 + 4 * d + q] for q in range(4)] for d in range(2)]
    tilk = [[XK[3 + 4 * d + q] for q in range(4)] for d in range(2)]
    r_, kh_, vh_, kap, tmpx = X[11], X[12], X[13], X[14], X[15]
    specs = [("av", [128, 1024], F32), ("sg_", [128, 1024], F32), ("css", [128, 1024], F32),
             ("Nm", [128, 4, 128], F32), ("NTm", [128, 4, 128], F32), ("TTm", [128, 4, 128], F32), ("TTb", [128, 4, 128], BF16), ("BTm", [128, 4, 128], F32),
             ("M1m", [128, 4, 128], F32), ("M2m", [128, 4, 128], F32), ("tokm", [128, 4, 128], F32),
             ("negR", [128, 256], F32), ("Um", [128, 256], F32), ("Sst", [128, 2, 2, 128], F32), ("Sdec", [128, 2, 128], F32),
             ("mk", [128, 4, 512], BF16), ("twl", [128, 1024], BF16), ("sgl_", [128, 1024], BF16), ("lamC", [128, 2, 8], F32),
             ("lw", [128, 5, 128], BF16), ("sm_", [128, 16], F32), ("onesf", [128, 128], F32), ("sfin", [128, 128], F32)]
    cms = [self.sbc(n, sh, dt) for (n, sh, dt) in specs]
    (av, sg_, css, Nm, NTm, TTm, TTb, BTm, M1m, M2m, tokm, negR, Um, Sst, Sdec, mk, twl, sgl_, lamC, lw, sm_, onesf, sfin) = [c.__enter__() for c in cms]
    self.S.dma('pool', mk[:], self.d_dmask, writes=['mk'])
    self.pool(lambda e: e.memset(onesf[:], 1.0), writes=['onesf'])
    mu = ppc(self, l, 'mu', 0, 12)
    self.dve(lambda e: e.tensor_scalar(sm_[:, 0:12], mu, -1.0, 1.0, op0=ALU.mult, op1=ALU.add), reads=['pp'], writes=['sm_'])
    self.dve(lambda e: e.tensor_scalar(sm_[:, 12:16], ppc(self, l, 'ka', 0, 4), -1.0, 1.0, op0=ALU.mult, op1=ALU.add), reads=['pp'], writes=['sm_'])
    wl_, wlk = _proj(self, l, OFF['dwl'], 256)
    b0 = _proj_tile(self, wl_, wlk, 0, NT)
    for tb, b in enumerate(b0):
        ts = slice(tb * 512, (tb + 1) * 512)
        self.act(lambda e, b=b, ts=ts: e.activation(twl[0:64, ts], self.ps[b][0:64, :], AF.Tanh), writes=['twl'], excl=['ps%d' % b])
        self.act(lambda e, b=b, ts=ts: e.activation(twl[64:128, ts], self.ps[b][64:128, :], AF.Copy), writes=['twl'], excl=['ps%d' % b])
    b1 = _proj_tile(self, wl_, wlk, 1, NT)
    _evac(self, 'act', sgl_, 'sgl_', b1, AF.Sigmoid)
    C_W = -0.6065306597126334

    def stopat(k):
        if self.cfg.get('dstop') == k:
            raise _Stop()

    def tiles():
      for i in range(4):
            for (nm, dst, dk_, mi) in (('dr', r_, 'X11', 0), ('dk', kh_, 'X12', 1), ('dv', vh_, 'X13', 2)):
                w_, wk_ = _proj(self, l, OFF[nm] + i * 128, 128)
                bb = _proj_tile(self, w_, wk_, 0, NT)
                _evac(self, 'act', tmpx, 'X15', bb)
                omm = sm_[:, mi * 4 + i:mi * 4 + i + 1]
                mu_c = self.pp[:, l, PPC['mu'][0] + mi * 4 + i:PPC['mu'][0] + mi * 4 + i + 1]
                self.dve(lambda e, dst=dst, omm=omm: e.tensor_scalar(dst[:, 0:NT], tmpx[:, 0:NT], omm, None, op0=ALU.mult), reads=['X15', 'sm_'], writes=[dk_])
                self.dve(lambda e, mu_c=mu_c: e.tensor_scalar(tmpx[:, 0:NT], tmpx[:, 0:NT], mu_c, 0.5, op0=ALU.mult, op1=ALU.mult), reads=['X15', 'pp', dk_], writes=['X15'])
                for (o, L) in seqs:
                    self.dve(lambda e, dst=dst, o=o, L=L: e.tensor_tensor(dst[:, o + 1:o + L], dst[:, o + 1:o + L], tmpx[:, o:o + L - 1], op=ALU.add), reads=['X15', dk_], writes=[dk_])
                    self.dve(lambda e, dst=dst, o=o, L=L: e.tensor_tensor(dst[:, o:o + L - 1], dst[:, o:o + L - 1], tmpx[:, o + 1:o + L], op=ALU.add), reads=['X15', dk_], writes=[dk_])
            self.dve(lambda e: e.tensor_scalar(kap[:, 0:NT], kh_[:, 0:NT], ppc(self, l, 'kk', i), None, op0=ALU.mult), reads=['X12', 'pp'], writes=['X14'])
            self.pool(lambda e: e.tensor_tensor(tmpx[:, 0:NT], kap[:, 0:NT], kap[:, 0:NT], op=ALU.mult), reads=['X14'], writes=['X15'])
            for tb in range(ntb):
                ts = slice(tb * 512, (tb + 1) * 512)
                b = self.bank()
                self.pe(lambda e, b=b, ts=ts: e.matmul(self.ps[b][:], self.bones, tmpx[:, ts], start=True, stop=True), reads=['cf', 'X15'], excl=['ps%d' % b])
                self.act(lambda e, b=b, ts=ts: e.activation(css[:, ts], self.ps[b][:], AF.Sqrt, bias=self.cst[:, 3:4]), reads=['cst'], writes=['css'], excl=['ps%d' % b])
            self.dve(lambda e: e.reciprocal(css[:, 0:NT], css[:, 0:NT]), reads=['css'], writes=['css'])
            self.dve(lambda e: e.tensor_tensor(kap[:, 0:NT], kap[:, 0:NT], css[:, 0:NT], op=ALU.mult), reads=['css', 'X14'], writes=['X14'])
            self.dve(lambda e: e.scalar_tensor_tensor(tmpx[:, 0:NT], r_[:, 0:NT], ppc(self, l, 'rk', i), kh_[:, 0:NT], op0=ALU.mult, op1=ALU.mult),
                     reads=['X11', 'X12', 'pp', 'X15'], writes=['X15'])
            for tb in range(ntb):
                ts = slice(tb * 512, (tb + 1) * 512)
                b = self.bank()
                self.pe(lambda e, b=b, ts=ts: e.matmul(self.ps[b][:], self.bones, tmpx[:, ts], start=True, stop=True), reads=['cf', 'X15'], excl=['ps%d' % b])
                self.dve(lambda e, b=b, ts=ts: e.tensor_tensor(bonus[:, ts], self.ps[b][:], vh_[:, ts], op=ALU.mult), reads=['X13'], writes=['X0'], excl=['ps%d' % b])
            for c4 in range(0, nch_all, 4):
                b = self.bank()
                n4 = min(4, nch_all - c4)
                for q in range(n4):
                    c = c4 + q
                    self.pe(lambda e, b=b, q=q, c=c: e.transpose(self.ps[b][:, q * 128:(q + 1) * 128], vh_[:, c * 128:(c + 1) * 128], self.ident), reads=['X13', 'cf'], excl=['ps%d' % b])
                self.act(lambda e, b=b, c4=c4, n4=n4: e.activation(vtokf[:, c4 * 128:(c4 + n4) * 128], self.ps[b][:, 0:n4 * 128], AF.Copy), writes=['X1'], excl=['ps%d' % b])
            self.pool(lambda e: e.memset(yacc[:, 0:NT], 0.0), writes=['X2'])
            if self.cfg.get('dstop') == 1:
                break
            self.S.dma('pool', lw[:], self.d_lora[l, i], writes=['lw'])
            for d in range(2):
                kt_, rt_, bh_, kh2_ = til[d]
                ktk, rtk, bhk, khk = tilk[d]
                for tb in range(ntb):
                    ts = slice(tb * 512, (tb + 1) * 512)
                    b = self.bank()
                    self.pe(lambda e, b=b, ts=ts, d=d: e.matmul(self.ps[b][:], lw[0:64, d, :], twl[0:64, ts], start=True, stop=True), reads=['lw', 'twl'], excl=['ps%d' % b])
                    self.act(lambda e, b=b, ts=ts, d=d: e.activation(sg_[:, ts], self.ps[b][:], AF.Sigmoid, bias=ppc(self, l, 'w0', d * 4 + i)),
                             reads=['pp'], writes=['sg_'], excl=['ps%d' % b])
                    b = self.bank()
                    self.pe(lambda e, b=b, ts=ts, d=d: e.matmul(self.ps[b][:], lw[64:128, 2 + d, :], twl[64:128, ts], start=True, stop=True), reads=['lw', 'twl'], excl=['ps%d' % b])
                    self.act(lambda e, b=b, ts=ts, d=d: e.activation(av[:, ts], self.ps[b][:], AF.Sigmoid, bias=ppc(self, l, 'a0', d * 4 + i)),
                             reads=['pp'], writes=['av'], excl=['ps%d' % b])
                for c in range(nch_all):
                    cs_ = slice(c * 128, (c + 1) * 128)
                    if d == 0:
                        self.dve(lambda e, cs_=cs_: e.tensor_tensor_scan(css[:, cs_], onesf[:], sg_[:, cs_], 0.0, op0=ALU.mult, op1=ALU.add), reads=['sg_', 'onesf'], writes=['css'])
                    else:
                        self.dve(lambda e, cs_=cs_: e.tensor_tensor_scan(css[:, cs_][:, ::-1], onesf[:], sg_[:, cs_][:, ::-1], 0.0, op0=ALU.mult, op1=ALU.add), reads=['sg_', 'onesf'], writes=['css'])
                self.act(lambda e: e.activation(tmpx[:, 0:NT], css[:, 0:NT], AF.Exp, scale=C_W), reads=['css'], writes=['X15'])
                ec = 127 if d == 0 else 0
                self.dve(lambda e, d=d, ec=ec: e.tensor_copy(lamC[:, d, 0:nch_all], tmpx[:, 0:NT].rearrange("p (c n) -> p c n", n=128)[:, :, ec]), reads=['X15'], writes=['lamC'])
                self.dve(lambda e: e.tensor_tensor(rt_[:, 0:NT], r_[:, 0:NT], tmpx[:, 0:NT], op=ALU.mult), reads=['X11', 'X15'], writes=[rtk])
                self.pool(lambda e: e.tensor_tensor(sg_[:, 0:NT], css[:, 0:NT], sg_[:, 0:NT], op=ALU.subtract), reads=['css', 'sg_'], writes=['sg_'])
                self.act(lambda e: e.activation(sg_[:, 0:NT], sg_[:, 0:NT], AF.Exp, scale=C_W), reads=['sg_'], writes=['sg_'])
                self.dve(lambda e: e.tensor_tensor(kt_[:, 0:NT], kap[:, 0:NT], sg_[:, 0:NT], op=ALU.mult), reads=['X14', 'sg_'], writes=[ktk])
                self.act(lambda e: e.activation(css[:, 0:NT], css[:, 0:NT], AF.Exp, scale=-C_W), reads=['css'], writes=['css'])
                self.dve(lambda e: e.tensor_tensor(bh_[:, 0:NT], av[:, 0:NT], kap[:, 0:NT], op=ALU.mult), reads=['av', 'X14'], writes=[bhk])
                self.pool(lambda e: e.tensor_tensor(bh_[:, 0:NT], bh_[:, 0:NT], css[:, 0:NT], op=ALU.mult), reads=[bhk, 'css'], writes=[bhk])
                self.dve(lambda e: e.tensor_scalar(av[:, 0:NT], av[:, 0:NT], ppc(self, l, 'ka', i), sm_[:, 12 + i:13 + i], op0=ALU.mult, op1=ALU.add), reads=['av', 'pp', 'sm_', bhk], writes=['av'])
                self.dve(lambda e: e.tensor_tensor(av[:, 0:NT], av[:, 0:NT], kh_[:, 0:NT], op=ALU.mult), reads=['av', 'X12'], writes=['av'])
                self.dve(lambda e: e.tensor_tensor(kh2_[:, 0:NT], av[:, 0:NT], css[:, 0:NT], op=ALU.mult), reads=['av', 'css'], writes=[khk])
            if self.cfg.get('dstop') == 2:
                break
            for si, (so, L) in enumerate(seqs):
                nch = L // 128
                c_base = so // 128
                for d in range(2):
                    if lat:
                        self.ld(Sst[:, d, 0, :], self.d_latwkv[l, d, i], writes=['Sst'])
                    else:
                        self.pool(lambda e, d=d: e.memset(Sst[:, d, 0, :], 0.0), writes=['Sst'])
                for step in range(nch):
                    cur = step % 2
                    nxt = 1 - cur
                    cc = [c_base + step, c_base + nch - 1 - step]
                    csl = [slice(c * 128, (c + 1) * 128) for c in cc]
                    b = self.bank()
                    for d in range(2):
                        for q, (src, sk) in enumerate(((til[d][2], tilk[d][2]), (til[d][3], tilk[d][3]))):
                            self.pe(lambda e, b=b, d=d, q=q, src=src: e.transpose(self.ps[b][:, (d * 2 + q) * 128:(d * 2 + q + 1) * 128], src[:, csl[d]], self.ident),
                                    reads=[sk, 'cf'], excl=['ps%d' % b])
                    self.act(lambda e, b=b: e.activation(tokm[:].rearrange("p u n -> p (u n)"), self.ps[b][:], AF.Copy), writes=['tokm'], excl=['ps%d' % b])
                    stopat(31)
                    kinds = [(NTm, 'NTm', 2, 0, 0), (Nm, 'Nm', 0, 2, 1), (BTm, 'BTm', 3, 0, 0), (M1m, 'M1m', 2, 1, 2), (M2m, 'M2m', 3, 1, 2)]
                    for (dst, dkey, li, ri, mi) in kinds:
                        msel = {('NTm'): 0, ('Nm'): 1, ('BTm'): 2, ('M1m'): 3, ('M2m'): 3}[dkey]
                        for hh in range(2):
                            b = self.bank()
                            hs_ = slice(hh * 64, (hh + 1) * 64)
                            for d in range(2):
                                self.pe(lambda e, b=b, d=d, hs_=hs_, li=li, ri=ri: e.matmul(self.ps[b][:, d * 128:(d + 1) * 128], til[d][li][hs_, csl[d]], til[d][ri][hs_, csl[d]],
                                                                                         start=True, stop=True), reads=[tilk[d][li], tilk[d][ri]], excl=['ps%d' % b])
                            dv_ = dst[:].rearrange("p (d h) n -> p d h n", h=2)[:, :, hh, :]
                            mv_ = mk[:, msel, :].rearrange("p (d h n) -> p d h n", d=2, h=2)[:, :, hh, :]
                            pv_ = self.ps[b][:, 0:256].rearrange("p (d n) -> p d n", n=128)
                            self.dve(lambda e, dv_=dv_, mv_=mv_, pv_=pv_: e.tensor_tensor(dv_, pv_, mv_, op=ALU.mult), reads=['mk'], writes=[dkey], excl=['ps%d' % b])
                    stopat(32)
                    for u in range(4):
                        self.pool(lambda e, u=u: e.tensor_tensor(TTm[:, u, :], NTm[:, u, :], self.ident, op=ALU.add), reads=['NTm', 'cf'], writes=['TTm'])
                    for lev in range(1, 7):
                        bX = self.bank()
                        for u in range(4):
                            self.pe(lambda e, bX=bX, u=u: e.matmul(self.ps[bX][:, u * 128:(u + 1) * 128], NTm[:, u, :], Nm[:, u, :], start=True, stop=True), reads=['NTm', 'Nm'], excl=['ps%d' % bX])
                        if lev < 6:
                            bY = self.bank()
                            for u in range(4):
                                self.pe(lambda e, bY=bY, u=u: e.matmul(self.ps[bY][:, u * 128:(u + 1) * 128], Nm[:, u, :], NTm[:, u, :], start=True, stop=True), reads=['NTm', 'Nm'], excl=['ps%d' % bY])
                        self.act(lambda e, bX=bX: e.activation(Nm[:].rearrange("p u n -> p (u n)"), self.ps[bX][:], AF.Copy), writes=['Nm'], excl=['ps%d' % bX])
                        if lev < 6:
                            self.dve(lambda e, bY=bY: e.tensor_copy(NTm[:].rearrange("p u n -> p (u n)"), self.ps[bY][:]), writes=['NTm'], excl=['ps%d' % bY])
                        bZ = self.bank()
                        for u in range(4):
                            self.pe(lambda e, bZ=bZ, u=u: e.matmul(self.ps[bZ][:, u * 128:(u + 1) * 128], Nm[:, u, :], TTm[:, u, :], start=True, stop=True), reads=['Nm', 'TTm'], excl=['ps%d' % bZ])
                        self.dve(lambda e, bZ=bZ: e.tensor_tensor(TTm[:].rearrange("p u n -> p (u n)"), TTm[:].rearrange("p u n -> p (u n)"), self.ps[bZ][:], op=ALU.add),
                                 reads=['TTm'], writes=['TTm'], excl=['ps%d' % bZ])
                        if lev < 6:
                            stopat(33)
                    bR = self.bank()
                    for d in range(2):
                        for hh in range(2):
                            u = d * 2 + hh
                            hs_ = slice(hh * 64, (hh + 1) * 64)
                            self.pe(lambda e, u=u, d=d, hs_=hs_: e.matmul(self.ps[bR][:, u * 64:(u + 1) * 64], til[d][0][hs_, csl[d]], Sst[hs_, d, cur, hs_], start=True, stop=False),
                                    reads=[tilk[d][0], 'Sst'], excl=['ps%d' % bR])
                            self.pe(lambda e, u=u, d=d, hs_=hs_: e.matmul(self.ps[bR][:, u * 64:(u + 1) * 64], BTm[:, u, :], vtok[:, cc[d], hs_], start=False, stop=True),
                                    reads=['BTm', 'X1'], excl=['ps%d' % bR])
                    self.act(lambda e: e.activation(negR[:], self.ps[bR][:, 0:256], AF.Copy, scale=-1.0), writes=['negR'], excl=['ps%d' % bR])
                    bU = self.bank()
                    for u in range(4):
                        self.pe(lambda e, u=u: e.matmul(self.ps[bU][:, u * 64:(u + 1) * 64], TTm[:, u, :], negR[:, u * 64:(u + 1) * 64], start=True, stop=True), reads=['TTm', 'negR'], excl=['ps%d' % bU])
                    self.act(lambda e: e.activation(Um[:], self.ps[bU][:, 0:256], AF.Copy), writes=['Um'], excl=['ps%d' % bU])
                    stopat(34)
                    bY2 = self.bank()
                    for d in range(2):
                        for hh in range(2):
                            u = d * 2 + hh
                            hs_ = slice(hh * 64, (hh + 1) * 64)
                            self.pe(lambda e, u=u, d=d, hs_=hs_: e.matmul(self.ps[bY2][:, u * 128:(u + 1) * 128], Sst[hs_, d, cur, :], til[d][1][hs_, csl[d]], start=True, stop=False),
                                    reads=['Sst', tilk[d][1]], excl=['ps%d' % bY2])
                            self.pe(lambda e, u=u, d=d: e.matmul(self.ps[bY2][:, u * 128:(u + 1) * 128], Um[:, d * 128:(d + 1) * 128], M1m[:, u, :], start=False, stop=False),
                                    reads=['Um', 'M1m'], excl=['ps%d' % bY2])
                            self.pe(lambda e, u=u, d=d: e.matmul(self.ps[bY2][:, u * 128:(u + 1) * 128], vtok[:, cc[d], :], M2m[:, u, :], start=False, stop=True),
                                    reads=['X1', 'M2m'], excl=['ps%d' % bY2])
                    for d in range(2):
                        for hh in range(2):
                            u = d * 2 + hh
                            hs_ = slice(hh * 64, (hh + 1) * 64)
                            self.dve(lambda e, u=u, d=d, hs_=hs_: e.tensor_tensor(yacc[hs_, csl[d]], yacc[hs_, csl[d]], self.ps[bY2][hs_, u * 128:(u + 1) * 128], op=ALU.add),
                                     reads=['X2'], writes=['X2'], excl=['ps%d' % bY2])
                    stopat(35)
                    bS = self.bank()
                    for d in range(2):
                        self.pe(lambda e, d=d: e.matmul(self.ps[bS][:, d * 128:(d + 1) * 128], tokm[:, d * 2 + 0, :], Um[:, d * 128:(d + 1) * 128], start=True, stop=False),
                                reads=['tokm', 'Um'], excl=['ps%d' % bS])
                        self.pe(lambda e, d=d: e.matmul(self.ps[bS][:, d * 128:(d + 1) * 128], tokm[:, d * 2 + 1, :], vtok[:, cc[d], :], start=False, stop=True),
                                reads=['tokm', 'X1'], excl=['ps%d' % bS])
                    for d in range(2):
                        lam = lamC[:, d, cc[d]:cc[d] + 1]
                        self.dve(lambda e, d=d, lam=lam: e.tensor_scalar(Sdec[:, d, :], Sst[:, d, cur, :], lam, None, op0=ALU.mult), reads=['Sst', 'lamC'], writes=['Sdec'])
                        self.dve(lambda e, d=d, lam=lam: e.scalar_tensor_tensor(Sst[:, d, nxt, :], self.ps[bS][:, d * 128:(d + 1) * 128], lam, Sdec[:, d, :], op0=ALU.mult, op1=ALU.add),
                                 reads=['Sdec', 'lamC'], writes=['Sst'], excl=['ps%d' % bS])
                if not lat:
                    fin_i = nch % 2
                    for d in range(2):
                        b = self.bank()
                        self.pe(lambda e, b=b, d=d: e.transpose(self.ps[b][:, 0:128], Sst[:, d, fin_i, :], self.ident), reads=['Sst', 'cf'], excl=['ps%d' % b])
                        self.act(lambda e, b=b: e.activation(sfin[:], self.ps[b][:, 0:128], AF.Copy), writes=['sfin'], excl=['ps%d' % b])
                        for hh in range(2):
                            hs_ = slice(hh * 64, (hh + 1) * 64)
                            self.st(self.o_wkv[si, l, d, 2 * i + hh], sfin[hs_, hs_], reads=['sfin'])
            if self.cfg.get('dstop') == 3:
                break
            for tb in range(ntb):
                ts = slice(tb * 512, (tb + 1) * 512)
                b = self.bank()
                self.pe(lambda e, b=b, ts=ts: e.matmul(self.ps[b][:], self.bones, yacc[:, ts], start=True, stop=True), reads=['cf', 'X2'], excl=['ps%d' % b])
                self.dve(lambda e, b=b, ts=ts: e.scalar_tensor_tensor(yacc[:, ts], self.ps[b][:], -1.0 / 64.0, yacc[:, ts], op0=ALU.mult, op1=ALU.add), reads=['X2'], writes=['X2'], excl=['ps%d' % b])
                self.act(lambda e, ts=ts: e.activation(tmpx[:, ts], yacc[:, ts], AF.Square), reads=['X2'], writes=['X15'])
                b = self.bank()
                self.pe(lambda e, b=b, ts=ts: e.matmul(self.ps[b][:], self.bones, tmpx[:, ts], start=True, stop=True), reads=['cf', 'X15'], excl=['ps%d' % b])
                self.act(lambda e, b=b, ts=ts: e.activation(css[:, ts], self.ps[b][:], AF.Sqrt, bias=self.cst[:, 2:3], scale=1.0 / 64.0), reads=['cst'], writes=['css'], excl=['ps%d' % b])
                self.dve(lambda e, ts=ts: e.reciprocal(css[:, ts], css[:, ts]), reads=['css'], writes=['css'])
                self.dve(lambda e, ts=ts: e.tensor_tensor(yacc[:, ts], yacc[:, ts], css[:, ts], op=ALU.mult), reads=['X2', 'css'], writes=['X2'])
                self.act(lambda e, ts=ts: e.activation(yacc[:, ts], yacc[:, ts], AF.Identity, bias=ppc(self, l, 'lnb', i), scale=ppc(self, l, 'lnw', i)), reads=['X2', 'pp'], writes=['X2'])
                self.dve(lambda e, ts=ts: e.tensor_tensor(yacc[:, ts], yacc[:, ts], bonus[:, ts], op=ALU.add), reads=['X2', 'X0'], writes=['X2'])
                b = self.bank()
                self.pe(lambda e, b=b, ts=ts: e.matmul(self.ps[b][:], lw[:, 4, :], sgl_[:, ts], start=True, stop=True), reads=['lw', 'sgl_'], excl=['ps%d' % b])
                self.dve(lambda e, b=b, ts=ts: e.tensor_tensor(self.yb[:, 12 + i, ts], yacc[:, ts], self.ps[b][:], op=ALU.mult), reads=['X2'], writes=['yb'], excl=['ps%d' % b])
            self.S.barrier()

    try:
        tiles()
    except _Stop:
        pass
    self.S.barrier()
    for c in reversed(cms):
        c.__exit__(None, None, None)


def kernel(**inputs):
    inp = {k: np.asarray(v) for k, v in inputs.items()}
    ncores = 8
    b = Builder({})
    nc = _build(b)
    maps = make_in_maps(inp, {}, ncores=ncores)
    in_maps = [{k: np.ascontiguousarray(v, dtype=np.float32) for k, v in m.items() if k in b.dram_in} for m in maps]
    res = run_bass_kernel_spmd(nc, in_maps, core_ids=list(range(ncores)))
    R = res.results
    B, S_, DB, DS = 16, 256, 8, 1024
    y_prompt = np.zeros((B, S_, D), np.float32)
    y_sample = np.zeros((DB, DS, D), np.float32)
    nk = np.zeros((B, DEPTH, S_, 2, 64), np.float32)
    nv = np.zeros((B, DEPTH, S_, 2, 64), np.float32)
    nlru = np.zeros((B, DEPTH, 2, W), np.float32)
    ns5 = np.zeros((B, DEPTH, 2, 2, 32, 64), np.float32)
    nwkv = np.zeros((B, DEPTH, 2, 8, 64, 64), np.float32)
    for c in range(ncores):
        r = R[c]
        y_sample[c] = np.asarray(r['yT_s']).T
        y_prompt[2 * c:2 * c + 2] = np.asarray(r['yT_p']).T.reshape(2, S_, D)
        nk[2 * c:2 * c + 2] = np.asarray(r['o_k']).reshape(2, DEPTH, S_, 2, 64)
        nv[2 * c:2 * c + 2] = np.asarray(r['o_v']).reshape(2, DEPTH, S_, 2, 64)
        nlru[2 * c:2 * c + 2] = np.asarray(r['o_lru'])
        o = np.asarray(r['o_s5']).reshape(2, DEPTH, 2, 64, 2, 2, 16)
        ns5[2 * c:2 * c + 2] = o.transpose(0, 1, 5, 4, 6, 2, 3).reshape(2, DEPTH, 2, 2, 32, 64)
        nwkv[2 * c:2 * c + 2] = np.asarray(r['o_wkv'])
    return (y_prompt, y_sample, nk, nv, nlru, ns5, nwkv)
```

```python
import math
import numpy as np
import concourse.bass as bass
import concourse.mybir as mybir
from concourse.bass_utils import run_bass_kernel_spmd

F32 = mybir.dt.float32
BF16 = mybir.dt.bfloat16
I32 = mybir.dt.int32
ALU = mybir.AluOpType
AF = mybir.ActivationFunctionType

D = 2048
DEPTH = 2
NKT = 16
W = 512
DIN = 12288
DFF = 5504
NFT = 43
EPS = 1e-6
GN_EPS = 64e-5
C_GELU = 1.5957691216057308
OFF = dict(ax=0, ag=512, bu=1024, cq=1536, ck=2048, cv=2176, dr=2304, dk=2816, dv=3328, dwl=3840, dal=3904, dgl=3968, mg=4096)


class _Stop(Exception):
    pass


class Sync:
    def __init__(self, nc, n_dma_sems=40):
        self.nc = nc
        self.engs = {'pe': nc.tensor, 'act': nc.scalar, 'dve': nc.vector, 'pool': nc.gpsimd, 'sp': nc.sync}
        self.sems = {}
        self.cnt = {}
        for k in self.engs:
            self.sems[k] = nc.semaphore("s_" + k).__enter__()
            self.cnt[k] = 0
        self.dma_sems = []
        for i in range(n_dma_sems):
            self.dma_sems.append([nc.semaphore("d_%d" % i).__enter__(), 0])
        self.dma_rr = 0
        self.waited = {k: {} for k in self.engs}
        self.lastw = {}
        self.readers = {}
        self.n_inst = 0
        self.n_wait = 0
        self.out_events = []

    def _need(self, e, events):
        eng = self.engs[e]
        w = self.waited[e]
        best = {}
        for ev in events:
            if ev is None:
                continue
            sem, val, owner, nm = ev
            if w.get(nm, 0) >= val:
                continue
            if nm not in best or best[nm][1] < val:
                best[nm] = (sem, val)
        for nm, (sem, val) in best.items():
            eng.wait_ge(sem, val)
            w[nm] = val
            self.n_wait += 1

    def _deps(self, e, reads, writes, excl):
        evs = []
        for k in reads:
            ev = self.lastw.get(k)
            if ev is not None and not (ev[2] == e and e == 'pe'):
                evs.append(ev)
        for k in list(writes) + list(excl):
            ev = self.lastw.get(k)
            if ev is not None and ev[2] != e:
                evs.append(ev)
            for r in self.readers.get(k, ()):
                if r[2] != e:
                    evs.append(r)
        return evs

    def _record(self, ev, reads, writes, excl):
        for k in list(writes) + list(excl):
            self.lastw[k] = ev
            self.readers[k] = []
        for k in reads:
            if k in writes or k in excl:
                continue
            self.readers.setdefault(k, []).append(ev)

    def op(self, e, fn, reads=(), writes=(), excl=()):
        self._need(e, self._deps(e, reads, writes, excl))
        ins = fn(self.engs[e])
        self.cnt[e] += 1
        ins.then_inc(self.sems[e], 1)
        ev = (self.sems[e], self.cnt[e], e, 's_' + e)
        self._record(ev, reads, writes, excl)
        self.n_inst += 1
        return ins

    def dma(self, q, out, in_, reads=(), writes=(), is_out=False, **kw):
        slot = self.dma_sems[self.dma_rr]
        idx = self.dma_rr
        self.dma_rr = (self.dma_rr + 1) % len(self.dma_sems)
        sem, val = slot
        evs = self._deps('dma_issue', reads, writes, ())
        if val > 0:
            evs.append((sem, val, 'dma', 'd_%d' % idx))
        self._need(q, evs)
        ins = self.engs[q].dma_start(out=out, in_=in_, **kw)
        ins.then_inc(sem, 16)
        slot[1] = val + 16
        ev = (sem, val + 16, 'dma', 'd_%d' % idx)
        self._record(ev, reads, writes, ())
        if is_out:
            self.out_events.append(ev)
        self.n_inst += 1
        return ev

    def barrier(self):
        evs = [(self.sems[k], self.cnt[k], k, 's_' + k) for k in ('pe', 'act', 'dve', 'pool') if self.cnt[k] > 0]
        for e in ('pe', 'act', 'dve', 'pool', 'sp'):
            self._need(e, [ev for ev in evs if ev[2] != e] + self.out_events)
        self.out_events = []

    def final(self):
        self.barrier()
        evs = []
        for i, (sem, val) in enumerate(self.dma_sems):
            if val:
                evs.append((sem, val, 'dma', 'd_%d' % i))
        self._need('sp', evs)


def _bcast(ap, shape):
    return ap.to_broadcast(list(shape))


class Builder:
    def __init__(self, cfg=None):
        self.cfg = cfg or {}
        self.nc = bass.Bass("TRN2", target_bir_lowering=False)
        self.S = Sync(self.nc)
        self.dram_in = {}
        self.dram_out = {}
        self.dbg = {}
        self._uid = 0
        self.marks = []
        self.bank_rr = 0
        self.bank_n = 8

    def din(self, name, shape, dt=F32):
        t = self.nc.dram_tensor(name, list(shape), dt, kind="ExternalInput").ap()
        self.dram_in[name] = t
        return t

    def dout(self, name, shape, dt=F32):
        t = self.nc.dram_tensor(name, list(shape), dt, kind="ExternalOutput").ap()
        self.dram_out[name] = t
        return t

    def sb(self, name, shape, dt=F32):
        return self.nc.sbuf_tensor(self.uid(name + "_"), list(shape), dt).__enter__()

    def sbc(self, name, shape, dt=F32):
        return self.nc.sbuf_tensor(self.uid(name + "_"), list(shape), dt)

    def mark(self, name):
        self.marks.append((name, self.S.cnt['act']))

    def uid(self, p="k"):
        self._uid += 1
        return "%s%d" % (p, self._uid)

    def bank(self):
        b = self.bank_rr % self.bank_n
        self.bank_rr = (b + 1) % self.bank_n
        return b

    def pe(self, fn, reads=(), writes=(), excl=()):
        return self.S.op('pe', fn, reads, writes, excl)

    def act(self, fn, reads=(), writes=(), excl=()):
        return self.S.op('act', fn, reads, writes, excl)

    def dve(self, fn, reads=(), writes=(), excl=()):
        return self.S.op('dve', fn, reads, writes, excl)

    def pool(self, fn, reads=(), writes=(), excl=()):
        return self.S.op('pool', fn, reads, writes, excl)

    def ld(self, out, in_, writes, q='sp', reads=(), **kw):
        return self.S.dma(q, out, in_, reads=reads, writes=writes, **kw)

    def st(self, out, in_, reads, q='sp'):
        return self.S.dma(q, out, in_, reads=reads, writes=(), is_out=True)

    def dump(self, name, ap, key, shape):
        if name not in self.cfg.get('dump', ()):
            return
        o = self.dout("dbg_" + name, shape)
        if ap.dtype != F32:
            self.S.dma('pool', o, ap, reads=[key], is_out=True)
        else:
            self.S.dma('sp', o, ap, reads=[key], is_out=True)

    def wload(self, src, nkt, ncols):
        i = self.w_rr
        self.w_rr = (self.w_rr + 1) % len(self.wbufs)
        assert nkt * ncols <= 4096
        view = self.wbufs[i][:, 0:nkt * ncols].rearrange("p (k n) -> p k n", n=ncols)
        stg = getattr(self, 'stg', None)
        wc = getattr(self, 'wc_mode', None)
        if wc is not None:
            mode, l_ = wc
            idx = self.wc_idx[l_]
            self.wc_idx[l_] += 1
            full = self.wbufs[i][:, 0:nkt * ncols]
            if mode == 'load' and idx < self.wc_n[l_]:
                self.S.dma('sp', full, self.wcache[l_][idx, :, 0:nkt * ncols], writes=['wb%d' % i])
                return view, 'wb%d' % i
            self.S.dma('pool', view, src.rearrange("(k p) n -> p k n", p=128), writes=['wb%d' % i])
            if mode == 'store' and idx < WC_MAX:
                self.S.dma('sp', self.wcache[l_][idx, :, 0:nkt * ncols], full, reads=['wb%d' % i], is_out=True)
                self.wc_n[l_] = idx + 1
            return view, 'wb%d' % i
        if not stg:
            self.S.dma('pool', view, src.rearrange("(k p) n -> p k n", p=128), writes=['wb%d' % i])
            return view, 'wb%d' % i
        kper = max(1, 1024 // ncols)
        for k0 in range(0, nkt, kper):
            kk = min(kper, nkt - k0)
            j = self.stg_rr
            self.stg_rr = (j + 1) % len(stg)
            sv = stg[j][:, 0:kk * ncols].rearrange("p (k n) -> p k n", n=ncols)
            self.S.dma('sp', sv, src[k0 * 128:(k0 + kk) * 128, :].rearrange("(k p) n -> p k n", p=128), writes=['stg%d' % j])
            self.S.op('pool', lambda e, sv=sv, k0=k0, kk=kk: e.tensor_copy(view[:, k0:k0 + kk, :], sv), reads=['stg%d' % j], writes=['wb%d' % i])
        return view, 'wb%d' % i

    def stg_on(self, n=4):
        self._stg_cms = [self.sbc("stg%d" % j, [128, 1024], F32) for j in range(n)]
        self.stg = [c.__enter__() for c in self._stg_cms]
        self.stg_rr = 0

    def stg_off(self):
        self.S.barrier()
        for c in reversed(self._stg_cms):
            c.__exit__(None, None, None)
        self.stg = None


def _pp_layout():
    cols = {}
    o = 0
    for name, n in [('norm1', 16), ('norm2', 16), ('mod_b', 96), ('fcw', 3 * NFT), ('fcb', NFT),
                    ('lcw', 16), ('lcb', 4), ('lba', 8), ('lbx', 8), ('llam', 8),
                    ('s5d', 4), ('s5gb', 4),
                    ('mu', 12), ('w0', 8), ('a0', 8), ('kk', 4), ('ka', 4), ('rk', 4), ('lnw', 4), ('lnb', 4),
                    ('are', 32), ('aim', 32), ('ldt', 32), ('sink', 8), ('normf', 16)]:
        cols[name] = (o, n)
        o += n
    return cols, o


PPC, NPP = _pp_layout()


def _fm(v, nt):
    return np.ascontiguousarray(np.asarray(v, np.float32).reshape(nt, 128).T)


def pack_params(inp, l):
    pp = np.zeros((128, NPP), np.float32)

    def put(name, arr):
        o, n = PPC[name]
        arr = np.asarray(arr, np.float32)
        assert arr.shape == (128, n), (name, arr.shape, n)
        pp[:, o:o + n] = arr

    put('norm1', _fm(inp['norm1'][l], 16))
    put('norm2', _fm(inp['norm2'][l], 16))
    put('mod_b', _fm(inp['mod_b'][l], 96))
    put('fcw', np.concatenate([_fm(inp['ffn_conv_w'][l, j], NFT) for j in range(3)], axis=1))
    put('fcb', _fm(inp['ffn_conv_b'][l], NFT))
    put('lcw', np.concatenate([_fm(inp['lru_conv_w'][l, j], 4) for j in range(4)], axis=1))
    put('lcb', _fm(inp['lru_conv_b'][l], 4))
    put('lba', np.concatenate([_fm(inp['lru_ba'][l, d], 4) for d in range(2)], axis=1))
    put('lbx', np.concatenate([_fm(inp['lru_bx'][l, d], 4) for d in range(2)], axis=1))
    put('llam', np.concatenate([_fm(inp['lru_lam'][l, d], 4) for d in range(2)], axis=1))
    put('s5d', _fm(inp['s5_d'][l], 4))
    put('s5gb', _fm(inp['s5_glu_b'][l], 4))
    put('mu', np.concatenate([_fm(inp['rwkv_mu'][l, j], 4) for j in range(3)], axis=1))
    put('w0', np.concatenate([_fm(inp['rwkv_w0'][l, d], 4) for d in range(2)], axis=1))
    put('a0', np.concatenate([_fm(inp['rwkv_a0'][l, d], 4) for d in range(2)], axis=1))
    put('kk', _fm(inp['rwkv_k_k'][l], 4))
    put('ka', _fm(inp['rwkv_k_a'][l], 4))
    put('rk', _fm(inp['rwkv_r_k'][l].reshape(-1), 4))
    put('lnw', _fm(inp['rwkv_ln_w'][l], 4))
    put('lnb', _fm(inp['rwkv_ln_b'][l], 4))
    put('are', np.concatenate([_fm(inp['s5_a_re'][l, d].reshape(-1), 16) for d in range(2)], axis=1))
    put('aim', np.concatenate([_fm(inp['s5_a_im'][l, d].reshape(-1), 16) for d in range(2)], axis=1))
    put('ldt', np.concatenate([_fm(np.repeat(inp['s5_log_dt'][l, d], 64), 16) for d in range(2)], axis=1))
    put('sink', np.broadcast_to(np.asarray(inp['attn_sink'][l], np.float32).reshape(1, 8), (128, 8)))
    put('normf', _fm(inp['norm_final'], 16))
    return pp


WC_MAX = 200


GROUPS = [dict(name='s', NT=1024, seqs=[(0, 1024)], lat=True, gi=0),
          dict(name='p', NT=512, seqs=[(0, 256), (256, 256)], lat=False, gi=1)]


def _setup(self):
    nc = self.nc
    cfg = self.cfg
    self.w_rr = 0
    self.wbufs = [self.sb("wbuf%d" % i, [128, 4096], BF16) for i in range(3)]
    self.ps = [nc.psum_tensor("ps%d" % i, [128, 512], F32).__enter__() for i in range(8)]
    self.xT = {'s': self.din("xT_s", [D, 1024]), 'p': self.din("xT_p", [D, 512])}
    self.yT = {'s': self.dout("yT_s", [D, 1024]), 'p': self.dout("yT_p", [D, 512])}
    self.d_pp = self.din("pp", [DEPTH, 128, NPP])
    self.d_cc = self.din("cc", [128, 16, 2])
    self.d_modw = self.din("mod_w", [DEPTH, D, 6 * D])
    self.d_win = self.din("w_in", [DEPTH, D, DIN])
    self.d_bw = self.din("branch_w", [DEPTH, 4 * W, D])
    self.d_wout = self.din("w_out", [DEPTH, D, D])
    self.d_fwin = self.din("ffn_w_in", [DEPTH, D, 2 * DFF])
    self.d_fwout = self.din("ffn_w_out", [DEPTH, DFF, D])
    self.d_consts = self.din("consts", [128, 128 * 4])
    self.x = self.sb("x", [128, 16, 1024])
    self.h = self.sb("h", [128, 16, 1024], BF16)
    self.yb = self.sb("yb", [128, 16, 1024], BF16)
    self.pp = self.sb("ppk", [128, DEPTH, NPP])
    self.modv = self.sb("modv", [128, DEPTH, 2, 96])
    self.mA = self.sb("mA", [128, DEPTH, 2, 2, 16])
    self.cst = self.sb("cst", [128, 8])
    self.ones_bf = self.sb("ones_bf", [128, 128], BF16)
    self.cf = self.sb("cf", [128, 4, 128])
    self.ident = self.cf[:, 0, :]
    self.bones = self.cf[:, 1, :]
    for i, v in enumerate([EPS, 1.0, GN_EPS, 1e-12, 0.0, -1.0, 0.5, 2.0]):
        self.pool(lambda e, i=i, v=v: e.memset(self.cst[:, i:i + 1], v), writes=['cst'])
    self.pool(lambda e: e.memset(self.ones_bf[:], 1.0), writes=['ones_bf'])
    self.ld(self.cf[:], self.d_consts.rearrange("p (a b) -> p a b", b=128), writes=['cf'])
    self.ld(self.pp[:], self.d_pp.rearrange("l p n -> p l n"), writes=['pp'])


def ppc(self, l, name, j=0, n=1):
    o, cnt = PPC[name]
    return self.pp[:, l, o + j:o + j + n]


def _prologue_mod(self):
    with self.sbc("cc32", [128, 16, 2]) as cc32, self.sbc("ccb", [128, 16, 2], BF16) as ccb:
        self.ld(cc32[:], self.d_cc, writes=['cc32'])
        self.act(lambda e: e.activation(ccb[:], cc32[:], AF.Silu), reads=['cc32'], writes=['ccb'])
        layers = self.cfg.get('layers', list(range(DEPTH)))
        for l in layers:
            b = self.bank()
            bk = 'ps%d' % b
            psv = self.ps[b][:, 0:192].rearrange("p (j g) -> p j g", g=2)
            for sl in range(48):
                wv, wk = self.wload(self.d_modw[l, :, sl * 256:(sl + 1) * 256], 16, 256)
                for jj in range(2):
                    j = sl * 2 + jj
                    for kt in range(16):
                        self.pe(lambda e, j=j, kt=kt, jj=jj, wv=wv: e.matmul(psv[:, j, :], wv[:, kt, jj * 128:(jj + 1) * 128], ccb[:, kt, :],
                                                                              start=(kt == 0), stop=(kt == 15)),
                                reads=[wk, 'ccb'], excl=[bk])
            o, n = PPC['mod_b']
            for g in range(2):
                self.dve(lambda e, g=g, l=l: e.tensor_tensor(self.modv[:, l, g, :], psv[:, :, g], self.pp[:, l, o:o + 96], op=ALU.add),
                         reads=['pp'], writes=['modv'], excl=[bk])
            for g in range(2):
                for wi, (nm, so) in enumerate([('norm1', 16), ('norm2', 64)]):
                    no, _ = PPC[nm]
                    self.dve(lambda e, g=g, l=l, wi=wi, so=so, no=no: e.scalar_tensor_tensor(
                        self.mA[:, l, g, wi, :], self.modv[:, l, g, so:so + 16], 1.0, self.pp[:, l, no:no + 16], op0=ALU.add, op1=ALU.mult),
                        reads=['modv', 'pp'], writes=['mA'])


def _norm(self, grp, scale_ap_fn, shift_ap_fn, out_fn, out_key_fn, rkeys):
    NT = grp['NT']
    ntb = NT // 512
    with self.sbc("rstd", [128, 1024]) as rstd, self.sbc("sq0", [128, 512], BF16) as sq0, self.sbc("sq1", [128, 512], BF16) as sq1, \
            self.sbc("nt0", [128, 1024]) as nt0, self.sbc("nt1", [128, 1024]) as nt1:
        sq = [sq0, sq1]
        ntmp = [nt0, nt1]
        for tb in range(ntb):
            b = self.bank()
            bk = 'ps%d' % b
            ts = slice(tb * 512, (tb + 1) * 512)
            for kt in range(16):
                s_ = sq[kt % 2]
                sk = 'sq%d' % (kt % 2)
                self.act(lambda e, kt=kt, s_=s_: e.activation(s_[:], self.x[:, kt, ts], AF.Square), reads=['x'], writes=[sk])
                self.pe(lambda e, kt=kt, s_=s_: e.matmul(self.ps[b][:], self.ones_bf[:], s_[:], start=(kt == 0), stop=(kt == 15)),
                        reads=[sk, 'ones_bf'], excl=[bk])
            self.act(lambda e: e.activation(rstd[:, ts], self.ps[b][:], AF.Sqrt, bias=self.cst[:, 0:1], scale=1.0 / D),
                     reads=['cst'], writes=['rstd'], excl=[bk])
        self.dve(lambda e: e.reciprocal(rstd[:, 0:NT], rstd[:, 0:NT]), reads=['rstd'], writes=['rstd'])
        for kt in range(16):
            t_ = ntmp[kt % 2]
            tk = 'nt%d' % (kt % 2)
            self.dve(lambda e, kt=kt, t_=t_: e.tensor_tensor(t_[:, 0:NT], self.x[:, kt, 0:NT], rstd[:, 0:NT], op=ALU.mult),
                     reads=['x', 'rstd'], writes=[tk])
            self.act(lambda e, kt=kt, t_=t_: e.activation(out_fn(kt), t_[:, 0:NT], AF.Identity, bias=shift_ap_fn(kt), scale=scale_ap_fn(kt)),
                     reads=[tk] + list(rkeys), writes=[out_key_fn(kt)])
    self.S.barrier()


def _norm_mod(self, grp, l, which):
    g = grp['gi']
    NT = grp['NT']
    so = 0 if which == 0 else 48
    _norm(self, grp,
          lambda kt: self.mA[:, l, g, which, kt:kt + 1],
          lambda kt: self.modv[:, l, g, so + kt:so + kt + 1],
          lambda kt: self.h[:, kt, 0:NT], lambda kt: 'h', ['mA', 'modv'])


def _ffn(self, grp, l):
    NT = grp['NT']
    g = grp['gi']
    ntb = NT // 512
    seqs = grp['seqs']
    fcw, _ = PPC['fcw']
    fcb, _ = PPC['fcb']
    gb = self.yb
    wbx = [self.sbc("wbufx%d" % j, [128, 4096], BF16) for j in range(3)]
    for c_ in wbx:
        self.wbufs.append(c_.__enter__())
    self.w_rr = 0
    with self.sbc("gt0", [128, 1024]) as gt0, self.sbc("gt1", [128, 1024]) as gt1, \
            self.sbc("cv0", [128, 1024]) as cv0, self.sbc("cv1", [128, 1024]) as cv1:
        gts = [gt0, gt1]
        cvs = [cv0, cv1]
        chunks = [(c0, min(c0 + 8, NFT)) for c0 in range(0, NFT, 8)]
        for ci, (f0, f1) in enumerate(chunks):
            par = ci % 2
            gkey = 'gb%d' % par
            nf = f1 - f0
            fl = list(range(f0, f1))
            for p0 in range(0, nf, 2):
                pair = fl[p0:p0 + 2]
                np_ = len(pair)
                wg, wgk = self.wload(self.d_fwin[l, :, pair[0] * 128:(pair[0] + np_) * 128], 16, np_ * 128)
                wv, wvk = self.wload(self.d_fwin[l, :, DFF + pair[0] * 128:DFF + (pair[0] + np_) * 128], 16, np_ * 128)
                for pi, ft in enumerate(pair):
                    gbanks = [self.bank() for _ in range(ntb)]
                    vbanks = [self.bank() for _ in range(ntb)]
                    for (wt, wk, banks) in ((wg, wgk, gbanks), (wv, wvk, vbanks)):
                        for tb in range(ntb):
                            b = banks[tb]
                            for kt in range(16):
                                self.pe(lambda e, b=b, kt=kt, tb=tb, wt=wt, pi=pi: e.matmul(
                                    self.ps[b][:], wt[:, kt, pi * 128:(pi + 1) * 128], self.h[:, kt, tb * 512:(tb + 1) * 512],
                                    start=(kt == 0), stop=(kt == 15)), reads=[wk, 'h'], excl=['ps%d' % b])
                    gt = gts[ft % 2]
                    gk = 'gt%d' % (ft % 2)
                    cv = cvs[ft % 2]
                    ck = 'cv%d' % (ft % 2)
                    for tb in range(ntb):
                        b = gbanks[tb]
                        self.act(lambda e, b=b, tb=tb, gt=gt: e.activation(gt[:, tb * 512:(tb + 1) * 512], self.ps[b][:], AF.Copy),
                                 writes=[gk], excl=['ps%d' % b])
                    w0 = self.pp[:, l, fcw + 0 * NFT + ft:fcw + 0 * NFT + ft + 1]
                    w1 = self.pp[:, l, fcw + 1 * NFT + ft:fcw + 1 * NFT + ft + 1]
                    w2 = self.pp[:, l, fcw + 2 * NFT + ft:fcw + 2 * NFT + ft + 1]
                    bb = self.pp[:, l, fcb + ft:fcb + ft + 1]
                    self.dve(lambda e, gt=gt, cv=cv, w1=w1, bb=bb: e.tensor_scalar(cv[:, 0:NT], gt[:, 0:NT], w1, bb, op0=ALU.mult, op1=ALU.add),
                             reads=[gk, 'pp'], writes=[ck])
                    for (o, L) in seqs:
                        self.dve(lambda e, gt=gt, cv=cv, w0=w0, o=o, L=L: e.scalar_tensor_tensor(
                            cv[:, o + 1:o + L], gt[:, o:o + L - 1], w0, cv[:, o + 1:o + L], op0=ALU.mult, op1=ALU.add),
                            reads=[gk, ck, 'pp'], writes=[ck])
                        self.dve(lambda e, gt=gt, cv=cv, w2=w2, o=o, L=L: e.scalar_tensor_tensor(
                            cv[:, o:o + L - 1], gt[:, o + 1:o + L], w2, cv[:, o:o + L - 1], op0=ALU.mult, op1=ALU.add),
                            reads=[gk, ck, 'pp'], writes=[ck])
                    self.act(lambda e, cv=cv: e.activation(cv[:, 0:NT], cv[:, 0:NT], AF.Silu), reads=[ck], writes=[ck])
                    for tb in range(ntb):
                        b = vbanks[tb]
                        self.dve(lambda e, b=b, tb=tb, cv=cv, ft=ft: e.tensor_tensor(
                            gb[:, par * 8 + (ft - f0), tb * 512:(tb + 1) * 512], cv[:, tb * 512:(tb + 1) * 512], self.ps[b][:], op=ALU.mult),
                            reads=[ck], writes=[gkey], excl=['ps%d' % b])
            for cs in range(4):
                w2v, w2k = self.wload(self.d_fwout[l, f0 * 128:f1 * 128, cs * 512:(cs + 1) * 512], nf, 512)
                for jj in range(4):
                    j = cs * 4 + jj
                    g2 = self.modv[:, l, g, 80 + j:80 + j + 1]
                    for tb in range(ntb):
                        b = self.bank()
                        for k in range(nf):
                            self.pe(lambda e, b=b, k=k, tb=tb, jj=jj, w2v=w2v: e.matmul(
                                self.ps[b][:], w2v[:, k, jj * 128:(jj + 1) * 128], gb[:, par * 8 + k, tb * 512:(tb + 1) * 512],
                                start=(k == 0), stop=(k == nf - 1)), reads=[w2k, gkey], excl=['ps%d' % b])
                        self.dve(lambda e, b=b, j=j, tb=tb, g2=g2: e.scalar_tensor_tensor(
                            self.x[:, j, tb * 512:(tb + 1) * 512], self.ps[b][:], g2, self.x[:, j, tb * 512:(tb + 1) * 512], op0=ALU.mult, op1=ALU.add),
                            reads=['modv', 'x'], writes=['x'], excl=['ps%d' % b])
    self.S.barrier()
    for c_ in reversed(wbx):
        self.wbufs.pop()
        c_.__exit__(None, None, None)
    self.w_rr = 0


def _final_norm(self, grp):
    NT = grp['NT']
    o, _ = PPC['normf']
    with self.sbc("yo0", [128, 1024]) as yo0, self.sbc("yo1", [128, 1024]) as yo1:
        yo = [yo0, yo1]
        yv = self.yT[grp['name']].rearrange("(k p) n -> p k n", p=128)

        def out_fn(kt):
            return yo[kt % 2][:, 0:NT]

        _norm_final_impl(self, grp, yo, yv, o)


def _norm_final_impl(self, grp, yo, yv, o):
    NT = grp['NT']
    ntb = NT // 512
    with self.sbc("rstd", [128, 1024]) as rstd, self.sbc("sq0", [128, 512], BF16) as sq0, self.sbc("sq1", [128, 512], BF16) as sq1:
        sq = [sq0, sq1]
        for tb in range(ntb):
            b = self.bank()
            bk = 'ps%d' % b
            ts = slice(tb * 512, (tb + 1) * 512)
            for kt in range(16):
                s_ = sq[kt % 2]
                sk = 'sq%d' % (kt % 2)
                self.act(lambda e, kt=kt, s_=s_: e.activation(s_[:], self.x[:, kt, ts], AF.Square), reads=['x'], writes=[sk])
                self.pe(lambda e, kt=kt, s_=s_: e.matmul(self.ps[b][:], self.ones_bf[:], s_[:], start=(kt == 0), stop=(kt == 15)),
                        reads=[sk, 'ones_bf'], excl=[bk])
            self.act(lambda e: e.activation(rstd[:, ts], self.ps[b][:], AF.Sqrt, bias=self.cst[:, 0:1], scale=1.0 / D),
                     reads=['cst'], writes=['rstd'], excl=[bk])
        self.dve(lambda e: e.reciprocal(rstd[:, 0:NT], rstd[:, 0:NT]), reads=['rstd'], writes=['rstd'])
        for kt in range(16):
            t_ = yo[kt % 2]
            tk = 'yo%d' % (kt % 2)
            self.dve(lambda e, kt=kt, t_=t_: e.scalar_tensor_tensor(t_[:, 0:NT], self.x[:, kt, 0:NT], self.pp[:, 0, o + kt:o + kt + 1], rstd[:, 0:NT],
                                                                    op0=ALU.mult, op1=ALU.mult), reads=['x', 'rstd', 'pp'], writes=[tk])
            self.st(yv[:, kt, :], t_[:, 0:NT], reads=[tk])
    self.S.barrier()


def _build(self):
    cfg = self.cfg
    _setup(self)
    _setup_mixer(self)
    self.mark('start')
    _prologue_mod(self)
    self.S.barrier()
    self.mark('prologue_end')
    layers = cfg.get('layers', list(range(DEPTH)))
    phases = cfg.get('phases', ('mix', 'ffn'))
    for grp in GROUPS:
        if grp['name'] not in cfg.get('groups', ('s', 'p')):
            continue
        NT = grp['NT']
        xv = self.xT[grp['name']].rearrange("(k p) n -> p k n", p=128)
        evs_ = [self.ld(self.x[:, kt, 0:NT], xv[:, kt, :], writes=['x']) for kt in range(16)]
        for e_ in ('pe', 'act', 'dve', 'pool'):
            self.S._need(e_, evs_)
        for l in layers:
            if self.cfg.get('wcache', True) and len(cfg.get('groups', ('s', 'p'))) == 2:
                self.wc_mode = ('store' if grp['name'] == 's' else 'load', l)
                self.wc_idx[l] = 0
            if 'mix' in phases:
                _norm_mod(self, grp, l, 0)
                self.mark(grp['name'] + ' l' + str(l) + ' norm1_end')
                _mixer(self, grp, l)
                self.mark(grp['name'] + ' l' + str(l) + ' mixer_end')
            if 'ffn' in phases:
                _norm_mod(self, grp, l, 1)
                _ffn(self, grp, l)
                self.mark(grp['name'] + ' l' + str(l) + ' ffn_end')
            self.wc_mode = None
        _final_norm(self, grp)
        if not grp['lat']:
            for si in range(2):
                self.st(self.o_s5[si].rearrange("l p c -> p l c"), self.s5o[:, si, :, :], reads=['s5o'])
    self.S.final()
    return self.nc


def rope_tables():
    t = np.arange(1024)
    inv = (10000.0 ** (-np.arange(16, dtype=np.float32) / 16.0)).astype(np.float32)
    cos = np.zeros((64, 1024), np.float32)
    sin = np.zeros((64, 1024), np.float32)
    for half, pos in ((0, t // 64), (1, t % 64)):
        ang = pos.astype(np.float32)[None, :] * inv[:, None]
        c_, s_ = np.cos(ang).astype(np.float32), np.sin(ang).astype(np.float32)
        cos[half * 32:half * 32 + 16] = c_
        cos[half * 32 + 16:half * 32 + 32] = c_
        sin[half * 32:half * 32 + 16] = -s_
        sin[half * 32 + 16:half * 32 + 32] = s_
    return np.stack([np.concatenate([cos, cos], 0), np.concatenate([sin, sin], 0)], 0)


def host_consts():
    c = np.zeros((128, 4, 128), np.float32)
    idx = np.arange(128)
    c[idx ^ 16, 2, idx] = 1.0
    c[:, 0, :] = np.eye(128, dtype=np.float32)
    c[0:64, 1, 0:64] = 1.0
    c[64:128, 1, 64:128] = 1.0
    return c.reshape(128, 512)


def make_in_maps(inp, cfg=None, ncores=8):
    cfg = cfg or {}
    pp = np.stack([pack_params(inp, l) for l in range(DEPTH)], axis=0)
    consts = host_consts()
    shared = dict(pp=pp, consts=consts,
                  mod_w=np.asarray(inp['mod_w'], np.float32), w_in=np.asarray(inp['w_in'], np.float32),
                  branch_w=np.asarray(inp['branch_w'], np.float32).reshape(DEPTH, 4 * W, D),
                  w_out=np.asarray(inp['w_out'], np.float32), ffn_w_in=np.asarray(inp['ffn_w_in'], np.float32),
                  ffn_w_out=np.asarray(inp['ffn_w_out'], np.float32))
    maps = []
    for c in range(ncores):
        m = dict(shared)
        m['xT_s'] = np.ascontiguousarray(np.asarray(inp['x_sample'][c], np.float32).T)
        m['xT_p'] = np.ascontiguousarray(np.asarray(inp['x_prompt'][2 * c:2 * c + 2], np.float32).reshape(512, D).T)
        cc = np.stack([_fm(inp['c'][c], 16), _fm(inp['c_ctx'], 16)], axis=-1)
        m['cc'] = np.ascontiguousarray(cc)
        _mixer_in_maps(inp, c, m)
        maps.append(m)
    return maps


def _proj(self, l, col0, ncols):
    return self.wload(self.d_win[l, :, col0:col0 + ncols], 16, ncols)


def _proj_tile(self, wv, wk, ci, NT):
    banks = []
    for tb in range(NT // 512):
        b = self.bank()
        for kt in range(16):
            self.pe(lambda e, b=b, kt=kt, tb=tb: e.matmul(self.ps[b][:], wv[:, kt, ci * 128:(ci + 1) * 128], self.h[:, kt, tb * 512:(tb + 1) * 512],
                                                           start=(kt == 0), stop=(kt == 15)), reads=[wk, 'h'], excl=['ps%d' % b])
        banks.append(b)
    return banks


def _evac(self, eng, dst, dkey, banks, func=None, reads=(), **kw):
    for tb, b in enumerate(banks):
        if eng == 'act':
            self.act(lambda e, b=b, tb=tb: e.activation(dst[:, tb * 512:(tb + 1) * 512], self.ps[b][:], func or AF.Copy, **kw),
                     reads=reads, writes=[dkey], excl=['ps%d' % b])
        else:
            self.dve(lambda e, b=b, tb=tb: e.tensor_copy(dst[:, tb * 512:(tb + 1) * 512], self.ps[b][:]), reads=reads, writes=[dkey], excl=['ps%d' % b])


def _gelu_from(self, src, skey, NT, t1, t1k, out, okey):
    self.act(lambda e: e.activation(t1[:, 0:NT], src[:, 0:NT], AF.Square), reads=[skey], writes=[t1k])
    self.dve(lambda e: e.tensor_scalar(t1[:, 0:NT], t1[:, 0:NT], 0.044715, 1.0, op0=ALU.mult, op1=ALU.add), reads=[t1k], writes=[t1k])
    self.dve(lambda e: e.tensor_tensor(t1[:, 0:NT], t1[:, 0:NT], src[:, 0:NT], op=ALU.mult), reads=[t1k, skey], writes=[t1k])
    self.act(lambda e: e.activation(t1[:, 0:NT], t1[:, 0:NT], AF.Sigmoid, scale=C_GELU), reads=[t1k], writes=[t1k])
    self.dve(lambda e: e.tensor_tensor(out[:, 0:NT], t1[:, 0:NT], src[:, 0:NT], op=ALU.mult), reads=[t1k, skey], writes=[okey])


def _mix_A(self, grp, l, T):
    NT, seqs, lat = grp['NT'], grp['seqs'], grp['lat']
    lcw, _ = PPC['lcw']
    with self.sbc("lruw", [128, 16, 128]) as lruw, self.sbc("clam", [128, 8]) as clam:
        self.ld(lruw[:], self.d_lruw[l].rearrange("m p n -> p m n"), writes=['lruw'])
        self.act(lambda e: e.activation(clam[:], ppc(self, l, 'llam', 0, 8), AF.Sigmoid), reads=['pp'], writes=['clam'])
        self.act(lambda e: e.activation(clam[:], clam[:], AF.Ln), reads=['clam'], writes=['clam'])
        self.dve(lambda e: e.tensor_scalar(clam[:], clam[:], 8.0, None, op0=ALU.mult), reads=['clam'], writes=['clam'])
        axs, xa, av, ig, hs0, hs1, t1, agc = T[0:8]
        hs = [hs0, hs1]
        for i in range(4):
            wx_, wxk = _proj(self, l, OFF['ax'] + i * 128, 128)
            wg_, wgk = _proj(self, l, OFF['ag'] + i * 128, 128)
            bx_ = _proj_tile(self, wx_, wxk, 0, NT)
            bg_ = _proj_tile(self, wg_, wgk, 0, NT)
            _evac(self, 'act', axs, 'T0', bx_)
            _evac(self, 'act', agc, 'T7', bg_)
            wj = [self.pp[:, l, lcw + j * 4 + i:lcw + j * 4 + i + 1] for j in range(4)]
            self.dve(lambda e: e.tensor_scalar(xa[:, 0:NT], axs[:, 0:NT], wj[2], ppc(self, l, 'lcb', i), op0=ALU.mult, op1=ALU.add),
                     reads=['T0', 'pp'], writes=['T1'])
            for (o, L) in seqs:
                for (j, sh) in ((0, -2), (1, -1), (3, 1)):
                    if sh < 0:
                        dst = xa[:, o - sh:o + L]
                        src = axs[:, o:o + L + sh]
                    else:
                        dst = xa[:, o:o + L - sh]
                        src = axs[:, o + sh:o + L]
                    self.dve(lambda e, dst=dst, src=src, j=j: e.scalar_tensor_tensor(dst, src, wj[j], dst, op0=ALU.mult, op1=ALU.add),
                             reads=['T0', 'T1', 'pp'], writes=['T1'])
            self.dump('xa%d' % i, xa[:, 0:NT], 'T1', [128, NT])
            for d in range(2):
                ba_ = [self.bank() for _ in range(NT // 512)]
                bi_ = [self.bank() for _ in range(NT // 512)]
                for (wi, banks) in ((0, ba_), (1, bi_)):
                    mi = (wi * 2 + d) * 4 + i
                    for tb, b in enumerate(banks):
                        self.pe(lambda e, b=b, tb=tb, mi=mi: e.matmul(self.ps[b][:], lruw[:, mi, :], xa[:, tb * 512:(tb + 1) * 512], start=True, stop=True),
                                reads=['lruw', 'T1'], excl=['ps%d' % b])
                _evac(self, 'act', av, 'T2', ba_, AF.Sigmoid, reads=['pp'], bias=ppc(self, l, 'lba', d * 4 + i))
                _evac(self, 'act', ig, 'T3', bi_, AF.Sigmoid, reads=['pp'], bias=ppc(self, l, 'lbx', d * 4 + i))
                self.dve(lambda e, d=d: e.tensor_scalar(t1[:, 0:NT], av[:, 0:NT], clam[:, d * 4 + i:d * 4 + i + 1], None, op0=ALU.mult), reads=['T2', 'clam'], writes=['T6'])
                self.dve(lambda e: e.tensor_scalar(av[:, 0:NT], t1[:, 0:NT], 1.0 / 120.0, None, op0=ALU.mult), reads=['T6'], writes=['T2'])
                for ck_ in (1.0 / 24.0, 1.0 / 6.0, 0.5, 1.0):
                    self.dve(lambda e, ck_=ck_: e.scalar_tensor_tensor(av[:, 0:NT], av[:, 0:NT], ck_, t1[:, 0:NT], op0=ALU.add, op1=ALU.mult), reads=['T2', 'T6'], writes=['T2'])
                self.dve(lambda e: e.scalar_tensor_tensor(t1[:, 0:NT], av[:, 0:NT], 2.0, av[:, 0:NT], op0=ALU.add, op1=ALU.mult), reads=['T2', 'T6'], writes=['T6'])
                self.act(lambda e: e.activation(t1[:, 0:NT], t1[:, 0:NT], AF.Sqrt, scale=-1.0), reads=['T6'], writes=['T6'])
                self.dve(lambda e: e.tensor_scalar(av[:, 0:NT], av[:, 0:NT], 1.0, None, op0=ALU.add), reads=['T2', 'T6'], writes=['T2'])
                self.dve(lambda e: e.tensor_tensor(ig[:, 0:NT], ig[:, 0:NT], xa[:, 0:NT], op=ALU.mult), reads=['T3', 'T1'], writes=['T3'])
                self.dve(lambda e: e.tensor_tensor(ig[:, 0:NT], ig[:, 0:NT], t1[:, 0:NT], op=ALU.mult), reads=['T3', 'T6'], writes=['T3'])
                hk = 'T%d' % (4 + d)
                for si, (o, L) in enumerate(seqs):
                    h0 = self.lat_lru[:, (l * 2 + d) * 4 + i:(l * 2 + d) * 4 + i + 1] if lat else 0.0
                    rk = ['T2', 'T3'] + (['lat_lru'] if lat else [])
                    if d == 0:
                        self.dve(lambda e, o=o, L=L, h0=h0: e.tensor_tensor_scan(hs[0][:, o:o + L], av[:, o:o + L], ig[:, o:o + L], h0, op0=ALU.mult, op1=ALU.add),
                                 reads=rk, writes=[hk])
                    else:
                        self.dve(lambda e, o=o, L=L, h0=h0: e.tensor_tensor_scan(hs[1][:, o:o + L][:, ::-1], av[:, o:o + L][:, ::-1], ig[:, o:o + L][:, ::-1], h0,
                                                                                  op0=ALU.mult, op1=ALU.add), reads=rk, writes=[hk])
                    if not lat:
                        col = o + L - 1 if d == 0 else o
                        self.st(self.o_lru[si, l, d, i * 128:(i + 1) * 128].rearrange("(p a) -> p a", a=1), hs[d][:, col:col + 1], reads=[hk])
            self.dve(lambda e: e.tensor_tensor(hs0[:, 0:NT], hs0[:, 0:NT], hs1[:, 0:NT], op=ALU.add), reads=['T4', 'T5'], writes=['T4'])
            _gelu_from(self, agc, 'T7', NT, t1, 'T6', agc, 'T7')
            self.dve(lambda e, i=i: e.tensor_tensor(self.yb[:, i, 0:NT], agc[:, 0:NT], hs0[:, 0:NT], op=ALU.mult), reads=['T7', 'T4'], writes=['yb'])
    self.S.barrier()


def _merge_out(self, grp, l):
    NT = grp['NT']
    g = grp['gi']
    ntb = NT // 512
    wb4 = self.sbc("wbuf3", [128, 4096], BF16)
    self.wbufs.append(wb4.__enter__())
    wb5 = self.sbc("wbuf4", [128, 4096], BF16)
    self.wbufs.append(wb5.__enter__())
    self.w_rr = 0
    with self.sbc("z", [128, 4, 1024]) as z, self.sbc("zb", [128, 4, 1024], BF16) as zb, self.sbc("sg", [128, 1024], BF16) as sg, \
            self.sbc("tm", [128, 512], BF16) as tm:
        for zc in range(4):
            for n in range(4):
                wb, wbk = self.wload(self.d_bw[l, n * 512:(n + 1) * 512, zc * 512:(zc + 1) * 512], 4, 512)
                for jp in range(2):
                    c0 = OFF['mg'] + n * D + (zc * 4 + jp * 2) * 128
                    wg, wgk = _proj(self, l, c0, 256)
                    for j2 in range(2):
                        jj = jp * 2 + j2
                        gb_ = _proj_tile(self, wg, wgk, j2, NT)
                        pb_ = []
                        for tb in range(ntb):
                            b = self.bank()
                            for k in range(4):
                                self.pe(lambda e, b=b, k=k, tb=tb: e.matmul(self.ps[b][:], wb[:, k, jj * 128:(jj + 1) * 128], self.yb[:, n * 4 + k, tb * 512:(tb + 1) * 512],
                                                                          start=(k == 0), stop=(k == 3)), reads=[wbk, 'yb'], excl=['ps%d' % b])
                            pb_.append(b)
                        _evac(self, 'act', sg, 'sg', gb_, AF.Sigmoid)
                        for tb, b in enumerate(pb_):
                            ts = slice(tb * 512, (tb + 1) * 512)
                            if n == 0:
                                self.dve(lambda e, b=b, ts=ts: e.tensor_tensor(z[:, jj, ts], sg[:, ts], self.ps[b][:], op=ALU.mult),
                                         reads=['sg'], writes=['z'], excl=['ps%d' % b])
                            else:
                                self.dve(lambda e, b=b, ts=ts: e.tensor_tensor(tm[:, 0:512], sg[:, ts], self.ps[b][:], op=ALU.mult),
                                         reads=['sg'], writes=['tm'], excl=['ps%d' % b])
                                self.dve(lambda e, ts=ts: e.tensor_tensor(z[:, jj, ts], z[:, jj, ts], tm[:, 0:512], op=ALU.add),
                                         reads=['tm', 'z'], writes=['z'])
            for jj in range(4):
                self.act(lambda e, jj=jj: e.activation(zb[:, jj, 0:NT], z[:, jj, 0:NT], AF.Copy), reads=['z'], writes=['zb'])
            if zc == 0:
                self.dump('z0', zb[:, :, 0:NT], 'zb', [128, 4, NT])
            for cs in range(4):
                wo, wok = self.wload(self.d_wout[l, zc * 512:(zc + 1) * 512, cs * 512:(cs + 1) * 512], 4, 512)
                for jj in range(4):
                    jo = cs * 4 + jj
                    g1 = self.modv[:, l, g, 32 + jo:32 + jo + 1]
                    for tb in range(ntb):
                        b = self.bank()
                        ts = slice(tb * 512, (tb + 1) * 512)
                        for k in range(4):
                            self.pe(lambda e, b=b, k=k, ts=ts: e.matmul(self.ps[b][:], wo[:, k, jj * 128:(jj + 1) * 128], zb[:, k, ts], start=(k == 0), stop=(k == 3)),
                                    reads=[wok, 'zb'], excl=['ps%d' % b])
                        self.dve(lambda e, b=b, jo=jo, ts=ts, g1=g1: e.scalar_tensor_tensor(self.x[:, jo, ts], self.ps[b][:], g1, self.x[:, jo, ts], op0=ALU.mult, op1=ALU.add),
                                 reads=['modv', 'x'], writes=['x'], excl=['ps%d' % b])
    self.S.barrier()
    self.wbufs.pop()
    self.wbufs.pop()
    wb5.__exit__(None, None, None)
    wb4.__exit__(None, None, None)
    self.w_rr = 0


def _mixer(self, grp, l):
    NT = grp['NT']
    parts = self.cfg.get('parts', 'ABCD')
    self.S.barrier()
    for kt in range(16):
        self.st(self.xspill[:, kt, 0:NT], self.x[:, kt, 0:NT], reads=['x'])
    if 'A' not in parts:
        self.S.barrier()
    if 'A' in parts:
        Tcm = [self.sbc("T%d" % i, [128, 1024]) for i in range(8)]
        T = [c.__enter__() for c in Tcm]
        _mix_A(self, grp, l, T)
        self.mark(grp['name'] + ' l' + str(l) + ' mixA_end')
        for c in reversed(Tcm):
            c.__exit__(None, None, None)
    if 'B' in parts:
        _mix_B(self, grp, l)
        self.mark(grp['name'] + ' l' + str(l) + ' mixB_end')
    if 'C' in parts:
        _mix_C(self, grp, l)
        self.mark(grp['name'] + ' l' + str(l) + ' mixC_end')
    if 'D' in parts:
        _mix_D(self, grp, l)
        self.mark(grp['name'] + ' l' + str(l) + ' mixD_end')
    self.S.barrier()
    evs_ = [self.ld(self.x[:, kt, 0:NT], self.xspill[:, kt, 0:NT], writes=['x']) for kt in range(16)]
    for e_ in ('pe', 'act', 'dve', 'pool'):
        self.S._need(e_, evs_)
    for n in range(4):
        self.dump('y%d' % n, self.yb[:, n * 4:(n + 1) * 4, 0:NT], 'yb', [128, 4, NT])
    if self.cfg.get('merge', True):
        _merge_out(self, grp, l)


def _setup_mixer(self):
    self.d_lruw = self.din("lruw", [DEPTH, 16, 128, 128])
    self.d_latlru = self.din("lat_lru", [128, DEPTH * 2 * 4])
    self.lat_lru = self.sb("lat_lru", [128, DEPTH * 2 * 4])
    self.ld(self.lat_lru[:], self.d_latlru, writes=['lat_lru'])
    self.o_lru = self.dout("o_lru", [2, DEPTH, 2, W])
    self.d_s5bl = self.din("s5bl", [DEPTH, 4, 128, 1024])
    self.d_s5cl = self.din("s5cl", [DEPTH, 4, 128, 1024])
    self.d_glu = self.din("s5_glu_w", [DEPTH, W, W])
    self.d_lats5 = self.din("lat_s5", [128, DEPTH * 2 * 32])
    self.lat_s5 = self.sb("lat_s5", [128, DEPTH * 2 * 32])
    self.ld(self.lat_s5[:], self.d_lats5, writes=['lat_s5'])
    self.o_s5 = self.dout("o_s5", [2, DEPTH, 128, 64])
    self.d_amask = self.din("amask", [128, 2, 128])
    self.d_kcT = self.din("kcT", [DEPTH, 128, 2, 256])
    self.d_vc = self.din("vc", [DEPTH, 256, 128])
    self.d_rope = self.din("rope", [2, 128, 1024])
    self.d_wkdup = self.din("wkdup", [DEPTH, 2, D, 128])
    self.o_k = self.dout("o_k", [2, DEPTH, 256, 128])
    self.o_v = self.dout("o_v", [2, DEPTH, 256, 128])
    self.d_dmask = self.din("dmask", [128, 4, 512])
    self.d_lora = self.din("lora", [DEPTH, 4, 128, 5, 128])
    self.d_latwkv = self.din("lat_wkv", [DEPTH, 2, 4, 128, 128])
    self.o_wkv = self.dout("o_wkv", [2, DEPTH, 2, 8, 64, 64])
    self.xspill = self.nc.dram_tensor("xspill", [128, 16, 1024], F32, kind="Internal").ap()
    self.wcache = [self.nc.dram_tensor("wcache%d" % l_, [WC_MAX, 128, 4096], BF16, kind="Internal").ap() for l_ in range(DEPTH)]
    self.wc_n = [0] * DEPTH
    self.wc_idx = [0] * DEPTH
    self.wc_mode = None
    self.s5o = self.sb("s5o", [128, 2, DEPTH, 64])
    self.pool(lambda e: e.memset(self.s5o[:], 0.0), writes=['s5o'])
    self.d_iota = self.din("iota", [128, 512])
    self.iota = self.sb("iota", [128, 512])
    self.ld(self.iota[:], self.d_iota, writes=['iota'])
    self.cst2 = self.sb("cst2", [128, 2])
    self.pool(lambda e: e.memset(self.cst2[:, 0:1], math.pi / 2.0), writes=['cst'])


def _mixer_in_maps(inp, c, m):
    lw = np.zeros((DEPTH, 2, 2, 4, 128, 128), np.float32)
    for l in range(DEPTH):
        for wi, nm in enumerate(('lru_wa', 'lru_wx')):
            for d in range(2):
                for i in range(4):
                    for hb in range(2):
                        lw[l, wi, d, i, hb * 64:(hb + 1) * 64, hb * 64:(hb + 1) * 64] = inp[nm][l, d, 2 * i + hb]
    m['lruw'] = lw.reshape(DEPTH, 16, 128, 128)
    sl = np.asarray(inp['state_lru'][c], np.float32)
    m['lat_lru'] = np.ascontiguousarray(sl.reshape(DEPTH * 2 * 4, 128).T)
    kk = np.arange(128)[:, None]
    qq = np.arange(128)[None, :]
    m['amask'] = np.stack([(kk >= qq), (kk <= qq)], axis=1).astype(np.float32)
    ck = np.asarray(inp['cache_k'][c], np.float32)
    kct = ck.transpose(0, 2, 3, 1)
    m['kcT'] = np.ascontiguousarray(np.stack([kct, kct], axis=2).reshape(DEPTH, 2, 128, 256).transpose(0, 2, 1, 3))
    m['vc'] = np.ascontiguousarray(np.asarray(inp['cache_v'][c], np.float32).reshape(DEPTH, 256, 128))
    m['rope'] = rope_tables()
    wk = np.asarray(inp['w_in'], np.float32)[:, :, OFF['ck']:OFF['ck'] + 128].reshape(DEPTH, D, 2, 64)
    m['wkdup'] = np.ascontiguousarray(np.stack([wk, wk], axis=3).transpose(0, 2, 1, 3, 4).reshape(DEPTH, 2, D, 128))
    p_ = np.arange(128)[:, None]
    f_ = np.arange(128)[None, :]
    Us, Ui, Ls, Li = [a.astype(np.float32) for a in ((f_ > p_), (f_ >= p_), (f_ < p_), (f_ <= p_))]
    dm = np.zeros((128, 4, 4, 128), np.float32)
    for u in range(4):
        fw = u < 2
        dm[:, 0, u, :] = -(Us if fw else Ls)
        dm[:, 1, u, :] = -(Ls if fw else Us)
        dm[:, 2, u, :] = (Us if fw else Ls)
        dm[:, 3, u, :] = (Ui if fw else Li)
    m['dmask'] = dm.reshape(128, 4, 512)
    lo = np.zeros((DEPTH, 4, 128, 5, 128), np.float32)
    for l in range(DEPTH):
        for i in range(4):
            cs = slice(i * 128, (i + 1) * 128)
            for d in range(2):
                lo[l, i, 0:64, d, :] = inp['rwkv_w2'][l, d][:, cs]
                lo[l, i, 64:128, 2 + d, :] = inp['rwkv_a2'][l, d][:, cs]
            lo[l, i, :, 4, :] = inp['rwkv_g2'][l][:, cs]
    m['lora'] = lo
    sw = np.asarray(inp['state_wkv'][c], np.float32)
    lwk = np.zeros((DEPTH, 2, 4, 128, 128), np.float32)
    for i in range(4):
        for hh in range(2):
            lwk[:, :, i, hh * 64:(hh + 1) * 64, hh * 64:(hh + 1) * 64] = sw[:, :, 2 * i + hh].transpose(0, 1, 3, 2)
    m['lat_wkv'] = lwk
    m['iota'] = np.broadcast_to(np.arange(1, 513, dtype=np.float32)[None, :], (128, 512)).copy()
    m['s5_glu_w'] = np.asarray(inp['s5_glu_w'], np.float32)
    bl = np.zeros((DEPTH, 4, 8, 16, 2, 2, 2, 2, 64), np.float32)
    cl = np.zeros((DEPTH, 4, 2, 64, 2, 2, 4, 4, 16), np.float32)
    for l in range(DEPTH):
        for d in range(2):
            for r, (bn, cn) in enumerate((('s5_b_re', 's5_c_re'), ('s5_b_im', 's5_c_im'))):
                Bm = np.asarray(inp[bn][l, d], np.float32)
                Cm = np.asarray(inp[cn][l, d], np.float32)
                for ci in range(4):
                    for g8 in range(8):
                        g = ci * 8 + g8
                        j = (g8 % 4) // 2
                        g2 = g8 % 2
                        bl[l, ci, g8, :, d, r, j, g2, :] = Bm[g].T
                        hf = g8 // 4
                        g4 = g8 % 4
                        stl = hf * 2 + j
                        cl[l, ci, g2, :, d, r, stl, g4, :] = Cm[g].T
    m['s5bl'] = bl.reshape(DEPTH, 4, 128, 1024)
    m['s5cl'] = cl.reshape(DEPTH, 4, 128, 1024)
    s5 = np.asarray(inp['state_s5'][c], np.float32)
    t = s5.reshape(DEPTH, 2, 2, 16, 2, 64).transpose(4, 5, 0, 2, 1, 3)
    m['lat_s5'] = np.ascontiguousarray(t.reshape(128, DEPTH * 2 * 32))


TWO_PI = 2.0 * math.pi


def _mix_B(self, grp, l):
    NT, seqs, lat = grp['NT'], grp['seqs'], grp['lat']
    ntb = NT // 512
    SM = 'smB'
    X = [self.x[:, k, :] for k in range(16)]
    sets = []
    specs = [("s5sm", [128, 24, 32], F32), ("s5i0", [128, 512], I32), ("s5i1", [128, 512], I32), ("hri", [128, 4, 1024], BF16), ("ub", [128, 1024], BF16),
             ("Bl", [128, 1024], BF16), ("C12", [128, 1024], BF16), ("carry", [128, 4], F32), ("fin", [128, 2, 2, 32], F32)]
    cms = [self.sbc(n, sh, dt) for (n, sh, dt) in specs]
    (sm, ti0, ti1, hri, ub, Bl, C12, carry, fin) = [c.__enter__() for c in cms]
    ti = ti0
    for d_ in range(2):
        xs = [X[3 * d_ + q][:, h_ * 512:(h_ + 1) * 512] for q in range(3) for h_ in range(2)]
        sets.append(tuple(xs) + ((ti0, ti1)[d_],))
    ys, uf = X[6], X[7]
    A2 = sets[0][3]
    if True:
        are = ppc(self, l, 'are', 0, 32)
        aim = ppc(self, l, 'aim', 0, 32)
        ldt = ppc(self, l, 'ldt', 0, 32)
        (dt, mag, phi, t0, t1, cr, sr, abr, abi, den, fr, fi, h0r, h0i, cL, sL, nfi, t2, t3) = [sm[:, i, :] for i in range(19)]
        smi = ti[:, 0:32]

        def V(fn, extra=()):
            self.dve(fn, reads=[SM, 'pp'] + list(extra), writes=[SM])

        def Ac(fn, extra=()):
            self.act(fn, reads=[SM, 'pp', 'cst'] + list(extra), writes=[SM])

        def sincos(ang_turns, c_out, s_out):
            V(lambda e: e.tensor_copy(smi, ang_turns))
            V(lambda e: e.tensor_copy(t1, smi))
            V(lambda e: e.tensor_tensor(t1, ang_turns, t1, op=ALU.subtract))
            Ac(lambda e: e.activation(s_out, t1, AF.Sin, scale=TWO_PI))
            Ac(lambda e: e.activation(t1, t1, AF.Abs))
            Ac(lambda e: e.activation(c_out, t1, AF.Sin, scale=-TWO_PI, bias=self.cst2[:, 0:1]))

        Ac(lambda e: e.activation(dt, ldt, AF.Exp))
        V(lambda e: e.tensor_tensor(t0, are, dt, op=ALU.mult))
        V(lambda e: e.tensor_scalar(mag, t0, 1.0 / 120.0, None, op0=ALU.mult))
        for ck_ in (1.0 / 24.0, 1.0 / 6.0, 0.5, 1.0):
            V(lambda e, ck_=ck_: e.scalar_tensor_tensor(mag, mag, ck_, t0, op0=ALU.add, op1=ALU.mult))
        V(lambda e: e.tensor_scalar(mag, mag, 1.0, None, op0=ALU.add))
        V(lambda e: e.scalar_tensor_tensor(phi, aim, 1.0 / TWO_PI, dt, op0=ALU.mult, op1=ALU.mult))
        sincos(phi, cr, sr)
        V(lambda e: e.tensor_tensor(abr, mag, cr, op=ALU.mult))
        V(lambda e: e.tensor_tensor(abi, mag, sr, op=ALU.mult))
        V(lambda e: e.tensor_tensor(den, are, are, op=ALU.mult))
        V(lambda e: e.tensor_tensor(t0, aim, aim, op=ALU.mult))
        V(lambda e: e.tensor_tensor(den, den, t0, op=ALU.add))
        V(lambda e: e.reciprocal(den, den))
        V(lambda e: e.tensor_scalar(t0, abr, -1.0, None, op0=ALU.add))
        V(lambda e: e.tensor_tensor(fr, t0, are, op=ALU.mult))
        V(lambda e: e.tensor_tensor(t2, abi, aim, op=ALU.mult))
        V(lambda e: e.tensor_tensor(fr, fr, t2, op=ALU.add))
        V(lambda e: e.tensor_tensor(fr, fr, den, op=ALU.mult))
        V(lambda e: e.tensor_tensor(fi, abi, are, op=ALU.mult))
        V(lambda e: e.tensor_tensor(t2, t0, aim, op=ALU.mult))
        V(lambda e: e.tensor_tensor(fi, fi, t2, op=ALU.subtract))
        V(lambda e: e.tensor_tensor(fi, fi, den, op=ALU.mult))
        V(lambda e: e.tensor_scalar(nfi, fi, -1.0, None, op0=ALU.mult))
        if lat:
            s0r = self.lat_s5[:, (l * 2 + 0) * 32:(l * 2 + 0) * 32 + 32]
            s0i = self.lat_s5[:, (l * 2 + 1) * 32:(l * 2 + 1) * 32 + 32]
            V(lambda e: e.tensor_tensor(t0, fr, fr, op=ALU.mult))
            V(lambda e: e.tensor_tensor(t2, fi, fi, op=ALU.mult))
            V(lambda e: e.tensor_tensor(t0, t0, t2, op=ALU.add))
            V(lambda e: e.reciprocal(t0, t0))
            V(lambda e: e.tensor_tensor(h0r, s0r, fr, op=ALU.mult), ['lat_s5'])
            V(lambda e: e.tensor_tensor(t2, s0i, fi, op=ALU.mult), ['lat_s5'])
            V(lambda e: e.tensor_tensor(h0r, h0r, t2, op=ALU.add))
            V(lambda e: e.tensor_tensor(h0r, h0r, t0, op=ALU.mult))
            V(lambda e: e.tensor_tensor(h0i, s0i, fr, op=ALU.mult), ['lat_s5'])
            V(lambda e: e.tensor_tensor(t2, s0r, fi, op=ALU.mult), ['lat_s5'])
            V(lambda e: e.tensor_tensor(h0i, h0i, t2, op=ALU.subtract))
            V(lambda e: e.tensor_tensor(h0i, h0i, t0, op=ALU.mult))
        else:
            Lq = seqs[0][1]
            V(lambda e: e.tensor_scalar(t3, phi, float(Lq), None, op0=ALU.mult))
            sincos(t3, cL, sL)

        def unit(ci, hf, hs_, j, stl, st, d, col, si, so, L, nblk, Lb, bi_, bk_):
            cosT, sinT, A1, A2, gr, gi, ti = sets[d]
            k0, k1, k2, k3, k4, k5, kti, khri, kcar = ['%s_%d' % (n, d) for n in ('T0', 'T1', 'T2', 'T3', 'T4', 'T5', 'ti', 'hri', 'carry')]
            o = so + bk_ * Lb
            cs_ = slice(o, o + Lb)
            ls_ = slice(0, Lb)
            for r_, dst, dk in ((0, A1, k2), (1, A2, k3)):
                b = self.bank()
                self.pe(lambda e, b=b, r_=r_: e.matmul(self.ps[b][:, 0:Lb], blv[hs_, d, r_, j, :], ub[hs_, cs_], start=True, stop=True),
                        reads=['Bl', 'ub'], excl=['ps%d' % b])
                self.act(lambda e, b=b, dst=dst: e.activation(dst[:, ls_], self.ps[b][:, 0:Lb], AF.Copy), writes=[dk], excl=['ps%d' % b])
            tl = bk_ * Lb
            if d == 0:
                io = self.iota[:, 0:Lb]
                ioff = float(tl)
            else:
                io = self.iota[:, 0:Lb][:, ::-1]
                ioff = float(L - tl - Lb)
            ph = sm[:, 2, col:col + 1]
            share_tab = (not lat) and si > 0 and nblk == 1 and seqs[si][1] == seqs[0][1]
            if not share_tab:
                self.dve(lambda e: e.tensor_scalar(sinT[:, ls_], io, ioff, ph, op0=ALU.add, op1=ALU.mult), reads=['iota', SM], writes=[k1])
                yield
                self.dve(lambda e: e.tensor_copy(ti[:, ls_], sinT[:, ls_]), reads=[k1], writes=[kti])
                yield
                self.dve(lambda e: e.tensor_copy(cosT[:, ls_], ti[:, ls_]), reads=[kti], writes=[k0])
                yield
                self.dve(lambda e: e.tensor_tensor(cosT[:, ls_], sinT[:, ls_], cosT[:, ls_], op=ALU.subtract), reads=[k1, k0], writes=[k0])
                yield
                self.act(lambda e: e.activation(sinT[:, ls_], cosT[:, ls_], AF.Sin, scale=TWO_PI), reads=[k0], writes=[k1])
                self.act(lambda e: e.activation(cosT[:, ls_], cosT[:, ls_], AF.Abs), reads=[k0, k1], writes=[k0])
                self.act(lambda e: e.activation(cosT[:, ls_], cosT[:, ls_], AF.Sin, scale=-TWO_PI, bias=self.cst2[:, 0:1]), reads=[k0, 'cst'], writes=[k0])
            yield
            self.dve(lambda e: e.tensor_tensor(gr[:, ls_], A1[:, ls_], cosT[:, ls_], op=ALU.mult), reads=[k2, k0], writes=[k4])
            self.pool(lambda e: e.tensor_tensor(gi[:, ls_], A2[:, ls_], cosT[:, ls_], op=ALU.mult), reads=[k3, k0], writes=[k5])
            yield
            self.dve(lambda e: e.tensor_tensor(A2[:, ls_], A2[:, ls_], sinT[:, ls_], op=ALU.mult), reads=[k3, k1, k5], writes=[k3])
            self.pool(lambda e: e.tensor_tensor(A1[:, ls_], A1[:, ls_], sinT[:, ls_], op=ALU.mult), reads=[k2, k1, k4], writes=[k2])
            yield
            self.dve(lambda e: e.tensor_tensor(gr[:, ls_], gr[:, ls_], A2[:, ls_], op=ALU.add), reads=[k4, k3], writes=[k4])
            self.pool(lambda e: e.tensor_tensor(gi[:, ls_], gi[:, ls_], A1[:, ls_], op=ALU.subtract), reads=[k5, k2], writes=[k5])
            yield
            mg_ = _bcast(sm[:, 1, col:col + 1], [128, Lb])
            if bi_ == 0:
                inr = sm[:, 12, col:col + 1] if lat else 0.0
                ini = sm[:, 13, col:col + 1] if lat else 0.0
            else:
                inr = carry[:, 2 * d:2 * d + 1]
                ini = carry[:, 2 * d + 1:2 * d + 2]
            for (gt_, gk_, in_) in ((gr, k4, inr), (gi, k5, ini)):
                if d == 0:
                    self.dve(lambda e, gt_=gt_, in_=in_: e.tensor_tensor_scan(gt_[:, ls_], mg_, gt_[:, ls_], in_, op0=ALU.mult, op1=ALU.add),
                             reads=[gk_, SM, kcar], writes=[gk_])
                else:
                    self.dve(lambda e, gt_=gt_, in_=in_: e.tensor_tensor_scan(gt_[:, ls_][:, ::-1], mg_, gt_[:, ls_][:, ::-1], in_, op0=ALU.mult, op1=ALU.add),
                             reads=[gk_, SM, kcar], writes=[gk_])
                yield
            ec = Lb - 1 if d == 0 else 0
            if bi_ < nblk - 1:
                self.act(lambda e: e.activation(carry[:, 2 * d:2 * d + 1], gr[:, ec:ec + 1], AF.Copy), reads=[k4], writes=[kcar])
                self.act(lambda e: e.activation(carry[:, 2 * d + 1:2 * d + 2], gi[:, ec:ec + 1], AF.Copy), reads=[k5], writes=[kcar])
            elif not lat:
                self.act(lambda e: e.activation(fin[:, si, 0, col:col + 1], gr[:, ec:ec + 1], AF.Copy), reads=[k4], writes=['fin'])
                self.act(lambda e: e.activation(fin[:, si, 1, col:col + 1], gi[:, ec:ec + 1], AF.Copy), reads=[k5], writes=['fin'])
            self.dve(lambda e: e.tensor_tensor(A1[:, ls_], gr[:, ls_], cosT[:, ls_], op=ALU.mult), reads=[k4, k0], writes=[k2])
            self.pool(lambda e: e.tensor_tensor(A2[:, ls_], gi[:, ls_], cosT[:, ls_], op=ALU.mult), reads=[k5, k0], writes=[k3])
            yield
            self.dve(lambda e: e.tensor_tensor(gi[:, ls_], gi[:, ls_], sinT[:, ls_], op=ALU.mult), reads=[k5, k1, k3], writes=[k5])
            self.pool(lambda e: e.tensor_tensor(gr[:, ls_], gr[:, ls_], sinT[:, ls_], op=ALU.mult), reads=[k4, k1, k2], writes=[k4])
            yield
            self.dve(lambda e: e.tensor_tensor(hri[:, d * 2 + 0, cs_], A1[:, ls_], gi[:, ls_], op=ALU.subtract), reads=[k2, k5], writes=[khri])
            self.pool(lambda e: e.tensor_tensor(hri[:, d * 2 + 1, cs_], A2[:, ls_], gr[:, ls_], op=ALU.add), reads=[k3, k4], writes=[khri])

        self.bank_n = 6
        self.bank_rr = 0
        for ci in range(4):
            wu, wuk = _proj(self, l, OFF['bu'] + ci * 128, 128)
            bu_ = _proj_tile(self, wu, wuk, 0, NT)
            _evac(self, 'act', uf, 'T7', bu_)
            self.dve(lambda e: e.tensor_copy(ub[:, 0:NT], uf[:, 0:NT]), reads=['T7'], writes=['ub'])
            self.S.dma('pool', Bl[:], self.d_s5bl[l, ci], writes=['Bl'])
            cl = ys
            self.ld(cl[:], self.d_s5cl[l, ci], writes=['T6'])
            clv = cl[:].rearrange("p (d r s c) -> p d r s c", d=2, r=2, s=4)
            c12v = C12[:].rearrange("p (d r s c) -> p d r s c", d=2, r=2, s=4)
            tmpv = A2[:, 0:256].rearrange("p (s c) -> p s c", s=4)
            tmp2 = A2[:, 256:512].rearrange("p (s c) -> p s c", s=4)
            for d in range(2):
                frb = _bcast(sm[:, 10, d * 16 + ci * 4:d * 16 + ci * 4 + 4].unsqueeze(2), [128, 4, 64])
                fib = _bcast(sm[:, 11, d * 16 + ci * 4:d * 16 + ci * 4 + 4].unsqueeze(2), [128, 4, 64])
                self.dve(lambda e, d=d, frb=frb: e.tensor_tensor(tmpv, clv[:, d, 0], frb, op=ALU.mult), reads=['T6', SM], writes=['T3_0'])
                self.dve(lambda e, d=d, fib=fib: e.tensor_tensor(tmp2, clv[:, d, 1], fib, op=ALU.mult), reads=['T6', SM], writes=['T3_0'])
                self.dve(lambda e, d=d: e.tensor_tensor(c12v[:, d, 0], tmpv, tmp2, op=ALU.subtract), reads=['T3_0'], writes=['C12'])
                self.dve(lambda e, d=d, fib=fib: e.tensor_tensor(tmpv, clv[:, d, 0], fib, op=ALU.mult), reads=['T6', SM, 'C12'], writes=['T3_0'])
                self.dve(lambda e, d=d, frb=frb: e.tensor_tensor(tmp2, clv[:, d, 1], frb, op=ALU.mult), reads=['T6', SM], writes=['T3_0'])
                self.dve(lambda e, d=d: e.scalar_tensor_tensor(c12v[:, d, 1], tmpv, -1.0, tmp2, op0=ALU.mult, op1=ALU.subtract), reads=['T3_0'], writes=['C12'])
            blv = Bl[:].rearrange("p (d r j c) -> p d r j c", d=2, r=2, j=2)
            for hf in range(2):
                hs_ = slice(hf * 64, (hf + 1) * 64)
                accb = [6, 7][:ntb]
                for j in range(2):
                    stl = hf * 2 + j
                    st = ci * 4 + stl
                    calls = [[], []]
                    for d in range(2):
                        col = d * 16 + st
                        for si, (so, L) in enumerate(seqs):
                            nblk = max(1, L // 512)
                            Lb = L // nblk
                            order = list(range(nblk)) if d == 0 else list(range(nblk - 1, -1, -1))
                            for bi_, bk_ in enumerate(order):
                                calls[d].append((ci, hf, hs_, j, stl, st, d, col, si, so, L, nblk, Lb, bi_, bk_))
                    for c0_, c1_ in zip(calls[0], calls[1]):
                        gens = [unit(*c0_), unit(*c1_)]
                        while gens:
                            for g_ in list(gens):
                                try:
                                    next(g_)
                                except StopIteration:
                                    gens.remove(g_)
                    for tb in range(ntb):
                        b = accb[tb]
                        ts = slice(tb * 512, (tb + 1) * 512)
                        for d in range(2):
                            for r_ in range(2):
                                first = (j == 0 and d == 0 and r_ == 0)
                                last = (j == 1 and d == 1 and r_ == 1)
                                self.pe(lambda e, b=b, d=d, r_=r_, ts=ts, first=first, last=last: e.matmul(
                                    self.ps[b][0:64, :], c12v[:, d, r_, stl, :], hri[:, d * 2 + r_, ts], start=first, stop=last),
                                    reads=['C12', 'hri_0', 'hri_1'], excl=['ps%d' % b])
                for tb in range(ntb):
                    b = accb[tb]
                    ts = slice(tb * 512, (tb + 1) * 512)
                    self.dve(lambda e, b=b, ts=ts: e.tensor_copy(ys[hs_, ts], self.ps[b][0:64, :]), writes=['T6'], excl=['ps%d' % b])
            self.dve(lambda e: e.scalar_tensor_tensor(ys[:, 0:NT], uf[:, 0:NT], ppc(self, l, 's5d', ci), ys[:, 0:NT], op0=ALU.mult, op1=ALU.add),
                     reads=['T6', 'T7', 'pp'], writes=['T6'])
            _gelu_from(self, ys, 'T6', NT, uf, 'T7', ys, 'T6')
            self.act(lambda e, ci=ci: e.activation(self.yb[:, 4 + ci, 0:NT], ys[:, 0:NT], AF.Copy), reads=['T6'], writes=['yb'])
        self.bank_n = 8
        if not lat and self.cfg.get('skip_fin') != 1:
            for si in range(len(seqs)):
                gr_, gi_ = fin[:, si, 0, :], fin[:, si, 1, :]
                V(lambda e: e.tensor_tensor(t0, gr_, cL, op=ALU.mult), ['fin'])
                V(lambda e: e.tensor_tensor(t2, gi_, sL, op=ALU.mult), ['fin'])
                V(lambda e: e.tensor_tensor(t0, t0, t2, op=ALU.subtract))
                V(lambda e: e.tensor_tensor(t2, gi_, cL, op=ALU.mult), ['fin'])
                V(lambda e: e.tensor_tensor(t3, gr_, sL, op=ALU.mult), ['fin'])
                V(lambda e: e.tensor_tensor(t2, t2, t3, op=ALU.add))
                V(lambda e: e.tensor_tensor(h0r, fr, t0, op=ALU.mult))
                V(lambda e: e.tensor_tensor(t3, fi, t2, op=ALU.mult))
                V(lambda e: e.tensor_tensor(h0r, h0r, t3, op=ALU.subtract))
                V(lambda e: e.tensor_tensor(h0i, fr, t2, op=ALU.mult))
                V(lambda e: e.tensor_tensor(t3, fi, t0, op=ALU.mult))
                V(lambda e: e.tensor_tensor(h0i, h0i, t3, op=ALU.add))
                for r_, src in ((0, h0r), (1, h0i)):
                    self.dve(lambda e, r_=r_, src=src, si=si: e.tensor_copy(self.s5o[:, si, l, r_ * 32:(r_ + 1) * 32], src), reads=[SM], writes=['s5o'])
                self.S.barrier()
    self.S.barrier()
    for c in reversed(cms):
        c.__exit__(None, None, None)
    with self.sbc("sgl", [128, 4, 1024], BF16) as sgl:
        wgl, wglk = self.wload(self.d_glu[l], 4, 512)
        for co in range(4):
            for tb in range(ntb):
                b = self.bank()
                ts = slice(tb * 512, (tb + 1) * 512)
                for k in range(4):
                    self.pe(lambda e, b=b, k=k, ts=ts: e.matmul(self.ps[b][:], wgl[:, k, co * 128:(co + 1) * 128], self.yb[:, 4 + k, ts], start=(k == 0), stop=(k == 3)),
                            reads=[wglk, 'yb'], excl=['ps%d' % b])
                self.act(lambda e, b=b, ts=ts: e.activation(sgl[:, co, ts], self.ps[b][:], AF.Sigmoid, bias=ppc(self, l, 's5gb', co)),
                         reads=['pp'], writes=['sgl'], excl=['ps%d' % b])
        self.S.barrier()
        for co in range(4):
            self.dve(lambda e, co=co: e.tensor_tensor(self.yb[:, 4 + co, 0:NT], self.yb[:, 4 + co, 0:NT], sgl[:, co, 0:NT], op=ALU.mult),
                     reads=['yb', 'sgl'], writes=['yb'])
    self.S.barrier()


def _mix_C(self, grp, l):
    NT, seqs, lat = grp['NT'], grp['seqs'], grp['lat']
    ntb = NT // 512
    ntt = NT // 128
    specs = [("qk", [128, 6, 1024], BF16), ("vP", [128, 8, 4, 128], BF16), ("vcP", [128, 2, 4, 128], BF16), ("kcT", [128, 2, 256], BF16),
             ("es", [128, 8], F32), ("msk", [128, 2, 128], BF16)]
    cms = [self.sbc(n, sh, dt) for (n, sh, dt) in specs]
    qk, vP, vcP, kcT, es, msk = [c.__enter__() for c in cms]
    self.act(lambda e: e.activation(es[:], ppc(self, l, 'sink', 0, 8), AF.Exp), reads=['pp'], writes=['es'])
    self.S.dma('pool', msk[:], self.d_amask, writes=['msk'])
    self.pool(lambda e: e.memset(vP[:], 0.0), writes=['vP'])
    if lat:
        self.pool(lambda e: e.memset(vcP[:], 0.0), writes=['vcP'])
        self.S.dma('pool', kcT[:], self.d_kcT[l], writes=['kcT'])
    specs1 = [("ropc", [128, 1024], F32), ("rops", [128, 1024], F32), ("qf", [128, 1024], F32), ("rt", [128, 1024], F32), ("kvf0", [128, 256], F32), ("kvf1", [128, 256], F32)]
    cms1 = [self.sbc(n, sh, dt) for (n, sh, dt) in specs1]
    ropc, rops, qf, rt, kvf0, kvf1 = [c.__enter__() for c in cms1]
    kvf = [kvf0, kvf1]
    if lat:
        self.ld(ropc[:], self.d_rope[0], writes=['ropc'])
        self.ld(rops[:], self.d_rope[1], writes=['rops'])
        with self.sbc("vcl", [128, 2, 128], BF16) as vcl:
            self.S.dma('pool', vcl[:], self.d_vc[l].rearrange("(t p) c -> p t c", p=128), writes=['vcl'])
            for t in range(2):
                for g in range(2):
                    for pos in range(2):
                        self.dve(lambda e, t=t, g=g, pos=pos: e.tensor_copy(vcP[:, t, g * 2 + pos, pos * 64:(pos + 1) * 64], vcl[:, t, g * 64:(g + 1) * 64]),
                                 reads=['vcl'], writes=['vcP'])
            self.S.barrier()
    wkv, wkvk = _proj(self, l, OFF['ck'], 256)
    for tt in range(ntt):
        b = self.bank()
        for kt in range(16):
            self.pe(lambda e, b=b, kt=kt, tt=tt: e.matmul(self.ps[b][:, 0:256], self.h[:, kt, tt * 128:(tt + 1) * 128], wkv[:, kt, :], start=(kt == 0), stop=(kt == 15)),
                    reads=[wkvk, 'h'], excl=['ps%d' % b])
        for g in range(2):
            for pos in range(2):
                self.dve(lambda e, b=b, tt=tt, g=g, pos=pos: e.tensor_copy(vP[:, tt, g * 2 + pos, pos * 64:(pos + 1) * 64], self.ps[b][:, 128 + g * 64:128 + (g + 1) * 64]),
                         writes=['vP'], excl=['ps%d' % b])
        if not lat:
            kf = kvf[tt % 2]
            kk_ = 'kvf%d' % (tt % 2)
            self.act(lambda e, b=b, kf=kf: e.activation(kf[:], self.ps[b][:, 0:256], AF.Copy), writes=[kk_], excl=['ps%d' % b])
            si = (tt * 128) // 256
            t0 = (tt * 128) % 256
            self.st(self.o_k[si, l, t0:t0 + 128, :], kf[:, 0:128], reads=[kk_])
            self.st(self.o_v[si, l, t0:t0 + 128, :], kf[:, 128:256], reads=[kk_])
    for ti_ in range(6):
        if ti_ < 4:
            wq, wqk = _proj(self, l, OFF['cq'] + ti_ * 128, 128)
        else:
            wq, wqk = self.wload(self.d_wkdup[l, ti_ - 4], 16, 128)
        bq = _proj_tile(self, wq, wqk, 0, NT)
        if not lat:
            _evac(self, 'act', qk[:, ti_, :], 'qk', bq)
        else:
            _evac(self, 'act', qf, 'qf', bq)
            for tb in range(ntb):
                ts = slice(tb * 512, (tb + 1) * 512)
                b = self.bank()
                self.pe(lambda e, b=b, ts=ts: e.matmul(self.ps[b][:], self.cf[:, 2, :], qf[:, ts], start=True, stop=True), reads=['cf', 'qf'], excl=['ps%d' % b])
                self.dve(lambda e, b=b, ts=ts: e.tensor_tensor(rt[:, ts], self.ps[b][:], rops[:, ts], op=ALU.mult), reads=['rops'], writes=['rt'], excl=['ps%d' % b])
            self.pool(lambda e: e.tensor_tensor(qf[:, 0:NT], qf[:, 0:NT], ropc[:, 0:NT], op=ALU.mult), reads=['qf', 'ropc'], writes=['qf'])
            self.dve(lambda e, ti_=ti_: e.tensor_tensor(qk[:, ti_, 0:NT], qf[:, 0:NT], rt[:, 0:NT], op=ALU.add), reads=['qf', 'rt'], writes=['qk'])
    self.S.barrier()
    for c in reversed(cms1):
        c.__exit__(None, None, None)
    if self.cfg.get('dump_qk'):
        self.dump('qk', qk[:, :, 0:NT], 'qk', [128, 6, NT])
    specs2 = [("pt0", [128, 512], BF16), ("pt1", [128, 512], BF16), ("pt2", [128, 512], BF16), ("rden", [128, 512], F32), ("oneP", [128, 2, 128], BF16)]
    cms2 = [self.sbc(n, sh, dt) for (n, sh, dt) in specs2]
    pt0, pt1, pt2, rden, oneP = [c.__enter__() for c in cms2]
    pts = [pt0, pt1, pt2]
    self.pool(lambda e: e.memset(oneP[:], 0.0), writes=['oneP'])
    self.pool(lambda e: e.memset(oneP[:, 0, 0:64], 1.0), writes=['oneP'])
    self.pool(lambda e: e.memset(oneP[:, 1, 64:128], 1.0), writes=['oneP'])
    self.bank_n = 6
    self.bank_rr = 0
    pt_rr = [0]

    def head_chunk(h, so, L, c0, ncol):
        g = h // 4
        hp = h % 2
        hs_ = slice(hp * 64, (hp + 1) * 64)
        qt = h // 2
        cols = slice(so + c0, so + c0 + ncol)
        qb0 = c0 // 128
        nqb = ncol // 128
        keys = []
        if lat:
            keys += [('ctx', 0), ('ctx', 1)]
            for i in range(max(0, qb0 - 1), min(L // 128 - 1, qb0 + nqb) + 1):
                keys.append(('lat', i))
        else:
            for i in range(L // 128):
                keys.append(('own', i))
        for ki, (kind, i) in enumerate(keys):
            b = self.bank()
            if kind == 'ctx':
                lhs = kcT[hs_, g, i * 128:(i + 1) * 128]
                lk = 'kcT'
                vl = vcP[:, i, g * 2 + hp, :]
                vk = 'vcP'
            else:
                lhs = qk[hs_, 4 + g, so + i * 128:so + (i + 1) * 128]
                lk = 'qk'
                vl = vP[:, (so // 128) + i, g * 2 + hp, :]
                vk = 'vP'
            if kind == 'lat':
                jlo = max(0, i - 1 - qb0)
                jhi = min(nqb - 1, i + 1 - qb0)
            else:
                jlo, jhi = 0, nqb - 1
            cl, ch = jlo * 128, (jhi + 1) * 128
            qcols = slice(so + c0 + cl, so + c0 + ch)
            self.pe(lambda e, b=b, lhs=lhs: e.matmul(self.ps[b][:, cl:ch], lhs, qk[hs_, qt, qcols], start=True, stop=True), reads=[lk, 'qk'], excl=['ps%d' % b])
            pi = pt_rr[0] % 3
            pt_rr[0] += 1
            pt = pts[pi]
            pk = 'pt%d' % pi
            self.act(lambda e, b=b, pt=pt: e.activation(pt[:, cl:ch], self.ps[b][:, cl:ch], AF.Exp, scale=0.125), writes=[pk], excl=['ps%d' % b])
            if kind == 'lat':
                for jq in range(jlo, jhi + 1):
                    qb = qb0 + jq
                    blk = pt[:, jq * 128:(jq + 1) * 128]
                    if qb == i + 1:
                        self.dve(lambda e, blk=blk: e.tensor_tensor(blk, blk, msk[:, 0, :], op=ALU.mult), reads=[pk, 'msk'], writes=[pk])
                    elif qb == i - 1:
                        self.dve(lambda e, blk=blk: e.tensor_tensor(blk, blk, msk[:, 1, :], op=ALU.mult), reads=[pk, 'msk'], writes=[pk])
            first = (ki == 0)
            last = (ki == len(keys) - 1)
            self.pe(lambda e, vl=vl, pt=pt, first=first, last=last: e.matmul(self.ps[6][:, cl:ch], vl, pt[:, cl:ch], start=first, stop=last, skip_group_check=True),
                    reads=[vk, pk], excl=['ps6'])
            self.pe(lambda e, pt=pt, first=first, last=last: e.matmul(self.ps[7][:, cl:ch], oneP[:, hp, :], pt[:, cl:ch], start=first, stop=last, skip_group_check=True),
                    reads=['oneP', pk], excl=['ps7'])
        self.dve(lambda e: e.tensor_scalar(rden[hs_, 0:ncol], self.ps[7][hs_, 0:ncol], es[hs_, h:h + 1], None, op0=ALU.add), reads=['es'], writes=['rden'], excl=['ps7'])
        self.dve(lambda e: e.reciprocal(rden[hs_, 0:ncol], rden[hs_, 0:ncol]), reads=['rden'], writes=['rden'])
        self.dve(lambda e: e.tensor_tensor(self.yb[hs_, 8 + qt, cols], self.ps[6][hs_, 0:ncol], rden[hs_, 0:ncol], op=ALU.mult), reads=['rden'], writes=['yb'], excl=['ps6'])

    for (so, L) in seqs:
        csz = min(512, L)
        for c0 in range(0, L, csz):
            for h in range(8):
                head_chunk(h, so, L, c0, csz)
    self.bank_n = 8
    self.S.barrier()
    for c in reversed(cms2):
        c.__exit__(None, None, None)
    for c in reversed(cms):
        c.__exit__(None, None, None)


def _mix_D(self, grp, l):
    NT, seqs, lat = grp['NT'], grp['seqs'], grp['lat']
    ntb = NT // 512
    nch_all = NT // 128
    X = [self.x[:, k, :] for k in range(16)]
    XK = ['X%d' % k for k in range(16)]
    bonus, vtokf, yacc = X[0], X[1], X[2]
    vtok = vtokf.rearrange("p (c n) -> p c n", n=128)
    til = [[X[3 + 4 * d + q] for q in range(4)] for d in range(2)]
    tilk = [[XK[3 + 4 * d + q] for q in range(4)] for d in range(2)]
    r_, kh_, vh_, kap, tmpx = X[11], X[12], X[13], X[14], X[15]
    specs = [("av", [128, 1024], F32), ("sg_", [128, 1024], F32), ("css", [128, 1024], F32),
             ("Nm", [128, 4, 128], F32), ("NTm", [128, 4, 128], F32), ("TTm", [128, 4, 128], F32), ("TTb", [128, 4, 128], BF16), ("BTm", [128, 4, 128], F32),
             ("M1m", [128, 4, 128], F32), ("M2m", [128, 4, 128], F32), ("tokm", [128, 4, 128], F32),
             ("negR", [128, 256], F32), ("Um", [128, 256], F32), ("Sst", [128, 2, 2, 128], F32), ("Sdec", [128, 2, 128], F32),
             ("mk", [128, 4, 512], BF16), ("twl", [128, 1024], BF16), ("sgl_", [128, 1024], BF16), ("lamC", [128, 2, 8], F32),
             ("lw", [128, 5, 128], BF16), ("sm_", [128, 16], F32), ("onesf", [128, 128], F32), ("sfin", [128, 128], F32)]
    cms = [self.sbc(n, sh, dt) for (n, sh, dt) in specs]
    (av, sg_, css, Nm, NTm, TTm, TTb, BTm, M1m, M2m, tokm, negR, Um, Sst, Sdec, mk, twl, sgl_, lamC, lw, sm_, onesf, sfin) = [c.__enter__() for c in cms]
    self.S.dma('pool', mk[:], self.d_dmask, writes=['mk'])
    self.pool(lambda e: e.memset(onesf[:], 1.0), writes=['onesf'])
    mu = ppc(self, l, 'mu', 0, 12)
    self.dve(lambda e: e.tensor_scalar(sm_[:, 0:12], mu, -1.0, 1.0, op0=ALU.mult, op1=ALU.add), reads=['pp'], writes=['sm_'])
    self.dve(lambda e: e.tensor_scalar(sm_[:, 12:16], ppc(self, l, 'ka', 0, 4), -1.0, 1.0, op0=ALU.mult, op1=ALU.add), reads=['pp'], writes=['sm_'])
    wl_, wlk = _proj(self, l, OFF['dwl'], 256)
    b0 = _proj_tile(self, wl_, wlk, 0, NT)
    for tb, b in enumerate(b0):
        ts = slice(tb * 512, (tb + 1) * 512)
        self.act(lambda e, b=b, ts=ts: e.activation(twl[0:64, ts], self.ps[b][0:64, :], AF.Tanh), writes=['twl'], excl=['ps%d' % b])
        self.act(lambda e, b=b, ts=ts: e.activation(twl[64:128, ts], self.ps[b][64:128, :], AF.Copy), writes=['twl'], excl=['ps%d' % b])
    b1 = _proj_tile(self, wl_, wlk, 1, NT)
    _evac(self, 'act', sgl_, 'sgl_', b1, AF.Sigmoid)
    C_W = -0.6065306597126334

    def stopat(k):
        if self.cfg.get('dstop') == k:
            raise _Stop()

    def tiles():
      for i in range(4):
            for (nm, dst, dk_, mi) in (('dr', r_, 'X11', 0), ('dk', kh_, 'X12', 1), ('dv', vh_, 'X13', 2)):
                w_, wk_ = _proj(self, l, OFF[nm] + i * 128, 128)
                bb = _proj_tile(self, w_, wk_, 0, NT)
                _evac(self, 'act', tmpx, 'X15', bb)
                omm = sm_[:, mi * 4 + i:mi * 4 + i + 1]
                mu_c = self.pp[:, l, PPC['mu'][0] + mi * 4 + i:PPC['mu'][0] + mi * 4 + i + 1]
                self.dve(lambda e, dst=dst, omm=omm: e.tensor_scalar(dst[:, 0:NT], tmpx[:, 0:NT], omm, None, op0=ALU.mult), reads=['X15', 'sm_'], writes=[dk_])
                self.dve(lambda e, mu_c=mu_c: e.tensor_scalar(tmpx[:, 0:NT], tmpx[:, 0:NT], mu_c, 0.5, op0=ALU.mult, op1=ALU.mult), reads=['X15', 'pp', dk_], writes=['X15'])
                for (o, L) in seqs:
                    self.dve(lambda e, dst=dst, o=o, L=L: e.tensor_tensor(dst[:, o + 1:o + L], dst[:, o + 1:o + L], tmpx[:, o:o + L - 1], op=ALU.add), reads=['X15', dk_], writes=[dk_])
                    self.dve(lambda e, dst=dst, o=o, L=L: e.tensor_tensor(dst[:, o:o + L - 1], dst[:, o:o + L - 1], tmpx[:, o + 1:o + L], op=ALU.add), reads=['X15', dk_], writes=[dk_])
            self.dve(lambda e: e.tensor_scalar(kap[:, 0:NT], kh_[:, 0:NT], ppc(self, l, 'kk', i), None, op0=ALU.mult), reads=['X12', 'pp'], writes=['X14'])
            self.pool(lambda e: e.tensor_tensor(tmpx[:, 0:NT], kap[:, 0:NT], kap[:, 0:NT], op=ALU.mult), reads=['X14'], writes=['X15'])
            for tb in range(ntb):
                ts = slice(tb * 512, (tb + 1) * 512)
                b = self.bank()
                self.pe(lambda e, b=b, ts=ts: e.matmul(self.ps[b][:], self.bones, tmpx[:, ts], start=True, stop=True), reads=['cf', 'X15'], excl=['ps%d' % b])
                self.act(lambda e, b=b, ts=ts: e.activation(css[:, ts], self.ps[b][:], AF.Sqrt, bias=self.cst[:, 3:4]), reads=['cst'], writes=['css'], excl=['ps%d' % b])
            self.dve(lambda e: e.reciprocal(css[:, 0:NT], css[:, 0:NT]), reads=['css'], writes=['css'])
            self.dve(lambda e: e.tensor_tensor(kap[:, 0:NT], kap[:, 0:NT], css[:, 0:NT], op=ALU.mult), reads=['css', 'X14'], writes=['X14'])
            self.dve(lambda e: e.scalar_tensor_tensor(tmpx[:, 0:NT], r_[:, 0:NT], ppc(self, l, 'rk', i), kh_[:, 0:NT], op0=ALU.mult, op1=ALU.mult),
                     reads=['X11', 'X12', 'pp', 'X15'], writes=['X15'])
            for tb in range(ntb):
                ts = slice(tb * 512, (tb + 1) * 512)
                b = self.bank()
                self.pe(lambda e, b=b, ts=ts: e.matmul(self.ps[b][:], self.bones, tmpx[:, ts], start=True, stop=True), reads=['cf', 'X15'], excl=['ps%d' % b])
                self.dve(lambda e, b=b, ts=ts: e.tensor_tensor(bonus[:, ts], self.ps[b][:], vh_[:, ts], op=ALU.mult), reads=['X13'], writes=['X0'], excl=['ps%d' % b])
            for c4 in range(0, nch_all, 4):
                b = self.bank()
                n4 = min(4, nch_all - c4)
                for q in range(n4):
                    c = c4 + q
                    self.pe(lambda e, b=b, q=q, c=c: e.transpose(self.ps[b][:, q * 128:(q + 1) * 128], vh_[:, c * 128:(c + 1) * 128], self.ident), reads=['X13', 'cf'], excl=['ps%d' % b])
                self.act(lambda e, b=b, c4=c4, n4=n4: e.activation(vtokf[:, c4 * 128:(c4 + n4) * 128], self.ps[b][:, 0:n4 * 128], AF.Copy), writes=['X1'], excl=['ps%d' % b])
            self.pool(lambda e: e.memset(yacc[:, 0:NT], 0.0), writes=['X2'])
            if self.cfg.get('dstop') == 1:
                break
            self.S.dma('pool', lw[:], self.d_lora[l, i], writes=['lw'])
            for d in range(2):
                kt_, rt_, bh_, kh2_ = til[d]
                ktk, rtk, bhk, khk = tilk[d]
                for tb in range(ntb):
                    ts = slice(tb * 512, (tb + 1) * 512)
                    b = self.bank()
                    self.pe(lambda e, b=b, ts=ts, d=d: e.matmul(self.ps[b][:], lw[0:64, d, :], twl[0:64, ts], start=True, stop=True), reads=['lw', 'twl'], excl=['ps%d' % b])
                    self.act(lambda e, b=b, ts=ts, d=d: e.activation(sg_[:, ts], self.ps[b][:], AF.Sigmoid, bias=ppc(self, l, 'w0', d * 4 + i)),
                             reads=['pp'], writes=['sg_'], excl=['ps%d' % b])
                    b = self.bank()
                    self.pe(lambda e, b=b, ts=ts, d=d: e.matmul(self.ps[b][:], lw[64:128, 2 + d, :], twl[64:128, ts], start=True, stop=True), reads=['lw', 'twl'], excl=['ps%d' % b])
                    self.act(lambda e, b=b, ts=ts, d=d: e.activation(av[:, ts], self.ps[b][:], AF.Sigmoid, bias=ppc(self, l, 'a0', d * 4 + i)),
                             reads=['pp'], writes=['av'], excl=['ps%d' % b])
                for c in range(nch_all):
                    cs_ = slice(c * 128, (c + 1) * 128)
                    if d == 0:
                        self.dve(lambda e, cs_=cs_: e.tensor_tensor_scan(css[:, cs_], onesf[:], sg_[:, cs_], 0.0, op0=ALU.mult, op1=ALU.add), reads=['sg_', 'onesf'], writes=['css'])
                    else:
                        self.dve(lambda e, cs_=cs_: e.tensor_tensor_scan(css[:, cs_][:, ::-1], onesf[:], sg_[:, cs_][:, ::-1], 0.0, op0=ALU.mult, op1=ALU.add), reads=['sg_', 'onesf'], writes=['css'])
                self.act(lambda e: e.activation(tmpx[:, 0:NT], css[:, 0:NT], AF.Exp, scale=C_W), reads=['css'], writes=['X15'])
                ec = 127 if d == 0 else 0
                self.dve(lambda e, d=d, ec=ec: e.tensor_copy(lamC[:, d, 0:nch_all], tmpx[:, 0:NT].rearrange("p (c n) -> p c n", n=128)[:, :, ec]), reads=['X15'], writes=['lamC'])
                self.dve(lambda e: e.tensor_tensor(rt_[:, 0:NT], r_[:, 0:NT], tmpx[:, 0:NT], op=ALU.mult), reads=['X11', 'X15'], writes=[rtk])
                self.pool(lambda e: e.tensor_tensor(sg_[:, 0:NT], css[:, 0:NT], sg_[:, 0:NT], op=ALU.subtract), reads=['css', 'sg_'], writes=['sg_'])
                self.act(lambda e: e.activation(sg_[:, 0:NT], sg_[:, 0:NT], AF.Exp, scale=C_W), reads=['sg_'], writes=['sg_'])
                self.dve(lambda e: e.tensor_tensor(kt_[:, 0:NT], kap[:, 0:NT], sg_[:, 0:NT], op=ALU.mult), reads=['X14', 'sg_'], writes=[ktk])
                self.act(lambda e: e.activation(css[:, 0:NT], css[:, 0:NT], AF.Exp, scale=-C_W), reads=['css'], writes=['css'])
                self.dve(lambda e: e.tensor_tensor(bh_[:, 0:NT], av[:, 0:NT], kap[:, 0:NT], op=ALU.mult), reads=['av', 'X14'], writes=[bhk])
                self.pool(lambda e: e.tensor_tensor(bh_[:, 0:NT], bh_[:, 0:NT], css[:, 0:NT], op=ALU.mult), reads=[bhk, 'css'], writes=[bhk])
                self.dve(lambda e: e.tensor_scalar(av[:, 0:NT], av[:, 0:NT], ppc(self, l, 'ka', i), sm_[:, 12 + i:13 + i], op0=ALU.mult, op1=ALU.add), reads=['av', 'pp', 'sm_', bhk], writes=['av'])
                self.dve(lambda e: e.tensor_tensor(av[:, 0:NT], av[:, 0:NT], kh_[:, 0:NT], op=ALU.mult), reads=['av', 'X12'], writes=['av'])
                self.dve(lambda e: e.tensor_tensor(kh2_[:, 0:NT], av[:, 0:NT], css[:, 0:NT], op=ALU.mult), reads=['av', 'css'], writes=[khk])
            if self.cfg.get('dstop') == 2:
                break
            for si, (so, L) in enumerate(seqs):
                nch = L // 128
                c_base = so // 128
                for d in range(2):
                    if lat:
                        self.ld(Sst[:, d, 0, :], self.d_latwkv[l, d, i], writes=['Sst'])
                    else:
                        self.pool(lambda e, d=d: e.memset(Sst[:, d, 0, :], 0.0), writes=['Sst'])
                for step in range(nch):
                    cur = step % 2
                    nxt = 1 - cur
                    cc = [c_base + step, c_base + nch - 1 - step]
                    csl = [slice(c * 128, (c + 1) * 128) for c in cc]
                    b = self.bank()
                    for d in range(2):
                        for q, (src, sk) in enumerate(((til[d][2], tilk[d][2]), (til[d][3], tilk[d][3]))):
                            self.pe(lambda e, b=b, d=d, q=q, src=src: e.transpose(self.ps[b][:, (d * 2 + q) * 128:(d * 2 + q + 1) * 128], src[:, csl[d]], self.ident),
                                    reads=[sk, 'cf'], excl=['ps%d' % b])
                    self.act(lambda e, b=b: e.activation(tokm[:].rearrange("p u n -> p (u n)"), self.ps[b][:], AF.Copy), writes=['tokm'], excl=['ps%d' % b])
                    stopat(31)
                    kinds = [(NTm, 'NTm', 2, 0, 0), (Nm, 'Nm', 0, 2, 1), (BTm, 'BTm', 3, 0, 0), (M1m, 'M1m', 2, 1, 2), (M2m, 'M2m', 3, 1, 2)]
                    for (dst, dkey, li, ri, mi) in kinds:
                        msel = {('NTm'): 0, ('Nm'): 1, ('BTm'): 2, ('M1m'): 3, ('M2m'): 3}[dkey]
                        for hh in range(2):
                            b = self.bank()
                            hs_ = slice(hh * 64, (hh + 1) * 64)
                            for d in range(2):
                                self.pe(lambda e, b=b, d=d, hs_=hs_, li=li, ri=ri: e.matmul(self.ps[b][:, d * 128:(d + 1) * 128], til[d][li][hs_, csl[d]], til[d][ri][hs_, csl[d]],
                                                                                         start=True, stop=True), reads=[tilk[d][li], tilk[d][ri]], excl=['ps%d' % b])
                            dv_ = dst[:].rearrange("p (d h) n -> p d h n", h=2)[:, :, hh, :]
                            mv_ = mk[:, msel, :].rearrange("p (d h n) -> p d h n", d=2, h=2)[:, :, hh, :]
                            pv_ = self.ps[b][:, 0:256].rearrange("p (d n) -> p d n", n=128)
                            self.dve(lambda e, dv_=dv_, mv_=mv_, pv_=pv_: e.tensor_tensor(dv_, pv_, mv_, op=ALU.mult), reads=['mk'], writes=[dkey], excl=['ps%d' % b])
                    stopat(32)
                    for u in range(4):
                        self.pool(lambda e, u=u: e.tensor_tensor(TTm[:, u, :], NTm[:, u, :], self.ident, op=ALU.add), reads=['NTm', 'cf'], writes=['TTm'])
                    for lev in range(1, 7):
                        bX = self.bank()
                        for u in range(4):
                            self.pe(lambda e, bX=bX, u=u: e.matmul(self.ps[bX][:, u * 128:(u + 1) * 128], NTm[:, u, :], Nm[:, u, :], start=True, stop=True), reads=['NTm', 'Nm'], excl=['ps%d' % bX])
                        if lev < 6:
                            bY = self.bank()
                            for u in range(4):
                                self.pe(lambda e, bY=bY, u=u: e.matmul(self.ps[bY][:, u * 128:(u + 1) * 128], Nm[:, u, :], NTm[:, u, :], start=True, stop=True), reads=['NTm', 'Nm'], excl=['ps%d' % bY])
                        self.act(lambda e, bX=bX: e.activation(Nm[:].rearrange("p u n -> p (u n)"), self.ps[bX][:], AF.Copy), writes=['Nm'], excl=['ps%d' % bX])
                        if lev < 6:
                            self.dve(lambda e, bY=bY: e.tensor_copy(NTm[:].rearrange("p u n -> p (u n)"), self.ps[bY][:]), writes=['NTm'], excl=['ps%d' % bY])
                        bZ = self.bank()
                        for u in range(4):
                            self.pe(lambda e, bZ=bZ, u=u: e.matmul(self.ps[bZ][:, u * 128:(u + 1) * 128], Nm[:, u, :], TTm[:, u, :], start=True, stop=True), reads=['Nm', 'TTm'], excl=['ps%d' % bZ])
                        self.dve(lambda e, bZ=bZ: e.tensor_tensor(TTm[:].rearrange("p u n -> p (u n)"), TTm[:].rearrange("p u n -> p (u n)"), self.ps[bZ][:], op=ALU.add),
                                 reads=['TTm'], writes=['TTm'], excl=['ps%d' % bZ])
                        if lev < 6:
                            stopat(33)
                    bR = self.bank()
                    for d in range(2):
                        for hh in range(2):
                            u = d * 2 + hh
                            hs_ = slice(hh * 64, (hh + 1) * 64)
                            self.pe(lambda e, u=u, d=d, hs_=hs_: e.matmul(self.ps[bR][:, u * 64:(u + 1) * 64], til[d][0][hs_, csl[d]], Sst[hs_, d, cur, hs_], start=True, stop=False),
                                    reads=[tilk[d][0], 'Sst'], excl=['ps%d' % bR])
                            self.pe(lambda e, u=u, d=d, hs_=hs_: e.matmul(self.ps[bR][:, u * 64:(u + 1) * 64], BTm[:, u, :], vtok[:, cc[d], hs_], start=False, stop=True),
                                    reads=['BTm', 'X1'], excl=['ps%d' % bR])
                    self.act(lambda e: e.activation(negR[:], self.ps[bR][:, 0:256], AF.Copy, scale=-1.0), writes=['negR'], excl=['ps%d' % bR])
                    bU = self.bank()
                    for u in range(4):
                        self.pe(lambda e, u=u: e.matmul(self.ps[bU][:, u * 64:(u + 1) * 64], TTm[:, u, :], negR[:, u * 64:(u + 1) * 64], start=True, stop=True), reads=['TTm', 'negR'], excl=['ps%d' % bU])
                    self.act(lambda e: e.activation(Um[:], self.ps[bU][:, 0:256], AF.Copy), writes=['Um'], excl=['ps%d' % bU])
                    stopat(34)
                    bY2 = self.bank()
                    for d in range(2):
                        for hh in range(2):
                            u = d * 2 + hh
                            hs_ = slice(hh * 64, (hh + 1) * 64)
                            self.pe(lambda e, u=u, d=d, hs_=hs_: e.matmul(self.ps[bY2][:, u * 128:(u + 1) * 128], Sst[hs_, d, cur, :], til[d][1][hs_, csl[d]], start=True, stop=False),
                                    reads=['Sst', tilk[d][1]], excl=['ps%d' % bY2])
                            self.pe(lambda e, u=u, d=d: e.matmul(self.ps[bY2][:, u * 128:(u + 1) * 128], Um[:, d * 128:(d + 1) * 128], M1m[:, u, :], start=False, stop=False),
                                    reads=['Um', 'M1m'], excl=['ps%d' % bY2])
                            self.pe(lambda e, u=u, d=d: e.matmul(self.ps[bY2][:, u * 128:(u + 1) * 128], vtok[:, cc[d], :], M2m[:, u, :], start=False, stop=True),
                                    reads=['X1', 'M2m'], excl=['ps%d' % bY2])
                    for d in range(2):
                        for hh in range(2):
                            u = d * 2 + hh
                            hs_ = slice(hh * 64, (hh + 1) * 64)
                            self.dve(lambda e, u=u, d=d, hs_=hs_: e.tensor_tensor(yacc[hs_, csl[d]], yacc[hs_, csl[d]], self.ps[bY2][hs_, u * 128:(u + 1) * 128], op=ALU.add),
                                     reads=['X2'], writes=['X2'], excl=['ps%d' % bY2])
                    stopat(35)
                    bS = self.bank()
                    for d in range(2):
                        self.pe(lambda e, d=d: e.matmul(self.ps[bS][:, d * 128:(d + 1) * 128], tokm[:, d * 2 + 0, :], Um[:, d * 128:(d + 1) * 128], start=True, stop=False),
                                reads=['tokm', 'Um'], excl=['ps%d' % bS])
                        self.pe(lambda e, d=d: e.matmul(self.ps[bS][:, d * 128:(d + 1) * 128], tokm[:, d * 2 + 1, :], vtok[:, cc[d], :], start=False, stop=True),
                                reads=['tokm', 'X1'], excl=['ps%d' % bS])
                    for d in range(2):
                        lam = lamC[:, d, cc[d]:cc[d] + 1]
                        self.dve(lambda e, d=d, lam=lam: e.tensor_scalar(Sdec[:, d, :], Sst[:, d, cur, :], lam, None, op0=ALU.mult), reads=['Sst', 'lamC'], writes=['Sdec'])
                        self.dve(lambda e, d=d, lam=lam: e.scalar_tensor_tensor(Sst[:, d, nxt, :], self.ps[bS][:, d * 128:(d + 1) * 128], lam, Sdec[:, d, :], op0=ALU.mult, op1=ALU.add),
                                 reads=['Sdec', 'lamC'], writes=['Sst'], excl=['ps%d' % bS])
                if not lat:
                    fin_i = nch % 2
                    for d in range(2):
                        b = self.bank()
                        self.pe(lambda e, b=b, d=d: e.transpose(self.ps[b][:, 0:128], Sst[:, d, fin_i, :], self.ident), reads=['Sst', 'cf'], excl=['ps%d' % b])
                        self.act(lambda e, b=b: e.activation(sfin[:], self.ps[b][:, 0:128], AF.Copy), writes=['sfin'], excl=['ps%d' % b])
                        for hh in range(2):
                            hs_ = slice(hh * 64, (hh + 1) * 64)
                            self.st(self.o_wkv[si, l, d, 2 * i + hh], sfin[hs_, hs_], reads=['sfin'])
            if self.cfg.get('dstop') == 3:
                break
            for tb in range(ntb):
                ts = slice(tb * 512, (tb + 1) * 512)
                b = self.bank()
                self.pe(lambda e, b=b, ts=ts: e.matmul(self.ps[b][:], self.bones, yacc[:, ts], start=True, stop=True), reads=['cf', 'X2'], excl=['ps%d' % b])
                self.dve(lambda e, b=b, ts=ts: e.scalar_tensor_tensor(yacc[:, ts], self.ps[b][:], -1.0 / 64.0, yacc[:, ts], op0=ALU.mult, op1=ALU.add), reads=['X2'], writes=['X2'], excl=['ps%d' % b])
                self.act(lambda e, ts=ts: e.activation(tmpx[:, ts], yacc[:, ts], AF.Square), reads=['X2'], writes=['X15'])
                b = self.bank()
                self.pe(lambda e, b=b, ts=ts: e.matmul(self.ps[b][:], self.bones, tmpx[:, ts], start=True, stop=True), reads=['cf', 'X15'], excl=['ps%d' % b])
                self.act(lambda e, b=b, ts=ts: e.activation(css[:, ts], self.ps[b][:], AF.Sqrt, bias=self.cst[:, 2:3], scale=1.0 / 64.0), reads=['cst'], writes=['css'], excl=['ps%d' % b])
                self.dve(lambda e, ts=ts: e.reciprocal(css[:, ts], css[:, ts]), reads=['css'], writes=['css'])
                self.dve(lambda e, ts=ts: e.tensor_tensor(yacc[:, ts], yacc[:, ts], css[:, ts], op=ALU.mult), reads=['X2', 'css'], writes=['X2'])
                self.act(lambda e, ts=ts: e.activation(yacc[:, ts], yacc[:, ts], AF.Identity, bias=ppc(self, l, 'lnb', i), scale=ppc(self, l, 'lnw', i)), reads=['X2', 'pp'], writes=['X2'])
                self.dve(lambda e, ts=ts: e.tensor_tensor(yacc[:, ts], yacc[:, ts], bonus[:, ts], op=ALU.add), reads=['X2', 'X0'], writes=['X2'])
                b = self.bank()
                self.pe(lambda e, b=b, ts=ts: e.matmul(self.ps[b][:], lw[:, 4, :], sgl_[:, ts], start=True, stop=True), reads=['lw', 'sgl_'], excl=['ps%d' % b])
                self.dve(lambda e, b=b, ts=ts: e.tensor_tensor(self.yb[:, 12 + i, ts], yacc[:, ts], self.ps[b][:], op=ALU.mult), reads=['X2'], writes=['yb'], excl=['ps%d' % b])
            self.S.barrier()

    try:
        tiles()
    except _Stop:
        pass
    self.S.barrier()
    for c in reversed(cms):
        c.__exit__(None, None, None)


def kernel(**inputs):
    inp = {k: np.asarray(v) for k, v in inputs.items()}
    ncores = 8
    b = Builder({})
    nc = _build(b)
    maps = make_in_maps(inp, {}, ncores=ncores)
    in_maps = [{k: np.ascontiguousarray(v, dtype=np.float32) for k, v in m.items() if k in b.dram_in} for m in maps]
    res = run_bass_kernel_spmd(nc, in_maps, core_ids=list(range(ncores)))
    R = res.results
    B, S_, DB, DS = 16, 256, 8, 1024
    y_prompt = np.zeros((B, S_, D), np.float32)
    y_sample = np.zeros((DB, DS, D), np.float32)
    nk = np.zeros((B, DEPTH, S_, 2, 64), np.float32)
    nv = np.zeros((B, DEPTH, S_, 2, 64), np.float32)
    nlru = np.zeros((B, DEPTH, 2, W), np.float32)
    ns5 = np.zeros((B, DEPTH, 2, 2, 32, 64), np.float32)
    nwkv = np.zeros((B, DEPTH, 2, 8, 64, 64), np.float32)
    for c in range(ncores):
        r = R[c]
        y_sample[c] = np.asarray(r['yT_s']).T
        y_prompt[2 * c:2 * c + 2] = np.asarray(r['yT_p']).T.reshape(2, S_, D)
        nk[2 * c:2 * c + 2] = np.asarray(r['o_k']).reshape(2, DEPTH, S_, 2, 64)
        nv[2 * c:2 * c + 2] = np.asarray(r['o_v']).reshape(2, DEPTH, S_, 2, 64)
        nlru[2 * c:2 * c + 2] = np.asarray(r['o_lru'])
        o = np.asarray(r['o_s5']).reshape(2, DEPTH, 2, 64, 2, 2, 16)
        ns5[2 * c:2 * c + 2] = o.transpose(0, 1, 5, 4, 6, 2, 3).reshape(2, DEPTH, 2, 2, 32, 64)
        nwkv[2 * c:2 * c + 2] = np.asarray(r['o_wkv'])
    return (y_prompt, y_sample, nk, nv, nlru, ns5, nwkv)
```
